# Optimizing a Trainium2 kernel written in Bass

```python
import math
import jax, jax.numpy as jnp
from jax import lax
import numpy as np

D_MODEL = 1024
BATCH = 16
SEQ = 2048
DEPTH = 2

CHUNK = 64
HEAD_DIM = 64
N_MAIN_HEADS = 12
N_MEM_HEADS = 4
MAIN_WIDTH = N_MAIN_HEADS * HEAD_DIM
MEM_WIDTH = N_MEM_HEADS * HEAD_DIM
MIX_WIDTH = MAIN_WIDTH + MEM_WIDTH
N_MEM = 256
D_FF = 2816
CONV_WIDTH = 3
Q_BLOCK = 128
N_A_LAYERS = DEPTH // 2
N_B_LAYERS = DEPTH - N_A_LAYERS
A_IN_WIDTH = 3 * MAIN_WIDTH + N_MAIN_HEADS + MEM_WIDTH
B_IN_WIDTH = MAIN_WIDTH + MEM_WIDTH
FORGET_BIAS_INIT = 3.0
FORGET_W_SCALE = 0.1
EPS = 1e-6

kernel_name = "yoco_fox_stickbreak_memory_convffn"


def rmsnorm(x, g):
    xf = x.astype(jnp.float32)
    y = xf * lax.rsqrt(jnp.mean(xf * xf, axis=-1, keepdims=True) + EPS)
    return (y * g.astype(jnp.float32)).astype(x.dtype)


def split_heads(t, n_heads):
    b, s, _ = t.shape
    return t.reshape(b, s, n_heads, HEAD_DIM).transpose(0, 2, 1, 3)


def merge_heads(t):
    b, n, s, d = t.shape
    return t.transpose(0, 2, 1, 3).reshape(b, s, n * d)


def forgetting_attention(q, k, v, log_f):
    seq = q.shape[2]
    c = jnp.cumsum(log_f, axis=-1)
    scale = HEAD_DIM ** -0.5
    outs = []
    for i in range(seq // Q_BLOCK):
        q0, q1 = i * Q_BLOCK, (i + 1) * Q_BLOCK
        kb, vb = k[:, :, :q1], v[:, :, :q1]
        logits = jnp.einsum('bhqd,bhkd->bhqk', q[:, :, q0:q1], kb).astype(jnp.float32) * scale
        logits = logits + c[:, :, q0:q1, None] - c[:, :, None, :q1]
        t_idx = jnp.arange(q0, q1)[:, None]
        s_idx = jnp.arange(q1)[None, :]
        logits = jnp.where(s_idx <= t_idx, logits, -jnp.inf)
        p = jax.nn.softmax(logits, axis=-1)
        outs.append(jnp.einsum('bhqk,bhkd->bhqd', p.astype(vb.dtype), vb))
    return jnp.concatenate(outs, axis=2)


def stick_breaking_attention(q, k, v):
    seq = q.shape[2]
    scale = HEAD_DIM ** -0.5
    outs = []
    for i in range(seq // Q_BLOCK):
        q0, q1 = i * Q_BLOCK, (i + 1) * Q_BLOCK
        kb, vb = k[:, :, :q1], v[:, :, :q1]
        z = jnp.einsum('bhqd,bhkd->bhqk', q[:, :, q0:q1], kb).astype(jnp.float32) * scale
        t_idx = jnp.arange(q0, q1)[:, None]
        s_idx = jnp.arange(q1)[None, :]
        causal = s_idx < t_idx
        log_1m_beta = jnp.where(causal, jax.nn.log_sigmoid(-z), 0.0)
        cum = jnp.cumsum(log_1m_beta, axis=-1)
        log_a = jax.nn.log_sigmoid(z) + cum[..., -1:] - cum
        a = jnp.where(causal, jnp.exp(log_a), 0.0)
        outs.append(jnp.einsum('bhqk,bhkd->bhqd', a.astype(vb.dtype), vb))
    return jnp.concatenate(outs, axis=2)


def memory_attention(q, mem_k, mem_v):
    logits = jnp.einsum('bhqd,bhmd->bhqm', q, mem_k).astype(jnp.float32) * (HEAD_DIM ** -0.5)
    p = jax.nn.softmax(logits, axis=-1)
    return jnp.einsum('bhqm,bhmd->bhqd', p.astype(mem_v.dtype), mem_v)


def conv_ffn(h, w_up, conv_w, conv_b, w_down):
    u = h @ w_up
    s = u.shape[1]
    up = jnp.pad(u, ((0, 0), (CONV_WIDTH - 1, 0), (0, 0)))
    uc = conv_b
    for j in range(CONV_WIDTH):
        uc = uc + conv_w[j] * up[:, j:j + s]
    gate, val = jnp.split(uc, 2, axis=-1)
    return (jax.nn.silu(gate) * val) @ w_down


def setup_inputs(seed: int = 0) -> dict:
    key = jax.random.key(seed)
    ks = jax.random.split(key, 17)
    f32 = jnp.float32

    def nrm(k, shape, scale):
        return jax.random.normal(k, shape, f32) * scale

    def gain(k, shape):
        return 1.0 + 0.05 * jax.random.normal(k, shape, f32)

    x = nrm(ks[0], (BATCH, SEQ, D_MODEL), 1.0)
    mem = nrm(ks[1], (BATCH, N_MEM, D_MODEL), 1.0)
    ln_mix_g = gain(ks[2], (DEPTH, D_MODEL))
    w_in_a = nrm(ks[3], (N_A_LAYERS, D_MODEL, A_IN_WIDTH), D_MODEL ** -0.5)
    w_in_a = w_in_a.at[:, :, 3 * MAIN_WIDTH:3 * MAIN_WIDTH + N_MAIN_HEADS].multiply(FORGET_W_SCALE)
    b_f_a = FORGET_BIAS_INIT + 0.5 * jax.random.normal(ks[4], (N_A_LAYERS, N_MAIN_HEADS), f32)
    w_in_b = nrm(ks[5], (N_B_LAYERS, D_MODEL, B_IN_WIDTH), D_MODEL ** -0.5)
    ln_kv_g = gain(ks[6], (D_MODEL,))
    w_kv = nrm(ks[7], (D_MODEL, 2 * MAIN_WIDTH), D_MODEL ** -0.5)
    ln_mem_g = gain(ks[8], (DEPTH, D_MODEL))
    w_memkv = nrm(ks[9], (DEPTH, D_MODEL, 2 * MEM_WIDTH), D_MODEL ** -0.5)
    w_out = nrm(ks[10], (DEPTH, MIX_WIDTH, D_MODEL), MIX_WIDTH ** -0.5)
    ln_ffn_g = gain(ks[11], (DEPTH, D_MODEL))
    w_up = nrm(ks[12], (DEPTH, D_MODEL, 2 * D_FF), D_MODEL ** -0.5)
    conv_w = nrm(ks[13], (DEPTH, CONV_WIDTH, 2 * D_FF), CONV_WIDTH ** -0.5)
    conv_b = nrm(ks[14], (DEPTH, 2 * D_FF), 0.02)
    w_down = nrm(ks[15], (DEPTH, D_FF, D_MODEL), D_FF ** -0.5)
    final_g = gain(ks[16], (D_MODEL,))
    return {'x': x, 'mem': mem, 'ln_mix_g': ln_mix_g, 'w_in_a': w_in_a, 'b_f_a': b_f_a,
            'w_in_b': w_in_b, 'ln_kv_g': ln_kv_g, 'w_kv': w_kv, 'ln_mem_g': ln_mem_g,
            'w_memkv': w_memkv, 'w_out': w_out, 'ln_ffn_g': ln_ffn_g, 'w_up': w_up,
            'conv_w': conv_w, 'conv_b': conv_b, 'w_down': w_down, 'final_g': final_g}


def reference(x, mem, ln_mix_g, w_in_a, b_f_a, w_in_b, ln_kv_g, w_kv, ln_mem_g,
              w_memkv, w_out, ln_ffn_g, w_up, conv_w, conv_b, w_down, final_g):
    k_sh = None
    v_sh = None
    for layer in range(DEPTH):
        h = rmsnorm(x, ln_mix_g[layer])
        if layer < N_A_LAYERS:
            proj = h @ w_in_a[layer]
            q, k, v, f_logit, q_mem = jnp.split(
                proj, [MAIN_WIDTH, 2 * MAIN_WIDTH, 3 * MAIN_WIDTH, 3 * MAIN_WIDTH + N_MAIN_HEADS], axis=-1)
            log_f = jax.nn.log_sigmoid((f_logit + b_f_a[layer]).astype(jnp.float32))
            o_main = forgetting_attention(split_heads(q, N_MAIN_HEADS), split_heads(k, N_MAIN_HEADS),
                                          split_heads(v, N_MAIN_HEADS), log_f.transpose(0, 2, 1))
        else:
            if layer == N_A_LAYERS:
                kv = rmsnorm(x, ln_kv_g) @ w_kv
                k_s, v_s = jnp.split(kv, 2, axis=-1)
                k_sh = split_heads(k_s, N_MAIN_HEADS)
                v_sh = split_heads(v_s, N_MAIN_HEADS)
            proj = h @ w_in_b[layer - N_A_LAYERS]
            q, q_mem = jnp.split(proj, [MAIN_WIDTH], axis=-1)
            o_main = stick_breaking_attention(split_heads(q, N_MAIN_HEADS), k_sh, v_sh)
        mem_kv = rmsnorm(mem, ln_mem_g[layer]) @ w_memkv[layer]
        mk, mv = jnp.split(mem_kv, 2, axis=-1)
        o_mem = memory_attention(split_heads(q_mem, N_MEM_HEADS), split_heads(mk, N_MEM_HEADS),
                                 split_heads(mv, N_MEM_HEADS))
        o = jnp.concatenate([merge_heads(o_main), merge_heads(o_mem)], axis=-1) @ w_out[layer]
        x = x + o
        x = x + conv_ffn(rmsnorm(x, ln_ffn_g[layer]), w_up[layer], conv_w[layer], conv_b[layer], w_down[layer])
    return rmsnorm(x, final_g)
```

```python
import numpy as np
from contextlib import ExitStack
import concourse.bass as bass
import concourse.mybir as mybir
from concourse.bass_utils import run_bass_kernel_spmd

F32 = mybir.dt.float32
BF16 = mybir.dt.bfloat16
AF = mybir.ActivationFunctionType
ALU = mybir.AluOpType

D = 1024
SEQ = 2048
NT = SEQ // 128
NMEM = 256
DFF = 2816
A_IN = 2572
EPS = 1e-6
NCORES = 8
ENGS = ["pe", "act", "dve", "pool", "sp"]
CENGS = ["pe", "act", "dve", "pool"]
FCH = [4, 4, 4, 4, 4, 2]


class Sched:
    def __init__(self, esem, dsem):
        self.esem = esem
        self.dsem = dsem
        self.prog = {e: [] for e in ENGS}
        self.cnt = {e: 0 for e in CENGS}
        self.dcnt = [0] * len(dsem)
        self.dpool = {"sp": list(range(0, 16)), "pool": list(range(16, 24)), "act": list(range(16, 24))}
        self.dnext = {"sp": 0, "pool": 0, "act": 0}
        self.known = {e: {} for e in ENGS}
        self.state = {}

    def _sem(self, src):
        return self.esem[src] if isinstance(src, str) else self.dsem[src]

    def _need(self, eng, reads, writes):
        need = {}

        def add(w, kind):
            src, val = w
            if src == eng and (eng == "pe" or kind != "RAW"):
                return
            if need.get(src, 0) < val:
                need[src] = val

        for r in reads:
            st = self.state.get(r)
            if st is not None and st[0] is not None:
                add(st[0], "RAW")
        for w in writes:
            st = self.state.get(w)
            if st is not None:
                if st[0] is not None:
                    add(st[0], "WAW")
                for s_, v_ in st[1].items():
                    add((s_, v_), "WAR")
        out = []
        kn = self.known[eng]
        for src, val in need.items():
            if kn.get(src, 0) < val:
                kn[src] = val
                out.append((self._sem(src), val))
        return out

    def _update(self, me, reads, writes):
        src, val = me
        for r in reads:
            st = self.state.get(r)
            if st is None:
                st = [None, {}]
                self.state[r] = st
            st[1][src] = val
        for w in writes:
            self.state[w] = [me, {}]

    def op(self, eng, fn, reads=(), writes=()):
        wl = self._need(eng, reads, writes)
        self.cnt[eng] += 1
        sem = self.esem[eng]

        def emit(e, wl=wl, fn=fn, sem=sem):
            for sm, v in wl:
                e.wait_ge(sm, v)
            fn(e).then_inc(sem, 1)

        self.prog[eng].append(emit)
        self._update((eng, self.cnt[eng]), reads, writes)

    def dma(self, q, out, in_, reads=(), writes=()):
        pl = self.dpool[q]
        j = pl[self.dnext[q] % len(pl)]
        self.dnext[q] += 1
        wl = self._need(q, reads, writes)
        prev = 16 * self.dcnt[j]
        if prev > 0 and self.known[q].get(j, 0) < prev:
            self.known[q][j] = prev
            wl.append((self.dsem[j], prev))
        self.dcnt[j] += 1
        sem = self.dsem[j]

        def emit(e, wl=wl, out=out, in_=in_, sem=sem):
            for sm, v in wl:
                e.wait_ge(sm, v)
            e.dma_start(out=out, in_=in_).then_inc(sem, 16)

        self.prog[q].append(emit)
        self._update((j, 16 * self.dcnt[j]), reads, writes)

    def fence(self):
        srcs = [(e, self.cnt[e]) for e in CENGS] + [(j, 16 * c) for j, c in enumerate(self.dcnt)]
        for e in ENGS:
            wl = []
            for src, val in srcs:
                if src == e and e == "pe":
                    continue
                if val > self.known[e].get(src, 0):
                    self.known[e][src] = val
                    wl.append((self._sem(src), val))

            def emit(en, wl=wl):
                for sm, v in wl:
                    en.wait_ge(sm, v)

            self.prog[e].append(emit)
        self.state = {}


class Builder:
    def __init__(self, nseq=2, stop=None, dbg=(), phases=None, opt=None):
        self.phases = phases
        self.opt = opt or {}
        self.nseq = nseq
        self.stop = stop
        self.dbg = set(dbg)
        self.nc = bass.Bass("TRN2", target_bir_lowering=False)
        self.rr = {}

    def nxt(self, name, n):
        v = self.rr.get(name, 0)
        self.rr[name] = (v + 1) % n
        return v

    def pe(self, mms, reads, writes):
        def fn(e, mms=mms):
            ins = None
            for m in mms:
                if m[0] == "tr":
                    ins = e.transpose(out=m[1], in_=m[2], identity=self.IDENT)
                else:
                    out, lhsT, rhs, start, stop = m
                    ins = e.matmul(out, lhsT, rhs, start=start, stop=stop)
            return ins

        self.S.op("pe", fn, reads, writes)

    def act(self, out, in_, func, reads, writes, **kw):
        self.S.op("act", lambda e: e.activation(out=out, in_=in_, func=func, **kw), reads, writes)

    def ts(self, eng, out, in0, s1, s2, op0, op1, reads, writes):
        if op1 is None:
            self.S.op(eng, lambda e: e.tensor_scalar(out=out, in0=in0, scalar1=s1, scalar2=s2, op0=op0), reads, writes)
        else:
            self.S.op(eng, lambda e: e.tensor_scalar(out=out, in0=in0, scalar1=s1, scalar2=s2, op0=op0, op1=op1), reads, writes)

    def tt(self, eng, out, in0, in1, op, reads, writes):
        self.S.op(eng, lambda e: e.tensor_tensor(out=out, in0=in0, in1=in1, op=op), reads, writes)

    def stt(self, out, in0, scalar, in1, op0, op1, reads, writes):
        self.S.op("dve", lambda e: e.scalar_tensor_tensor(out=out, in0=in0, scalar=scalar, in1=in1, op0=op0, op1=op1), reads, writes)

    def cp(self, eng, out, in_, reads, writes):
        if eng == "act":
            self.S.op("act", lambda e: e.activation(out=out, in_=in_, func=AF.Copy), reads, writes)
        else:
            self.S.op(eng, lambda e: e.tensor_copy(out=out, in_=in_), reads, writes)

    def memset(self, eng, ap, val, writes):
        self.S.op(eng, lambda e: e.memset(ap, val), (), writes)

    def dma(self, q, out, in_, reads, writes):
        self.S.dma(q, out, in_, reads, writes)

    def psA(self):
        k = self.nxt("psA", 3)
        return self.PSA[k], ("psA", k)

    def norm_tile(self, xin, xreg, gb, dst, dstreg, evac_eng):
        b = self.nxt("hn", 2)
        ss = self.STAT[:, 2 * b:2 * b + 1]
        rs = self.STAT[:, 2 * b + 1:2 * b + 2]
        hn = self.HN[b]
        self.act(self.SQ[:], xin, AF.Square, [xreg], [("sq",), ("ss", b)], scale=1.0 / 32.0, accum_out=ss)
        self.act(ss, ss, AF.Sqrt, [("ss", b)], [("ss", b)], bias=self.EPSC[:, 0:1])
        self.S.op("dve", lambda e: e.reciprocal(out=rs, in_=ss), [("ss", b)], [("rs", b)])
        if gb is None:
            self.ts("dve", hn[:], xin, rs, None, ALU.mult, None, [xreg, ("rs", b)], [("hn", b)])
        else:
            self.stt(hn[:], xin, rs, gb, ALU.mult, ALU.mult, [xreg, ("rs", b), ("gb",)], [("hn", b)])
        self.pe([("tr", self.PST[:, c, :], hn[:, c * 128:(c + 1) * 128]) for c in range(8)], [("hn", b)], [("psT",)])
        self.cp(evac_eng, dst, self.PST[:, :, :], [("psT",)], [dstreg])

    def load_gb(self, idx):
        self.dma("sp", self.GB[:], self.d_gb[idx], [], [("gb",)])

    def wload(self, dst, src2d, reg, fold=None):
        self.dma("pool", dst, src2d.rearrange("(c p) n -> p c n", p=128), [], [reg])
        if fold is not None:
            for c in range(8):
                self.ts("pool", dst[:, c, :], dst[:, c, :], self.GC[:, fold, c:c + 1], 1.0, ALU.mult, ALU.mult, [reg], [reg])

    def attn_softmax_pair(self, QT, KT, V, nkt, causal, bias, oc, kreg, vreg):
        items = []
        for qb in range(4):
            for hh in range(2):
                kts = list(range(4 * qb + 4)) if causal else list(range(nkt))
                for kt in kts:
                    items.append((qb, hh, kt, kt == kts[-1]))
        n = len(items)
        pend = {}

        def stage_a(it):
            qb, hh, kt, last = it
            rows = slice(64 * hh, 64 * hh + 64)
            j0 = kt - 4 * qb if causal else -1
            col0 = max(0, j0) * 128
            S_, sreg = self.psA()
            qc = slice(qb * 512 + col0, (qb + 1) * 512)
            mms = [(S_[:, col0:512], KT[rows, kt * 128:(kt + 1) * 128], QT[rows, qc], True, not bias)]
            rd = [kreg(kt), ("QT", qb)]
            if bias:
                br = slice(32 * hh, 32 * hh + 4)
                mms.append((S_[:, col0:512], self.BK[br, kt * 128:(kt + 1) * 128], self.BQ[br, qc], False, True))
                rd += [("BK",), ("BQ",)]
            self.pe(mms, rd, [sreg])
            pb = self.nxt("PT", 3)
            PT = self.PT[pb]
            self.act(PT[:, col0:512], S_[:, col0:512], AF.Exp, [sreg], [("PT", pb)])
            if j0 >= 0:
                self.tt("dve", PT[:, col0:col0 + 128], PT[:, col0:col0 + 128], self.TRI, ALU.mult, [("PT", pb)], [("PT", pb)])
            pend[it] = (pb, j0)

        def stage_c(it):
            qb, hh, kt, last = it
            pb, j0 = pend.pop(it)
            PT = self.PT[pb]
            js = list(range(max(j0, 0), 4))
            mms = []
            for j in js:
                lastj = (kt == 4 * qb + j) if causal else last
                mms.append((self.ACC[j][:, 0:65], PT[:, 128 * j:128 * j + 128], V[:, kt, hh, :], kt == 0, lastj))
            self.pe(mms, [("PT", pb), vreg(kt)], [("acc", j) for j in js])
            if last:
                par = self.nxt("rl", 2)
                for j in range(4):
                    rl = self.STAT[:, 4 + 4 * par + j:5 + 4 * par + j]
                    self.S.op("dve", lambda e, rl=rl, j=j: e.reciprocal(out=rl, in_=self.ACC[j][:, 64:65]), [("acc", j)], [("rl", par, j)])
                for j in range(4):
                    qt = 4 * qb + j
                    rl = self.STAT[:, 4 + 4 * par + j:5 + 4 * par + j]
                    self.ts("dve", self.OP[:, qt, 64 * hh:64 * hh + 64], self.ACC[j][:, 0:64], rl, None,
                            ALU.mult, None, [("acc", j), ("rl", par, j)], [("OP", qt)])
                if hh == 1:
                    self.o_transpose(qb, oc)

        for i in range(n + 1):
            if i < n:
                stage_a(items[i])
            if i >= 1:
                stage_c(items[i - 1])

    def o_transpose(self, qb, oc):
        self.pe([("tr", self.PST[:, j, :], self.OP[:, 4 * qb + j, :]) for j in range(4)],
                [("OP", 4 * qb + j) for j in range(4)], [("psT",)])
        self.cp("act", self.OT[:, oc, qb * 512:(qb + 1) * 512].rearrange("p (a b) -> p a b", a=4), self.PST[:, 0:4, :],
                [("psT",)], [("OT", oc, 4 * qb + j) for j in range(4)])

    def attn_sb_pair(self, QT, KT, V, oc):
        items = []
        for qb in range(4):
            for hh in range(2):
                kts = list(reversed(range(4 * qb + 4)))
                for kt in kts:
                    items.append((qb, hh, kt, kt == kts[0], kt == kts[-1]))
        n = len(items)
        pend = {}

        def stage_a(it):
            qb, hh, kt, first, last = it
            rows = slice(64 * hh, 64 * hh + 64)
            j0 = kt - 4 * qb
            col0 = max(0, j0) * 128
            Z, zreg = self.psA()
            qc = slice(qb * 512 + col0, (qb + 1) * 512)
            self.pe([(Z[:, col0:512], KT[rows, kt * 128:(kt + 1) * 128], QT[rows, qc], True, False)],
                    [("KT", kt // 4), ("QT", qb)], [zreg])
            eb = self.nxt("E", 2)
            E = self.EX[eb]
            self.act(E[:, col0:512], Z[:, col0:512], AF.Exp, [zreg], [("E", eb)])
            sb = self.nxt("SP", 3)
            SP = self.SPT[sb]
            self.act(SP[:, col0:512], E[:, col0:512], AF.Ln, [("E", eb)], [("SP", sb)], bias=1.0)
            if j0 >= 0:
                self.tt("pool", SP[:, col0:col0 + 128], SP[:, col0:col0 + 128], self.TRIS, ALU.mult, [("SP", sb)], [("SP", sb)])
            pend[it] = (Z, zreg, sb, col0, j0)

        def stage_c(it):
            qb, hh, kt, first, last = it
            Z, zreg, sb, col0, j0 = pend[it]
            SP = self.SPT[sb]
            if first:
                self.memset("pool", self.SL[:, :], 0.0, [("SL",)])
            mms = [(Z[:, col0:512], self.NTRI, SP[:, col0:512], False, first)]
            rd = [("SP", sb)]
            if not first:
                mms.append((Z[:, col0:512], self.NONES, self.SL[:, col0:512], False, True))
                rd.append(("SL",))
            self.pe(mms, rd, [zreg])
            if not last:
                self.tt("dve", self.SL[:, col0:512], self.SL[:, col0:512], SP[:, col0:512], ALU.add, [("SL",), ("SP", sb)], [("SL",)])
            pb = self.nxt("PT", 3)
            PT = self.PT[pb]
            self.act(PT[:, col0:512], Z[:, col0:512], AF.Exp, [zreg], [("PT", pb)])
            if j0 >= 0:
                self.tt("dve", PT[:, col0:col0 + 128], PT[:, col0:col0 + 128], self.TRIS, ALU.mult, [("PT", pb)], [("PT", pb)])
            pend[it] = (pb, j0)

        def stage_e(it):
            qb, hh, kt, first, last = it
            pb, j0 = pend.pop(it)
            PT = self.PT[pb]
            js = list(range(max(j0, 0), 4))
            mms = []
            for j in js:
                mms.append((self.ACC[j][:, 0:64], PT[:, 128 * j:128 * j + 128], V[:, kt, hh, 0:64], kt == 4 * qb + j, kt == 0))
            self.pe(mms, [("PT", pb), ("V", kt)], [("acc", j) for j in js])
            if last:
                for j in range(4):
                    qt = 4 * qb + j
                    self.cp("dve", self.OP[:, qt, 64 * hh:64 * hh + 64], self.ACC[j][:, 0:64], [("acc", j)], [("OP", qt)])
                if hh == 1:
                    self.o_transpose(qb, oc)

        for i in range(n + 2):
            if i < n:
                stage_a(items[i])
            if 1 <= i <= n:
                stage_c(items[i - 1])
            if i >= 2:
                stage_e(items[i - 2])

    def proj_fm(self, W, wsl, M, blk, wreg, dst, dreg, scale, eng):
        ps, preg = self.psA()
        tsl = slice(blk * 512, (blk + 1) * 512)
        self.pe([(ps[0:M, :], W[:, c, wsl], self.HT[:, c, tsl], c == 0, c == 7) for c in range(8)],
                [wreg] + [("HT", 4 * blk + j) for j in range(4)], [preg])
        if eng == "act":
            self.S.op("act", lambda e: e.activation(out=dst, in_=ps[0:M, :], func=AF.Copy, scale=scale), [preg], [dreg])
        else:
            self.ts(eng, dst, ps[0:M, :], scale, None, ALU.mult, None, [preg], [dreg])

    def v_proj(self, W, wsl, nheads, i, wreg, V, src, sreg, vreg):
        ps, preg = self.psA()
        n = 64 * nheads
        self.pe([(ps[:, 0:n], src[:, c, i * 128:(i + 1) * 128], W[:, c, wsl], c == 0, c == 7) for c in range(8)],
                [wreg, sreg], [preg])
        self.cp("dve", V[:, i, :, 0:64], ps[:, 0:n].rearrange("p (a b) -> p a b", a=nheads), [preg], [vreg])

    def mem_branch(self, s, l):
        self.load_gb(1 + l)
        self.wload(self.WM, self.d_wmemkv[l], ("WM",))
        self.memset("pool", self.MV[:, :, :, 64:65], 1.0, [("MV", 0), ("MV", 1)])
        for mt in range(2):
            self.dma("sp", self.MTMP[:], self.d_mem[s, mt * 128:(mt + 1) * 128, :], [], [("mtmp",)])
            self.norm_tile(self.MTMP[:], ("mtmp",), self.GB[:], self.HMT[:, :, mt * 128:(mt + 1) * 128], ("HMT", mt), "act")
        for p in range(2):
            ps, preg = self.psA()
            self.pe([(ps[:, 0:256], self.WM[:, c, 128 * p:128 * p + 128], self.HMT[:, c, :], c == 0, c == 7) for c in range(8)],
                    [("WM",), ("HMT", 0), ("HMT", 1)], [preg])
            self.cp("act", self.MKT[:, p, :], ps[:, 0:256], [preg], [("MKT", p)])
        for mt in range(2):
            self.v_proj(self.WM, slice(256, 512), 4, mt, ("WM",), self.MV, self.HMT, ("HMT", mt), ("MV", mt))

    def mem_attn(self, l, qsrc, qcol0, fold):
        for p in range(2):
            b = self.nxt("WP", 2)
            self.wload(self.WP[b][:, :, 0:128], qsrc[:, qcol0 + 128 * p:qcol0 + 128 * p + 128], ("WP", b), fold)
            for blk in range(4):
                self.proj_fm(self.WP[b], slice(0, 128), 128, blk, ("WP", b), self.QT[:, blk * 512:(blk + 1) * 512], ("QT", blk), 0.125, "act")
            self.attn_softmax_pair(self.QT, self.MKT[:, p, :], self.MV[:, :, 2 * p:2 * p + 2, :], 2, False, False, 6 + p,
                                   lambda kt, p=p: ("MKT", p), lambda kt: ("MV", kt))

    def out_proj(self, l):
        self.wload(self.WO, self.d_wout[l], ("WO",))
        for i in range(NT):
            a = self.nxt("acc2", 2)
            ps = self.ACC2[a]
            regs = [("acc", 2 * a), ("acc", 2 * a + 1)]
            mms = []
            for h in range(2):
                for c in range(8):
                    mms.append((ps[:, 512 * h:512 * h + 512], self.OT[:, c, i * 128:(i + 1) * 128], self.WO[:, c, 512 * h:512 * h + 512], c == 0, c == 7))
            self.pe(mms, [("WO",)] + [("OT", c, i) for c in range(8)], regs)
            self.tt("dve", self.X[:, i, :], ps[:, :], self.X[:, i, :], ALU.add, regs + [("X", i)], [("X", i)])

    def ffn(self, l):
        self.load_gb(3 + l)
        self.dma("sp", self.CW[:], self.d_cw[l], [], [("cw",)])
        self.dma("sp", self.CB[:], self.d_cb[l], [], [("cb",)])
        for i in range(NT):
            self.norm_tile(self.X[:, i, :], ("X", i), self.GB[:], self.HT[:, :, i * 128:(i + 1) * 128], ("HT", i), "act" if i % 2 else "dve")
        wup = self.d_wup[l]
        wdn = self.d_wdown[l]

        def load_chunk(fi):
            b = fi % 2
            f0 = 128 * sum(FCH[:fi])
            nf = 128 * FCH[fi]
            self.wload(self.WUP[b][:, :, 0, 0:nf], wup[:, f0:f0 + nf], ("WUP", b, 0))
            self.wload(self.WUP[b][:, :, 1, 0:nf], wup[:, DFF + f0:DFF + f0 + nf], ("WUP", b, 1))
            self.wload(self.WDN[b][:, 0:FCH[fi], :], wdn[f0:f0 + nf, :], ("WDN", b))

        def down(fi):
            b = fi % 2
            for i in range(NT):
                a = self.nxt("acc2", 2)
                ps = self.ACC2[a]
                regs = [("acc", 2 * a), ("acc", 2 * a + 1)]
                mms = []
                for h in range(2):
                    for k in range(FCH[fi]):
                        mms.append((ps[:, 512 * h:512 * h + 512], self.AT[b][:, k, i * 128:(i + 1) * 128],
                                    self.WDN[b][:, k, 512 * h:512 * h + 512], k == 0, k == FCH[fi] - 1))
                self.pe(mms, [("WDN", b)] + [("AT", b, k, i // 4) for k in range(FCH[fi])], regs)
                self.tt("dve", self.X[:, i, :], ps[:, :], self.X[:, i, :], ALU.add, regs + [("X", i)], [("X", i)])

        load_chunk(0)
        for fi in range(len(FCH)):
            b = fi % 2
            if fi + 1 < len(FCH):
                load_chunk(fi + 1)
            cbase = sum(FCH[:fi])
            for k in range(FCH[fi]):
                for tb in range(4):
                    tsl = slice(tb * 512, (tb + 1) * 512)
                    gT = None
                    for gv in range(2):
                        ci = cbase + k + (DFF // 128) * gv
                        ps, preg = self.psA()
                        self.pe([(ps[:, :], self.WUP[b][:, c, gv, 128 * k:128 * k + 128], self.HT[:, c, tsl], c == 0, c == 7) for c in range(8)],
                                [("WUP", b, gv)] + [("HT", 4 * tb + j) for j in range(4)], [preg])
                        UB = self.UBS[gv][tb % 2]
                        ureg = ("UB", gv, tb % 2)
                        if tb == 0:
                            self.memset("pool", UB[:, 0:2], 0.0, [ureg])
                        else:
                            self.cp("pool", UB[:, 0:2], self.UBS[gv][(tb + 1) % 2][:, 512:514], [("UB", gv, (tb + 1) % 2)], [ureg])
                        self.cp("act", UB[:, 2:514], ps[:, :], [preg], [ureg])
                        tbuf = self.nxt("T%d" % gv, 2)
                        T = (self.TG if gv == 0 else self.TV)[tbuf]
                        treg = ("T", gv, tbuf)
                        self.act(T[:, :], ps[:, :], AF.Identity, [preg, ("cw",), ("cb",)], [treg],
                                 scale=self.CW[:, ci, 2:3], bias=self.CB[:, ci:ci + 1])
                        self.stt(T[:, :], UB[:, 1:513], self.CW[:, ci, 1:2], T[:, :], ALU.mult, ALU.add, [ureg, treg, ("cw",)], [treg])
                        self.stt(T[:, :], UB[:, 0:512], self.CW[:, ci, 0:1], T[:, :], ALU.mult, ALU.add, [ureg, treg, ("cw",)], [treg])
                        if gv == 0:
                            self.act(T[:, :], T[:, :], AF.Silu, [treg], [treg])
                            gT = (T, treg)
                        else:
                            self.tt("dve", self.AT[b][:, k, tsl], gT[0][:, :], T[:, :], ALU.mult, [gT[1], treg], [("AT", b, k, tb)])
            down(fi)

    def layer0_mixer(self, s):
        self.mem_branch(s, 0)
        self.S.fence()
        self.load_gb(0)
        self.memset("pool", self.V[:, :, :, 64:65], 1.0, [("V", i) for i in range(NT)])
        for g in range(2):
            self.memset("pool", self.BQ[32 * g:32 * g + 4, :], -1.0, [("BQ",)])
            self.memset("pool", self.BK[32 * g:32 * g + 4, :], 1.0, [("BK",)])
        for i in range(NT):
            self.norm_tile(self.X[:, i, :], ("X", i), self.GB[:], self.HT[:, :, i * 128:(i + 1) * 128], ("HT", i), "act" if i % 2 else "dve")
        b = self.nxt("WP", 2)
        WF = self.WP[b][:, :, 0:12]
        self.wload(WF, self.d_wina[:, 2304:2316], ("WP", b))
        for blk in range(4):
            ps, preg = self.psA()
            tsl = slice(blk * 512, (blk + 1) * 512)
            self.pe([(ps[0:12, :], WF[:, c, :], self.HT[:, c, tsl], c == 0, c == 7) for c in range(8)],
                    [("WP", b)] + [("HT", 4 * blk + j) for j in range(4)], [preg])
            self.act(self.CF[0:12, tsl], ps[0:12, :], AF.Exp, [preg], [("CF", blk)], scale=-1.0, bias=self.NB[:, 0:1])
            self.act(self.CF[0:12, tsl], self.CF[0:12, tsl], AF.Ln, [("CF", blk)], [("CF", blk)], bias=1.0)
            self.ts("dve", self.CF[0:12, tsl], self.CF[0:12, tsl], -0.5, None, ALU.mult, None, [("CF", blk)], [("CF", blk)])
        self.S.op("dve", lambda e: e.tensor_tensor_scan(out=self.CFO[0:12, :], data0=self.CF[0:12, :], data1=self.CF[0:12, :],
                                                         initial=0.0, op0=ALU.add, op1=ALU.add),
                  [("CF", k) for k in range(4)], [("CFO",)])
        self.cp("dve", self.CH[0:12, 0, :], self.CFO[0:12, :], [("CFO",)], [("CH", 0)])
        self.tt("dve", self.CH[0:12, 1, :], self.CFO[0:12, :], self.CH[0:12, 0, :], ALU.subtract, [("CFO",), ("CH", 0)], [("CH", 1)])
        if "cf" in self.d_dbg:
            self.dma("sp", self.d_dbg["cf"], self.CFO[0:12, :], [("CFO",)], [])
        for p in range(6):
            b = self.nxt("WP", 2)
            for k in range(3):
                self.wload(self.WP[b][:, :, 128 * k:128 * k + 128], self.d_wina[:, 768 * k + 128 * p:768 * k + 128 * p + 128], ("WP", b))
            for hh in range(2):
                h = 2 * p + hh
                self.dma("sp", self.BQ[32 * hh:32 * hh + 1, :], self.CH[h:h + 1, 0, :], [("CH", 0)], [("BQ",)])
                self.dma("sp", self.BQ[32 * hh + 1:32 * hh + 2, :], self.CH[h:h + 1, 1, :], [("CH", 1)], [("BQ",)])
                self.dma("sp", self.BK[32 * hh + 2:32 * hh + 3, :], self.CH[h:h + 1, 0, :], [("CH", 0)], [("BK",)])
                self.dma("sp", self.BK[32 * hh + 3:32 * hh + 4, :], self.CH[h:h + 1, 1, :], [("CH", 1)], [("BK",)])
            for blk in range(4):
                self.proj_fm(self.WP[b], slice(0, 128), 128, blk, ("WP", b), self.QT[:, blk * 512:(blk + 1) * 512], ("QT", blk), 0.125, "act")
            for blk in range(4):
                self.proj_fm(self.WP[b], slice(128, 256), 128, blk, ("WP", b), self.KT[:, blk * 512:(blk + 1) * 512], ("KT", blk), 1.0, "dve")
            for i in range(NT):
                self.v_proj(self.WP[b], slice(256, 384), 2, i, ("WP", b), self.V, self.HT, ("HT", i), ("V", i))
            self.attn_softmax_pair(self.QT, self.KT, self.V, None, True, True, p, lambda kt: ("KT", kt // 4), lambda kt: ("V", kt))
        self.mem_attn(0, self.d_wina, 2316, None)

    def layer1_mixer(self, s):
        self.mem_branch(s, 1)
        self.S.fence()
        for i in range(NT):
            self.norm_tile(self.X[:, i, :], ("X", i), None, self.HT[:, :, i * 128:(i + 1) * 128], ("HT", i), "act" if i % 2 else "dve")
        for p in range(self.opt.get("l1_pairs", 6)):
            b = self.nxt("WP", 2)
            self.wload(self.WP[b][:, :, 0:128], self.d_winb[:, 128 * p:128 * p + 128], ("WP", b), 1)
            self.wload(self.WP[b][:, :, 128:256], self.d_wkv[:, 128 * p:128 * p + 128], ("WP", b), 0)
            self.wload(self.WP[b][:, :, 256:384], self.d_wkv[:, 768 + 128 * p:768 + 128 * p + 128], ("WP", b), 0)
            for blk in range(4):
                self.proj_fm(self.WP[b], slice(0, 128), 128, blk, ("WP", b), self.QT[:, blk * 512:(blk + 1) * 512], ("QT", blk), 0.125, "act")
            for blk in range(4):
                self.proj_fm(self.WP[b], slice(128, 256), 128, blk, ("WP", b), self.KT[:, blk * 512:(blk + 1) * 512], ("KT", blk), 1.0, "dve")
            for i in range(NT):
                self.v_proj(self.WP[b], slice(256, 384), 2, i, ("WP", b), self.V, self.HT, ("HT", i), ("V", i))
            if self.opt.get("l1_attn", True):
                self.attn_sb_pair(self.QT, self.KT, self.V, p)
        if self.opt.get("l1_mem", True):
            self.mem_attn(1, self.d_winb, 768, 1)

    def dump_x(self, name):
        if name in self.d_dbg:
            for i in range(NT):
                self.dma("sp", self.d_dbg[name][i * 128:(i + 1) * 128, :], self.X[:, i, :], [("X", i)], [])

    def final_out(self, s, normed):
        if normed:
            self.load_gb(5)
        for i in range(NT):
            ob = self.nxt("outt", 2)
            O = self.OUTT[ob]
            if normed:
                b = self.nxt("hn", 2)
                ss = self.STAT[:, 2 * b:2 * b + 1]
                rs = self.STAT[:, 2 * b + 1:2 * b + 2]
                xin = self.X[:, i, :]
                self.act(self.SQ[:], xin, AF.Square, [("X", i)], [("sq",), ("ss", b)], scale=1.0 / 32.0, accum_out=ss)
                self.act(ss, ss, AF.Sqrt, [("ss", b)], [("ss", b)], bias=self.EPSC[:, 0:1])
                self.S.op("dve", lambda e, rs=rs, ss=ss: e.reciprocal(out=rs, in_=ss), [("ss", b)], [("rs", b)])
                self.stt(O[:], xin, rs, self.GB[:], ALU.mult, ALU.mult, [("X", i), ("rs", b), ("gb",)], [("outt", ob)])
                self.dma("sp", self.d_out[s, i * 128:(i + 1) * 128, :], O[:], [("outt", ob)], [])
            else:
                self.dma("sp", self.d_out[s, i * 128:(i + 1) * 128, :], self.X[:, i, :], [("X", i)], [])

    def program(self):
        order = ["l0mix", "l0out", "l0ffn", "l1mix", "l1out", "l1ffn"]
        stop = self.stop
        nph = len(order) if stop is None else order.index(stop) + 1
        self.dma("pool", self.CONST[:], self.d_consts, [], [("const",)])
        self.dma("sp", self.GC[:], self.d_gc, [], [("gc",)])
        self.dma("sp", self.NB[:], self.d_bf, [], [("nb",)])
        self.memset("dve", self.EPSC[:], EPS, [("eps",)])
        self.ts("dve", self.NB[:], self.NB[:], -1.0, None, ALU.mult, None, [("nb",)], [("nb",)])
        self.S.fence()
        for s in range(self.nseq):
            for i in range(NT):
                self.dma("sp", self.X[:, i, :], self.d_x[s, i * 128:(i + 1) * 128, :], [], [("X", i)])
            for ph in (order[:nph] if self.phases is None else self.phases):
                if ph == "l0mix":
                    self.layer0_mixer(s)
                elif ph == "l1mix":
                    self.layer1_mixer(s)
                elif ph == "l0out":
                    self.out_proj(0)
                    if s == 0:
                        self.dump_x("x_l0mix")
                elif ph == "l1out":
                    self.out_proj(1)
                    if s == 0:
                        self.dump_x("x_l1mix")
                elif ph == "l0ffn":
                    self.ffn(0)
                    if s == 0:
                        self.dump_x("x_l0")
                elif ph == "l1ffn":
                    self.ffn(1)
                self.S.fence()
            self.final_out(s, stop is None)
            self.S.fence()

    def build(self):
        nc = self.nc
        nseq = self.nseq
        dt_in = lambda name, shape: nc.dram_tensor(name, shape, F32, kind="ExternalInput").ap()
        self.d_x = dt_in("x", [nseq, SEQ, D])
        self.d_mem = dt_in("mem", [nseq, NMEM, D])
        self.d_wina = dt_in("w_in_a", [D, A_IN])
        self.d_winb = dt_in("w_in_b", [D, D])
        self.d_wkv = dt_in("w_kv", [D, 1536])
        self.d_wmemkv = dt_in("w_memkv", [2, D, 512])
        self.d_wout = dt_in("w_out", [2, D, D])
        self.d_wup = dt_in("w_up", [2, D, 2 * DFF])
        self.d_wdown = dt_in("w_down", [2, DFF, D])
        self.d_gb = dt_in("gb", [6, 128, D])
        self.d_gc = dt_in("gc", [128, 2, 8])
        self.d_bf = dt_in("bf", [12, 1])
        self.d_cw = dt_in("cw", [2, 128, 44, 3])
        self.d_cb = dt_in("cb", [2, 128, 44])
        self.d_consts = dt_in("consts", [128, 5, 128])
        self.d_out = nc.dram_tensor("out", [nseq, SEQ, D], F32, kind="ExternalOutput").ap()
        self.d_dbg = {}
        for name, shape in self.dbg_shapes().items():
            if name in self.dbg:
                self.d_dbg[name] = nc.dram_tensor("dbg_" + name, shape, F32, kind="ExternalOutput").ap()

        with ExitStack() as es:
            sb = lambda name, shape, dt: es.enter_context(nc.sbuf_tensor(name, shape, dt))
            self.X = sb("X", [128, NT, D], F32)
            self.HT = sb("HT", [128, 8, SEQ], BF16)
            self.CONST = sb("CONST", [128, 5, 128], BF16)
            self.IDENT = self.CONST[:, 0, :]
            self.TRI = self.CONST[:, 1, :]
            self.TRIS = self.CONST[:, 2, :]
            self.NTRI = self.CONST[:, 3, :]
            self.NONES = self.CONST[:, 4, :]
            self.GB = sb("GB", [128, D], F32)
            self.GC = sb("GC", [128, 2, 8], F32)
            self.NB = sb("NB", [12, 1], F32)
            self.CW = sb("CW", [128, 44, 3], F32)
            self.CB = sb("CB", [128, 44], F32)
            self.STAT = sb("STAT", [128, 16], F32)
            self.EPSC = sb("EPSC", [128, 1], F32)
            self.HN = [sb("HN%d" % i, [128, D], BF16) for i in range(2)]
            self.SQ = sb("SQ", [128, D], BF16)
            self.ARENA_B = 101000
            self.ARENA = sb("ARENA", [128, self.ARENA_B // 2], BF16)
            self.PSA = [es.enter_context(nc.psum_tensor("psA%d" % i, [128, 512], F32)) for i in range(3)]
            self.ACC2 = [es.enter_context(nc.psum_tensor("acc2_%d" % i, [128, 1024], F32)) for i in range(2)]
            self.ACC = [self.ACC2[0][:, 0:512], self.ACC2[0][:, 512:1024], self.ACC2[1][:, 0:512], self.ACC2[1][:, 512:1024]]
            self.PST = es.enter_context(nc.psum_tensor("psT", [128, 8, 128], BF16))
            esem = {e: es.enter_context(nc.semaphore("sem_" + e)) for e in CENGS}
            dsem = [es.enter_context(nc.semaphore("dsem%d" % i)) for i in range(24)]
            self.S = Sched(esem, dsem)
            self.carve_all()
            self.program()
            self.emit()
        return nc

    def dbg_shapes(self):
        return {"x_l0mix": [SEQ, D], "x_l0": [SEQ, D], "x_l1mix": [SEQ, D], "cf": [12, SEQ]}

    def carve(self, off, shape, dt):
        n = int(np.prod(shape[1:]))
        esz = 2 if dt == BF16 else 4
        assert off % 4 == 0, off
        a = self.ARENA[:, off // 2:off // 2 + n * esz // 2]
        if dt == F32:
            a = a.bitcast(F32)
        if len(shape) == 3:
            a = a.rearrange("p (a b) -> p a b", a=shape[1])
        elif len(shape) == 4:
            a = a.rearrange("p (a b c) -> p a b c", a=shape[1], b=shape[2])
        self._off = off + n * esz
        assert self._off <= self.ARENA_B, self._off
        return a

    def carve_all(self):
        al = lambda o: (o + 3) // 4 * 4
        self.OT = self.carve(0, [128, 8, SEQ], BF16)
        o = self._off
        self.MTMP = self.carve(0, [128, D], F32)
        self.WM = self.carve(self._off, [128, 8, 512], BF16)
        self.HMT = self.carve(self._off, [128, 8, NMEM], BF16)
        self.QT = self.carve(o, [128, SEQ], BF16); o = self._off
        self.KT = self.carve(o, [128, SEQ], BF16); o = self._off
        self.V = self.carve(o, [128, NT, 2, 65], BF16); o = al(self._off)
        self.OP = self.carve(o, [128, NT, 128], BF16); o = self._off
        self.WP = []
        for i in range(2):
            self.WP.append(self.carve(o, [128, 8, 384], BF16)); o = self._off
        self.PT = []
        for i in range(3):
            self.PT.append(self.carve(o, [128, 512], BF16)); o = self._off
        self.MKT = self.carve(o, [128, 2, NMEM], BF16); o = self._off
        self.MV = self.carve(o, [128, 2, 4, 65], BF16); o = al(self._off)
        o0 = o
        self.BQ = self.carve(o, [128, SEQ], BF16); o = self._off
        self.BK = self.carve(o, [128, SEQ], BF16); o = self._off
        self.CH = self.carve(o, [128, 2, SEQ], BF16); o = self._off
        self.CF = self.carve(o, [128, SEQ], F32); o = self._off
        o = o0
        self.EX = []
        for i in range(2):
            self.EX.append(self.carve(o, [128, 512], F32)); o = self._off
        self.SPT = []
        for i in range(3):
            self.SPT.append(self.carve(o, [128, 512], BF16)); o = self._off
        self.SL = self.carve(o, [128, 512], BF16); o = self._off
        self.CFO = self.carve(self.ARENA_B - 8192 - 8, [128, SEQ], F32)
        self.WO = self.carve(32768, [128, 8, D], BF16)
        o = 0
        self.WUP = []
        for i in range(2):
            self.WUP.append(self.carve(o, [128, 8, 2, 512], BF16)); o = self._off
        self.WDN = []
        for i in range(2):
            self.WDN.append(self.carve(o, [128, 4, D], BF16)); o = self._off
        self.AT = []
        for i in range(2):
            self.AT.append(self.carve(o, [128, 4, SEQ], BF16)); o = self._off
        self.UBS = [[None, None], [None, None]]
        for g in range(2):
            for i in range(2):
                self.UBS[g][i] = self.carve(o, [128, 516], F32); o = self._off
        self.TG = []
        self.TV = []
        for i in range(2):
            self.TG.append(self.carve(o, [128, 512], F32)); o = self._off
            self.TV.append(self.carve(o, [128, 512], F32)); o = self._off
        self.OUTT = [self.carve(0, [128, D], F32), self.carve(4096, [128, D], F32)]

    def emit(self):
        nc = self.nc
        S = self.S
        with nc.Block() as block:
            @block.tensor
            def _(e):
                for f in S.prog["pe"]:
                    f(e)

            @block.scalar
            def _(e):
                for f in S.prog["act"]:
                    f(e)

            @block.vector
            def _(e):
                for f in S.prog["dve"]:
                    f(e)

            @block.gpsimd
            def _(e):
                for f in S.prog["pool"]:
                    f(e)

            @block.sync
            def _(e):
                for f in S.prog["sp"]:
                    f(e)


def host_inputs(x, mem, ln_mix_g, w_in_a, b_f_a, w_in_b, ln_kv_g, w_kv, ln_mem_g, w_memkv, w_out, ln_ffn_g,
                w_up, conv_w, conv_b, w_down, final_g):
    f = lambda a: np.ascontiguousarray(np.asarray(a, dtype=np.float32))
    rep = lambda g: np.broadcast_to(np.asarray(g, np.float32)[None, :], (128, D))
    gb = f(np.stack([rep(ln_mix_g[0]), rep(ln_mem_g[0]), rep(ln_mem_g[1]), rep(ln_ffn_g[0]), rep(ln_ffn_g[1]), rep(final_g)]))
    col = lambda g: np.asarray(g, np.float32).reshape(8, 128).T
    gc = f(np.stack([col(ln_kv_g), col(ln_mix_g[1])], axis=1))
    cw = f(np.asarray(conv_w, np.float32).reshape(2, 3, 44, 128).transpose(0, 3, 2, 1))
    cb = f(np.asarray(conv_b, np.float32).reshape(2, 44, 128).transpose(0, 2, 1))
    i = np.arange(128)
    ident = (i[:, None] == i[None, :]).astype(np.float32)
    tri = (i[:, None] <= i[None, :]).astype(np.float32)
    tris = (i[:, None] < i[None, :]).astype(np.float32)
    ntri = -(i[:, None] >= i[None, :]).astype(np.float32)
    nones = -np.ones((128, 128), np.float32)
    consts = f(np.stack([ident, tri, tris, ntri, nones], axis=1))
    return {
        "w_in_a": f(w_in_a[0]), "w_in_b": f(w_in_b[0]), "w_kv": f(w_kv), "w_memkv": f(w_memkv), "w_out": f(w_out),
        "w_up": f(w_up), "w_down": f(w_down), "gb": gb, "gc": gc, "bf": f(np.asarray(b_f_a, np.float32).reshape(12, 1)),
        "cw": cw, "cb": cb, "consts": consts,
    }


def kernel(**inputs):
    x = np.asarray(inputs["x"], np.float32)
    mem = np.asarray(inputs["mem"], np.float32)
    shared = host_inputs(**inputs)
    nseq = x.shape[0] // NCORES
    nc = Builder(nseq=nseq).build()
    in_maps = []
    for c in range(NCORES):
        m = dict(shared)
        m["x"] = np.ascontiguousarray(x[c * nseq:(c + 1) * nseq])
        m["mem"] = np.ascontiguousarray(mem[c * nseq:(c + 1) * nseq])
        in_maps.append(m)
    res = run_bass_kernel_spmd(nc, in_maps, core_ids=list(range(NCORES)))
    return np.concatenate([np.asarray(r["out"], np.float32) for r in res.results], axis=0)
```

```python
import numpy as np
from contextlib import ExitStack
import concourse.bass as bass
import concourse.mybir as mybir
from concourse.bass_utils import run_bass_kernel_spmd

F32 = mybir.dt.float32
BF16 = mybir.dt.bfloat16
AF = mybir.ActivationFunctionType
ALU = mybir.AluOpType

D = 1024
SEQ = 2048
NT = SEQ // 128
NMEM = 256
DFF = 2816
A_IN = 2572
EPS = 1e-6
NCORES = 8
ENGS = ["pe", "act", "dve", "pool", "sp"]
CENGS = ["pe", "act", "dve", "pool"]
FCH = [4, 4, 4, 4, 4, 2]


class Sched:
    def __init__(self, esem, dsem):
        self.esem = esem
        self.dsem = dsem
        self.prog = {e: [] for e in ENGS}
        self.cnt = {e: 0 for e in CENGS}
        self.dcnt = [0] * len(dsem)
        self.dpool = {"sp": list(range(0, 16)), "pool": list(range(16, 24)), "act": list(range(16, 24))}
        self.dnext = {"sp": 0, "pool": 0, "act": 0}
        self.known = {e: {} for e in ENGS}
        self.state = {}

    def _sem(self, src):
        return self.esem[src] if isinstance(src, str) else self.dsem[src]

    def _need(self, eng, reads, writes):
        need = {}

        def add(w, kind):
            src, val = w
            if src == eng and (eng == "pe" or kind != "RAW"):
                return
            if need.get(src, 0) < val:
                need[src] = val

        for r in reads:
            st = self.state.get(r)
            if st is not None and st[0] is not None:
                add(st[0], "RAW")
        for w in writes:
            st = self.state.get(w)
            if st is not None:
                if st[0] is not None:
                    add(st[0], "WAW")
                for s_, v_ in st[1].items():
                    add((s_, v_), "WAR")
        out = []
        kn = self.known[eng]
        for src, val in need.items():
            if kn.get(src, 0) < val:
                kn[src] = val
                out.append((self._sem(src), val))
        return out

    def _update(self, me, reads, writes):
        src, val = me
        for r in reads:
            st = self.state.get(r)
            if st is None:
                st = [None, {}]
                self.state[r] = st
            st[1][src] = val
        for w in writes:
            self.state[w] = [me, {}]

    def op(self, eng, fn, reads=(), writes=()):
        wl = self._need(eng, reads, writes)
        self.cnt[eng] += 1
        sem = self.esem[eng]

        def emit(e, wl=wl, fn=fn, sem=sem):
            for sm, v in wl:
                e.wait_ge(sm, v)
            fn(e).then_inc(sem, 1)

        self.prog[eng].append(emit)
        self._update((eng, self.cnt[eng]), reads, writes)

    def dma(self, q, out, in_, reads=(), writes=()):
        pl = self.dpool[q]
        j = pl[self.dnext[q] % len(pl)]
        self.dnext[q] += 1
        wl = self._need(q, reads, writes)
        prev = 16 * self.dcnt[j]
        if prev > 0 and self.known[q].get(j, 0) < prev:
            self.known[q][j] = prev
            wl.append((self.dsem[j], prev))
        self.dcnt[j] += 1
        sem = self.dsem[j]

        def emit(e, wl=wl, out=out, in_=in_, sem=sem):
            for sm, v in wl:
                e.wait_ge(sm, v)
            e.dma_start(out=out, in_=in_).then_inc(sem, 16)

        self.prog[q].append(emit)
        self._update((j, 16 * self.dcnt[j]), reads, writes)

    def fence(self):
        srcs = [(e, self.cnt[e]) for e in CENGS] + [(j, 16 * c) for j, c in enumerate(self.dcnt)]
        for e in ENGS:
            wl = []
            for src, val in srcs:
                if src == e and e == "pe":
                    continue
                if val > self.known[e].get(src, 0):
                    self.known[e][src] = val
                    wl.append((self._sem(src), val))

            def emit(en, wl=wl):
                for sm, v in wl:
                    en.wait_ge(sm, v)

            self.prog[e].append(emit)
        self.state = {}


class Builder:
    def __init__(self, nseq=2, stop=None, dbg=(), phases=None, opt=None):
        self.phases = phases
        self.opt = opt or {}
        self.nseq = nseq
        self.stop = stop
        self.dbg = set(dbg)
        self.nc = bass.Bass("TRN2", target_bir_lowering=False)
        self.rr = {}

    def nxt(self, name, n):
        v = self.rr.get(name, 0)
        self.rr[name] = (v + 1) % n
        return v

    def pe(self, mms, reads, writes):
        def fn(e, mms=mms):
            ins = None
            for m in mms:
                if m[0] == "tr":
                    ins = e.transpose(out=m[1], in_=m[2], identity=self.IDENT)
                else:
                    out, lhsT, rhs, start, stop = m
                    ins = e.matmul(out, lhsT, rhs, start=start, stop=stop)
            return ins

        self.S.op("pe", fn, reads, writes)

    def act(self, out, in_, func, reads, writes, **kw):
        self.S.op("act", lambda e: e.activation(out=out, in_=in_, func=func, **kw), reads, writes)

    def ts(self, eng, out, in0, s1, s2, op0, op1, reads, writes):
        if op1 is None:
            self.S.op(eng, lambda e: e.tensor_scalar(out=out, in0=in0, scalar1=s1, scalar2=s2, op0=op0), reads, writes)
        else:
            self.S.op(eng, lambda e: e.tensor_scalar(out=out, in0=in0, scalar1=s1, scalar2=s2, op0=op0, op1=op1), reads, writes)

    def tt(self, eng, out, in0, in1, op, reads, writes):
        self.S.op(eng, lambda e: e.tensor_tensor(out=out, in0=in0, in1=in1, op=op), reads, writes)

    def stt(self, out, in0, scalar, in1, op0, op1, reads, writes):
        self.S.op("dve", lambda e: e.scalar_tensor_tensor(out=out, in0=in0, scalar=scalar, in1=in1, op0=op0, op1=op1), reads, writes)

    def cp(self, eng, out, in_, reads, writes):
        if eng == "act":
            self.S.op("act", lambda e: e.activation(out=out, in_=in_, func=AF.Copy), reads, writes)
        else:
            self.S.op(eng, lambda e: e.tensor_copy(out=out, in_=in_), reads, writes)

    def memset(self, eng, ap, val, writes):
        self.S.op(eng, lambda e: e.memset(ap, val), (), writes)

    def dma(self, q, out, in_, reads, writes):
        self.S.dma(q, out, in_, reads, writes)

    def psA(self):
        k = self.nxt("psA", 3)
        return self.PSA[k], ("psA", k)

    def norm_tile(self, xin, xreg, gb, dst, dstreg, evac_eng):
        b = self.nxt("hn", 2)
        ss = self.STAT[:, 2 * b:2 * b + 1]
        rs = self.STAT[:, 2 * b + 1:2 * b + 2]
        hn = self.HN[b]
        self.act(self.SQ[:], xin, AF.Square, [xreg], [("sq",), ("ss", b)], scale=1.0 / 32.0, accum_out=ss)
        self.act(ss, ss, AF.Sqrt, [("ss", b)], [("ss", b)], bias=self.EPSC[:, 0:1])
        self.S.op("dve", lambda e: e.reciprocal(out=rs, in_=ss), [("ss", b)], [("rs", b)])
        if gb is None:
            self.ts("dve", hn[:], xin, rs, None, ALU.mult, None, [xreg, ("rs", b)], [("hn", b)])
        else:
            self.stt(hn[:], xin, rs, gb, ALU.mult, ALU.mult, [xreg, ("rs", b), ("gb",)], [("hn", b)])
        self.pe([("tr", self.PST[:, c, :], hn[:, c * 128:(c + 1) * 128]) for c in range(8)], [("hn", b)], [("psT",)])
        self.cp(evac_eng, dst, self.PST[:, :, :], [("psT",)], [dstreg])

    def load_gb(self, idx):
        self.dma("sp", self.GB[:], self.d_gb[idx], [], [("gb",)])

    def wload(self, dst, src2d, reg, fold=None):
        self.dma("pool", dst, src2d.rearrange("(c p) n -> p c n", p=128), [], [reg])
        if fold is not None:
            for c in range(8):
                self.ts("pool", dst[:, c, :], dst[:, c, :], self.GC[:, fold, c:c + 1], 1.0, ALU.mult, ALU.mult, [reg], [reg])

    def attn_softmax_pair(self, QZ, KZ, VP, nkt, causal, oc, kreg, vreg):
        items = []
        for qb in range(4):
            for hh in range(2):
                kts = list(range(4 * qb + 4)) if causal else list(range(nkt))
                for kt in kts:
                    items.append((qb, hh, kt, kt == kts[0], kt == kts[-1]))
        n = len(items)
        pend = {}

        def stage_a(it):
            qb, hh, kt, first, last = it
            j0 = kt - 4 * qb if causal else -1
            col0 = max(0, j0) * 128
            S_, sreg = self.psA()
            qc = slice(qb * 512 + col0, (qb + 1) * 512)
            self.pe([(S_[:, col0:512], KZ[hh][:, kt * 128:(kt + 1) * 128], QZ[hh][:, qc], True, True)],
                    [kreg(hh, kt), ("QZ", hh, qb)], [sreg])
            pb = self.nxt("PT", 3)
            PT = self.PT[pb]
            self.act(PT[:, col0:512], S_[:, col0:512], AF.Exp, [sreg], [("PT", pb)])
            if j0 >= 0:
                self.tt("dve", PT[:, col0:col0 + 128], PT[:, col0:col0 + 128], self.TRI, ALU.mult, [("PT", pb)], [("PT", pb)])
            pend[it] = (pb, col0)

        def stage_c(it):
            qb, hh, kt, first, last = it
            pb, col0 = pend.pop(it)
            PT = self.PT[pb]
            if first:
                self.accb = self.nxt("accb", 2)
            ab = self.accb
            self.pe([(self.OACC[ab][:, col0:512], VP(kt), PT[:, col0:512], first, last),
                     (self.LACC[ab][:, col0:512], self.ONES, PT[:, col0:512], first, last)],
                    [("PT", pb), vreg(kt)], [("acc", ab), ("acc", 2 + ab)])
            if last:
                rows = slice(64 * hh, 64 * hh + 64)
                rb = self.RB[ab]
                self.S.op("dve", lambda e: e.reciprocal(out=rb[rows, :], in_=self.LACC[ab][rows, :]), [("acc", 2 + ab)], [("RB", ab)])
                self.tt("dve", self.OT[rows, oc, qb * 512:(qb + 1) * 512], self.OACC[ab][rows, :], rb[rows, :], ALU.mult,
                        [("acc", ab), ("RB", ab)], [("OT", oc, 4 * qb + j) for j in range(4)])

        for i in range(n + 1):
            if i < n:
                stage_a(items[i])
            if i >= 1:
                stage_c(items[i - 1])

    def attn_sb_pair(self, QZ, KP, VP, oc):
        items = []
        for qb in range(4):
            for hh in range(2):
                kts = list(reversed(range(4 * qb + 4)))
                for kt in kts:
                    items.append((qb, hh, kt, kt == kts[0], kt == kts[-1]))
        n = len(items)
        pend = {}

        def stage_a(it):
            qb, hh, kt, first, last = it
            j0 = kt - 4 * qb
            col0 = max(0, j0) * 128
            Z, zreg = self.psA()
            qc = slice(qb * 512 + col0, (qb + 1) * 512)
            self.pe([(Z[:, col0:512], KP[:, kt * 128:(kt + 1) * 128], QZ[hh][:, qc], True, False)],
                    [("KZ", 0, kt // 4), ("QZ", hh, qb)], [zreg])
            eb = self.nxt("E", 2)
            E = self.EX[eb]
            self.act(E[:, col0:512], Z[:, col0:512], AF.Exp, [zreg], [("E", eb)])
            sb = self.nxt("SP", 3)
            SP = self.SPT[sb]
            self.act(SP[:, col0:512], E[:, col0:512], AF.Ln, [("E", eb)], [("SP", sb)], bias=1.0)
            if j0 >= 0:
                self.tt("pool", SP[:, col0:col0 + 128], SP[:, col0:col0 + 128], self.TRIS, ALU.mult, [("SP", sb)], [("SP", sb)])
            pend[it] = (Z, zreg, sb, col0, j0)

        def stage_c(it):
            qb, hh, kt, first, last = it
            Z, zreg, sb, col0, j0 = pend[it]
            SP = self.SPT[sb]
            if first:
                self.memset("pool", self.SL[:, :], 0.0, [("SL",)])
            mms = [(Z[:, col0:512], self.NTRI, SP[:, col0:512], False, first)]
            rd = [("SP", sb)]
            if not first:
                mms.append((Z[:, col0:512], self.NONES, self.SL[:, col0:512], False, True))
                rd.append(("SL",))
            self.pe(mms, rd, [zreg])
            if not last:
                self.tt("dve", self.SL[:, col0:512], self.SL[:, col0:512], SP[:, col0:512], ALU.add, [("SL",), ("SP", sb)], [("SL",)])
            pb = self.nxt("PT", 3)
            PT = self.PT[pb]
            self.act(PT[:, col0:512], Z[:, col0:512], AF.Exp, [zreg], [("PT", pb)])
            if j0 >= 0:
                self.tt("dve", PT[:, col0:col0 + 128], PT[:, col0:col0 + 128], self.TRIS, ALU.mult, [("PT", pb)], [("PT", pb)])
            pend[it] = (pb, col0)

        def stage_e(it):
            qb, hh, kt, first, last = it
            pb, col0 = pend.pop(it)
            PT = self.PT[pb]
            mms = []
            if first:
                self.accb = self.nxt("accb", 2)
                mms.append((self.OACC[self.accb][:, :], self.ZEROS, QZ[hh][:, qb * 512:(qb + 1) * 512], True, False))
            ab = self.accb
            mms.append((self.OACC[ab][:, col0:512], VP(kt), PT[:, col0:512], False, last))
            self.pe(mms, [("PT", pb), ("V", kt), ("QZ", hh, qb)], [("acc", ab)])
            if last:
                rows = slice(64 * hh, 64 * hh + 64)
                self.cp("dve", self.OT[rows, oc, qb * 512:(qb + 1) * 512], self.OACC[ab][rows, :],
                        [("acc", ab)], [("OT", oc, 4 * qb + j) for j in range(4)])

        for i in range(n + 2):
            if i < n:
                stage_a(items[i])
            if 1 <= i <= n:
                stage_c(items[i - 1])
            if i >= 2:
                stage_e(items[i - 2])

    def proj_fm(self, W, wsl, blk, wreg, dsts, scale, eng):
        ps, preg = self.psA()
        tsl = slice(blk * 512, (blk + 1) * 512)
        self.pe([(ps[:, :], W[:, c, wsl], self.HT[:, c, tsl], c == 0, c == 7) for c in range(8)],
                [wreg] + [("HT", 4 * blk + j) for j in range(4)], [preg])
        for dst, rows, dreg in dsts:
            if eng == "act":
                self.S.op("act", lambda e, dst=dst, rows=rows: e.activation(out=dst, in_=ps[rows, :], func=AF.Copy, scale=scale), [preg], [dreg])
            else:
                self.ts(eng, dst, ps[rows, :], scale, None, ALU.mult, None, [preg], [dreg])

    def v_proj(self, W, wsl, n, i, wreg, dst, src, sreg, vreg):
        ps, preg = self.psA()
        self.pe([(ps[:, 0:n], src[:, c, i * 128:(i + 1) * 128], W[:, c, wsl], c == 0, c == 7) for c in range(8)],
                [wreg, sreg], [preg])
        self.cp("dve", dst, ps[:, 0:n], [preg], [vreg])

    def qk_proj(self, W, b, l0):
        h0 = slice(0, 64)
        h1 = slice(64, 128)
        for blk in range(4):
            tsl = slice(blk * 512, (blk + 1) * 512)
            self.proj_fm(W, slice(0, 128), blk, ("WP", b),
                         [(self.QZ[0][h0, tsl], h0, ("QZ", 0, blk)), (self.QZ[1][h1, tsl], h1, ("QZ", 1, blk))], 0.125, "act")
        for blk in range(4):
            tsl = slice(blk * 512, (blk + 1) * 512)
            if l0:
                dsts = [(self.KZ[0][h0, tsl], h0, ("KZ", 0, blk)), (self.KZ[1][h1, tsl], h1, ("KZ", 1, blk))]
            else:
                dsts = [(self.KZ[0][:, tsl], slice(0, 128), ("KZ", 0, blk))]
            self.proj_fm(W, slice(128, 256), blk, ("WP", b), dsts, 1.0, "dve")

    def mem_branch(self, s, l):
        self.load_gb(1 + l)
        self.wload(self.WM, self.d_wmemkv[l], ("WM",))
        for mt in range(2):
            self.dma("sp", self.MTMP[:], self.d_mem[s, mt * 128:(mt + 1) * 128, :], [], [("mtmp",)])
            self.norm_tile(self.MTMP[:], ("mtmp",), self.GB[:], self.HMT[:, :, mt * 128:(mt + 1) * 128], ("HMT", mt), "act")
        for p in range(2):
            ps, preg = self.psA()
            self.pe([(ps[:, 0:256], self.WM[:, c, 128 * p:128 * p + 128], self.HMT[:, c, :], c == 0, c == 7) for c in range(8)],
                    [("WM",), ("HMT", 0), ("HMT", 1)], [preg])
            self.cp("act", self.MKT[:, p, :], ps[:, 0:256], [preg], [("MKT", p)])
        for mt in range(2):
            self.v_proj(self.WM, slice(256, 512), 256, mt, ("WM",), self.MVP[:, mt, :], self.HMT, ("HMT", mt), ("MV", mt))

    def mem_attn(self, l, qsrc, qcol0, fold):
        h0 = slice(0, 64)
        h1 = slice(64, 128)
        for p in range(2):
            b = self.nxt("WP", 2)
            self.wload(self.WP[b][:, :, 0:128], qsrc[:, qcol0 + 128 * p:qcol0 + 128 * p + 128], ("WP", b), fold)
            for blk in range(4):
                tsl = slice(blk * 512, (blk + 1) * 512)
                self.proj_fm(self.WP[b], slice(0, 128), blk, ("WP", b),
                             [(self.QZ[0][h0, tsl], h0, ("QZ", 0, blk)), (self.QZ[1][h1, tsl], h1, ("QZ", 1, blk))], 0.125, "act")
            MK = self.MKT[:, p, :]
            self.attn_softmax_pair(self.QZ, [MK, MK], lambda kt, p=p: self.MVP[:, kt, 128 * p:128 * p + 128], 2, False, 6 + p,
                                   lambda hh, kt, p=p: ("MKT", p), lambda kt: ("MV", kt))

    def out_proj(self, l):
        self.wload(self.WO, self.d_wout[l], ("WO",))
        for i in range(NT):
            a = self.nxt("acc2", 2)
            ps = self.ACC2[a]
            regs = [("acc", 2 * a), ("acc", 2 * a + 1)]
            mms = []
            for h in range(2):
                for c in range(8):
                    mms.append((ps[:, 512 * h:512 * h + 512], self.OT[:, c, i * 128:(i + 1) * 128], self.WO[:, c, 512 * h:512 * h + 512], c == 0, c == 7))
            self.pe(mms, [("WO",)] + [("OT", c, i) for c in range(8)], regs)
            self.tt("dve", self.X[:, i, :], ps[:, :], self.X[:, i, :], ALU.add, regs + [("X", i)], [("X", i)])

    def ffn(self, l):
        self.load_gb(3 + l)
        self.dma("sp", self.CW[:], self.d_cw[l], [], [("cw",)])
        self.dma("sp", self.CB[:], self.d_cb[l], [], [("cb",)])
        for i in range(NT):
            self.norm_tile(self.X[:, i, :], ("X", i), self.GB[:], self.HT[:, :, i * 128:(i + 1) * 128], ("HT", i), "act" if i % 2 else "dve")
        wup = self.d_wup[l]
        wdn = self.d_wdown[l]

        def load_chunk(fi):
            b = fi % 2
            f0 = 128 * sum(FCH[:fi])
            nf = 128 * FCH[fi]
            self.wload(self.WUP[b][:, :, 0, 0:nf], wup[:, f0:f0 + nf], ("WUP", b, 0))
            self.wload(self.WUP[b][:, :, 1, 0:nf], wup[:, DFF + f0:DFF + f0 + nf], ("WUP", b, 1))
            self.wload(self.WDN[b][:, 0:FCH[fi], :], wdn[f0:f0 + nf, :], ("WDN", b))

        def down(fi):
            b = fi % 2
            for i in range(NT):
                a = self.nxt("acc2", 2)
                ps = self.ACC2[a]
                regs = [("acc", 2 * a), ("acc", 2 * a + 1)]
                mms = []
                for h in range(2):
                    for k in range(FCH[fi]):
                        mms.append((ps[:, 512 * h:512 * h + 512], self.AT[b][:, k, i * 128:(i + 1) * 128],
                                    self.WDN[b][:, k, 512 * h:512 * h + 512], k == 0, k == FCH[fi] - 1))
                self.pe(mms, [("WDN", b)] + [("AT", b, k, i // 4) for k in range(FCH[fi])], regs)
                self.tt("dve", self.X[:, i, :], ps[:, :], self.X[:, i, :], ALU.add, regs + [("X", i)], [("X", i)])

        load_chunk(0)
        for fi in range(len(FCH)):
            b = fi % 2
            if fi + 1 < len(FCH):
                load_chunk(fi + 1)
            cbase = sum(FCH[:fi])
            for k in range(FCH[fi]):
                for tb in range(4):
                    tsl = slice(tb * 512, (tb + 1) * 512)
                    gT = None
                    for gv in range(2):
                        ci = cbase + k + (DFF // 128) * gv
                        ps, preg = self.psA()
                        self.pe([(ps[:, :], self.WUP[b][:, c, gv, 128 * k:128 * k + 128], self.HT[:, c, tsl], c == 0, c == 7) for c in range(8)],
                                [("WUP", b, gv)] + [("HT", 4 * tb + j) for j in range(4)], [preg])
                        UB = self.UBS[gv][tb % 2]
                        ureg = ("UB", gv, tb % 2)
                        if tb == 0:
                            self.memset("pool", UB[:, 0:2], 0.0, [ureg])
                        else:
                            self.cp("pool", UB[:, 0:2], self.UBS[gv][(tb + 1) % 2][:, 512:514], [("UB", gv, (tb + 1) % 2)], [ureg])
                        self.cp("act", UB[:, 2:514], ps[:, :], [preg], [ureg])
                        tbuf = self.nxt("T%d" % gv, 2)
                        T = (self.TG if gv == 0 else self.TV)[tbuf]
                        treg = ("T", gv, tbuf)
                        self.act(T[:, :], ps[:, :], AF.Identity, [preg, ("cw",), ("cb",)], [treg],
                                 scale=self.CW[:, ci, 2:3], bias=self.CB[:, ci:ci + 1])
                        self.stt(T[:, :], UB[:, 1:513], self.CW[:, ci, 1:2], T[:, :], ALU.mult, ALU.add, [ureg, treg, ("cw",)], [treg])
                        self.stt(T[:, :], UB[:, 0:512], self.CW[:, ci, 0:1], T[:, :], ALU.mult, ALU.add, [ureg, treg, ("cw",)], [treg])
                        if gv == 0:
                            self.act(T[:, :], T[:, :], AF.Silu, [treg], [treg])
                            gT = (T, treg)
                        else:
                            self.tt("dve", self.AT[b][:, k, tsl], gT[0][:, :], T[:, :], ALU.mult, [gT[1], treg], [("AT", b, k, tb)])
            down(fi)

    def layer0_mixer(self, s):
        self.mem_branch(s, 0)
        self.S.fence()
        self.load_gb(0)
        for hh in range(2):
            self.memset("pool", self.QZ[hh][:, :], 0.0, [("QZ", hh, k) for k in range(4)])
            self.memset("pool", self.KZ[hh][:, :], 0.0, [("KZ", hh, k) for k in range(4)])
        brow = [64, 0]
        for hh in range(2):
            r = brow[hh]
            self.memset("pool", self.QZ[hh][r:r + 4, :], -1.0, [("QZ", hh, k) for k in range(4)])
            self.memset("pool", self.KZ[hh][r:r + 4, :], 1.0, [("KZ", hh, k) for k in range(4)])
        for i in range(NT):
            self.norm_tile(self.X[:, i, :], ("X", i), self.GB[:], self.HT[:, :, i * 128:(i + 1) * 128], ("HT", i), "act" if i % 2 else "dve")
        b = self.nxt("WP", 2)
        WF = self.WP[b][:, :, 0:12]
        self.wload(WF, self.d_wina[:, 2304:2316], ("WP", b))
        for blk in range(4):
            ps, preg = self.psA()
            tsl = slice(blk * 512, (blk + 1) * 512)
            self.pe([(ps[0:12, :], WF[:, c, :], self.HT[:, c, tsl], c == 0, c == 7) for c in range(8)],
                    [("WP", b)] + [("HT", 4 * blk + j) for j in range(4)], [preg])
            self.act(self.CF[0:12, tsl], ps[0:12, :], AF.Exp, [preg], [("CF", blk)], scale=-1.0, bias=self.NB[:, 0:1])
            self.act(self.CF[0:12, tsl], self.CF[0:12, tsl], AF.Ln, [("CF", blk)], [("CF", blk)], bias=1.0)
            self.ts("dve", self.CF[0:12, tsl], self.CF[0:12, tsl], -0.5, None, ALU.mult, None, [("CF", blk)], [("CF", blk)])
        self.S.op("dve", lambda e: e.tensor_tensor_scan(out=self.CFO[0:12, :], data0=self.CF[0:12, :], data1=self.CF[0:12, :],
                                                         initial=0.0, op0=ALU.add, op1=ALU.add),
                  [("CF", k) for k in range(4)], [("CFO",)])
        self.cp("dve", self.CH[0:12, 0, :], self.CFO[0:12, :], [("CFO",)], [("CH", 0)])
        self.tt("dve", self.CH[0:12, 1, :], self.CFO[0:12, :], self.CH[0:12, 0, :], ALU.subtract, [("CFO",), ("CH", 0)], [("CH", 1)])
        if "cf" in self.d_dbg:
            self.dma("sp", self.d_dbg["cf"], self.CFO[0:12, :], [("CFO",)], [])
        allq = lambda hh: [("QZ", hh, k) for k in range(4)]
        allk = lambda hh: [("KZ", hh, k) for k in range(4)]
        for p in range(6):
            b = self.nxt("WP", 2)
            for k in range(3):
                self.wload(self.WP[b][:, :, 128 * k:128 * k + 128], self.d_wina[:, 768 * k + 128 * p:768 * k + 128 * p + 128], ("WP", b))
            for hh in range(2):
                h = 2 * p + hh
                r = brow[hh]
                self.dma("sp", self.QZ[hh][r:r + 1, :], self.CH[h:h + 1, 0, :], [("CH", 0)], allq(hh))
                self.dma("sp", self.QZ[hh][r + 1:r + 2, :], self.CH[h:h + 1, 1, :], [("CH", 1)], allq(hh))
                self.dma("sp", self.KZ[hh][r + 2:r + 3, :], self.CH[h:h + 1, 0, :], [("CH", 0)], allk(hh))
                self.dma("sp", self.KZ[hh][r + 3:r + 4, :], self.CH[h:h + 1, 1, :], [("CH", 1)], allk(hh))
            self.qk_proj(self.WP[b], b, True)
            for i in range(NT):
                self.v_proj(self.WP[b], slice(256, 384), 128, i, ("WP", b), self.VP[:, i, :], self.HT, ("HT", i), ("V", i))
            self.attn_softmax_pair(self.QZ, self.KZ, lambda kt: self.VP[:, kt, :], None, True, p,
                                   lambda hh, kt: ("KZ", hh, kt // 4), lambda kt: ("V", kt))
        for hh in range(2):
            r = brow[hh]
            self.memset("pool", self.QZ[hh][r:r + 4, :], 0.0, allq(hh))
        self.mem_attn(0, self.d_wina, 2316, None)

    def layer1_mixer(self, s):
        self.mem_branch(s, 1)
        self.S.fence()
        for hh in range(2):
            self.memset("pool", self.QZ[hh][:, :], 0.0, [("QZ", hh, k) for k in range(4)])
        for i in range(NT):
            self.norm_tile(self.X[:, i, :], ("X", i), None, self.HT[:, :, i * 128:(i + 1) * 128], ("HT", i), "act" if i % 2 else "dve")
        for p in range(self.opt.get("l1_pairs", 6)):
            b = self.nxt("WP", 2)
            self.wload(self.WP[b][:, :, 0:128], self.d_winb[:, 128 * p:128 * p + 128], ("WP", b), 1)
            self.wload(self.WP[b][:, :, 128:256], self.d_wkv[:, 128 * p:128 * p + 128], ("WP", b), 0)
            self.wload(self.WP[b][:, :, 256:384], self.d_wkv[:, 768 + 128 * p:768 + 128 * p + 128], ("WP", b), 0)
            self.qk_proj(self.WP[b], b, False)
            for i in range(NT):
                self.v_proj(self.WP[b], slice(256, 384), 128, i, ("WP", b), self.VP[:, i, :], self.HT, ("HT", i), ("V", i))
            if self.opt.get("l1_attn", True):
                self.attn_sb_pair(self.QZ, self.KZ[0], lambda kt: self.VP[:, kt, :], p)
        if self.opt.get("l1_mem", True):
            self.mem_attn(1, self.d_winb, 768, 1)

    def dump_x(self, name):
        if name in self.d_dbg:
            for i in range(NT):
                self.dma("sp", self.d_dbg[name][i * 128:(i + 1) * 128, :], self.X[:, i, :], [("X", i)], [])

    def final_out(self, s, normed):
        if normed:
            self.load_gb(5)
        for i in range(NT):
            ob = self.nxt("outt", 2)
            O = self.OUTT[ob]
            if normed:
                b = self.nxt("hn", 2)
                ss = self.STAT[:, 2 * b:2 * b + 1]
                rs = self.STAT[:, 2 * b + 1:2 * b + 2]
                xin = self.X[:, i, :]
                self.act(self.SQ[:], xin, AF.Square, [("X", i)], [("sq",), ("ss", b)], scale=1.0 / 32.0, accum_out=ss)
                self.act(ss, ss, AF.Sqrt, [("ss", b)], [("ss", b)], bias=self.EPSC[:, 0:1])
                self.S.op("dve", lambda e, rs=rs, ss=ss: e.reciprocal(out=rs, in_=ss), [("ss", b)], [("rs", b)])
                self.stt(O[:], xin, rs, self.GB[:], ALU.mult, ALU.mult, [("X", i), ("rs", b), ("gb",)], [("outt", ob)])
                self.dma("sp", self.d_out[s, i * 128:(i + 1) * 128, :], O[:], [("outt", ob)], [])
            else:
                self.dma("sp", self.d_out[s, i * 128:(i + 1) * 128, :], self.X[:, i, :], [("X", i)], [])

    def program(self):
        order = ["l0mix", "l0out", "l0ffn", "l1mix", "l1out", "l1ffn"]
        stop = self.stop
        nph = len(order) if stop is None else order.index(stop) + 1
        self.dma("pool", self.CONST[:], self.d_consts, [], [("const",)])
        self.dma("sp", self.GC[:], self.d_gc, [], [("gc",)])
        self.dma("sp", self.NB[:], self.d_bf, [], [("nb",)])
        self.memset("dve", self.EPSC[:], EPS, [("eps",)])
        self.ts("dve", self.NB[:], self.NB[:], -1.0, None, ALU.mult, None, [("nb",)], [("nb",)])
        self.S.fence()
        for s in range(self.nseq):
            for i in range(NT):
                self.dma("sp", self.X[:, i, :], self.d_x[s, i * 128:(i + 1) * 128, :], [], [("X", i)])
            for ph in (order[:nph] if self.phases is None else self.phases):
                if ph == "l0mix":
                    self.layer0_mixer(s)
                elif ph == "l1mix":
                    self.layer1_mixer(s)
                elif ph == "l0out":
                    self.out_proj(0)
                    if s == 0:
                        self.dump_x("x_l0mix")
                elif ph == "l1out":
                    self.out_proj(1)
                    if s == 0:
                        self.dump_x("x_l1mix")
                elif ph == "l0ffn":
                    self.ffn(0)
                    if s == 0:
                        self.dump_x("x_l0")
                elif ph == "l1ffn":
                    self.ffn(1)
                self.S.fence()
            self.final_out(s, stop is None)
            self.S.fence()

    def build(self):
        nc = self.nc
        nseq = self.nseq
        dt_in = lambda name, shape: nc.dram_tensor(name, shape, F32, kind="ExternalInput").ap()
        self.d_x = dt_in("x", [nseq, SEQ, D])
        self.d_mem = dt_in("mem", [nseq, NMEM, D])
        self.d_wina = dt_in("w_in_a", [D, A_IN])
        self.d_winb = dt_in("w_in_b", [D, D])
        self.d_wkv = dt_in("w_kv", [D, 1536])
        self.d_wmemkv = dt_in("w_memkv", [2, D, 512])
        self.d_wout = dt_in("w_out", [2, D, D])
        self.d_wup = dt_in("w_up", [2, D, 2 * DFF])
        self.d_wdown = dt_in("w_down", [2, DFF, D])
        self.d_gb = dt_in("gb", [6, 128, D])
        self.d_gc = dt_in("gc", [128, 2, 8])
        self.d_bf = dt_in("bf", [12, 1])
        self.d_cw = dt_in("cw", [2, 128, 44, 3])
        self.d_cb = dt_in("cb", [2, 128, 44])
        self.d_consts = dt_in("consts", [128, 7, 128])
        self.d_out = nc.dram_tensor("out", [nseq, SEQ, D], F32, kind="ExternalOutput").ap()
        self.d_dbg = {}
        for name, shape in self.dbg_shapes().items():
            if name in self.dbg:
                self.d_dbg[name] = nc.dram_tensor("dbg_" + name, shape, F32, kind="ExternalOutput").ap()

        with ExitStack() as es:
            sb = lambda name, shape, dt: es.enter_context(nc.sbuf_tensor(name, shape, dt))
            self.X = sb("X", [128, NT, D], F32)
            self.HT = sb("HT", [128, 8, SEQ], BF16)
            self.CONST = sb("CONST", [128, 7, 128], BF16)
            self.IDENT = self.CONST[:, 0, :]
            self.TRI = self.CONST[:, 1, :]
            self.TRIS = self.CONST[:, 2, :]
            self.NTRI = self.CONST[:, 3, :]
            self.NONES = self.CONST[:, 4, :]
            self.ZEROS = self.CONST[:, 5, :]
            self.ONES = self.CONST[:, 6, :]
            self.GB = sb("GB", [128, D], F32)
            self.GC = sb("GC", [128, 2, 8], F32)
            self.NB = sb("NB", [12, 1], F32)
            self.CW = sb("CW", [128, 44, 3], F32)
            self.CB = sb("CB", [128, 44], F32)
            self.STAT = sb("STAT", [128, 16], F32)
            self.EPSC = sb("EPSC", [128, 1], F32)
            self.HN = [sb("HN%d" % i, [128, D], BF16) for i in range(2)]
            self.SQ = sb("SQ", [128, D], BF16)
            self.ARENA_B = 101000
            self.ARENA = sb("ARENA", [128, self.ARENA_B // 2], BF16)
            self.PSA = [es.enter_context(nc.psum_tensor("psA%d" % i, [128, 512], F32)) for i in range(3)]
            self.ACC2 = [es.enter_context(nc.psum_tensor("acc2_%d" % i, [128, 1024], F32)) for i in range(2)]
            self.OACC = [self.ACC2[0][:, 0:512], self.ACC2[0][:, 512:1024]]
            self.LACC = [self.ACC2[1][:, 0:512], self.ACC2[1][:, 512:1024]]
            self.PST = es.enter_context(nc.psum_tensor("psT", [128, 8, 128], BF16))
            esem = {e: es.enter_context(nc.semaphore("sem_" + e)) for e in CENGS}
            dsem = [es.enter_context(nc.semaphore("dsem%d" % i)) for i in range(24)]
            self.S = Sched(esem, dsem)
            self.carve_all()
            self.program()
            self.emit()
        return nc

    def dbg_shapes(self):
        return {"x_l0mix": [SEQ, D], "x_l0": [SEQ, D], "x_l1mix": [SEQ, D], "cf": [12, SEQ]}

    def carve(self, off, shape, dt):
        n = int(np.prod(shape[1:]))
        esz = 2 if dt == BF16 else 4
        assert off % 4 == 0, off
        a = self.ARENA[:, off // 2:off // 2 + n * esz // 2]
        if dt == F32:
            a = a.bitcast(F32)
        if len(shape) == 3:
            a = a.rearrange("p (a b) -> p a b", a=shape[1])
        elif len(shape) == 4:
            a = a.rearrange("p (a b c) -> p a b c", a=shape[1], b=shape[2])
        self._off = off + n * esz
        assert self._off <= self.ARENA_B, self._off
        return a

    def carve_all(self):
        al = lambda o: (o + 3) // 4 * 4
        self.OT = self.carve(0, [128, 8, SEQ], BF16)
        o = self._off
        self.MTMP = self.carve(0, [128, D], F32)
        self.WM = self.carve(self._off, [128, 8, 512], BF16)
        self.HMT = self.carve(self._off, [128, 8, NMEM], BF16)
        self.QZ = []
        self.KZ = []
        for i in range(2):
            self.QZ.append(self.carve(o, [128, SEQ], BF16)); o = self._off
        for i in range(2):
            self.KZ.append(self.carve(o, [128, SEQ], BF16)); o = self._off
        self.VP = self.carve(o, [128, NT, 128], BF16); o = self._off
        self.WP = []
        for i in range(2):
            self.WP.append(self.carve(o, [128, 8, 384], BF16)); o = self._off
        self.PT = []
        for i in range(3):
            self.PT.append(self.carve(o, [128, 512], BF16)); o = self._off
        self.MKT = self.carve(o, [128, 2, NMEM], BF16); o = self._off
        self.MVP = self.carve(o, [128, 2, 256], BF16); o = self._off
        self.RB = []
        for i in range(2):
            self.RB.append(self.carve(o, [128, 512], F32)); o = self._off
        o0 = o
        self.CH = self.carve(o, [128, 2, SEQ], BF16); o = self._off
        self.CF = self.carve(o, [128, SEQ], F32); o = self._off
        o = o0
        self.EX = []
        for i in range(2):
            self.EX.append(self.carve(o, [128, 512], F32)); o = self._off
        self.SPT = []
        for i in range(3):
            self.SPT.append(self.carve(o, [128, 512], BF16)); o = self._off
        self.SL = self.carve(o, [128, 512], BF16); o = self._off
        self.CFO = self.carve(self.ARENA_B - 8192 - 8, [128, SEQ], F32)
        self.WO = self.carve(32768, [128, 8, D], BF16)
        o = 0
        self.WUP = []
        for i in range(2):
            self.WUP.append(self.carve(o, [128, 8, 2, 512], BF16)); o = self._off
        self.WDN = []
        for i in range(2):
            self.WDN.append(self.carve(o, [128, 4, D], BF16)); o = self._off
        self.AT = []
        for i in range(2):
            self.AT.append(self.carve(o, [128, 4, SEQ], BF16)); o = self._off
        self.UBS = [[None, None], [None, None]]
        for g in range(2):
            for i in range(2):
                self.UBS[g][i] = self.carve(o, [128, 516], F32); o = self._off
        self.TG = []
        self.TV = []
        for i in range(2):
            self.TG.append(self.carve(o, [128, 512], F32)); o = self._off
            self.TV.append(self.carve(o, [128, 512], F32)); o = self._off
        self.OUTT = [self.carve(0, [128, D], F32), self.carve(4096, [128, D], F32)]

    def emit(self):
        nc = self.nc
        S = self.S
        with nc.Block() as block:
            @block.tensor
            def _(e):
                for f in S.prog["pe"]:
                    f(e)

            @block.scalar
            def _(e):
                for f in S.prog["act"]:
                    f(e)

            @block.vector
            def _(e):
                for f in S.prog["dve"]:
                    f(e)

            @block.gpsimd
            def _(e):
                for f in S.prog["pool"]:
                    f(e)

            @block.sync
            def _(e):
                for f in S.prog["sp"]:
                    f(e)


def host_inputs(x, mem, ln_mix_g, w_in_a, b_f_a, w_in_b, ln_kv_g, w_kv, ln_mem_g, w_memkv, w_out, ln_ffn_g,
                w_up, conv_w, conv_b, w_down, final_g):
    f = lambda a: np.ascontiguousarray(np.asarray(a, dtype=np.float32))
    rep = lambda g: np.broadcast_to(np.asarray(g, np.float32)[None, :], (128, D))
    gb = f(np.stack([rep(ln_mix_g[0]), rep(ln_mem_g[0]), rep(ln_mem_g[1]), rep(ln_ffn_g[0]), rep(ln_ffn_g[1]), rep(final_g)]))
    col = lambda g: np.asarray(g, np.float32).reshape(8, 128).T
    gc = f(np.stack([col(ln_kv_g), col(ln_mix_g[1])], axis=1))
    cw = f(np.asarray(conv_w, np.float32).reshape(2, 3, 44, 128).transpose(0, 3, 2, 1))
    cb = f(np.asarray(conv_b, np.float32).reshape(2, 44, 128).transpose(0, 2, 1))
    i = np.arange(128)
    ident = (i[:, None] == i[None, :]).astype(np.float32)
    tri = (i[:, None] <= i[None, :]).astype(np.float32)
    tris = (i[:, None] < i[None, :]).astype(np.float32)
    ntri = -(i[:, None] >= i[None, :]).astype(np.float32)
    nones = -np.ones((128, 128), np.float32)
    consts = f(np.stack([ident, tri, tris, ntri, nones, 0.0 * nones, -nones], axis=1))
    return {
        "w_in_a": f(w_in_a[0]), "w_in_b": f(w_in_b[0]), "w_kv": f(w_kv), "w_memkv": f(w_memkv), "w_out": f(w_out),
        "w_up": f(w_up), "w_down": f(w_down), "gb": gb, "gc": gc, "bf": f(np.asarray(b_f_a, np.float32).reshape(12, 1)),
        "cw": cw, "cb": cb, "consts": consts,
    }


def kernel(**inputs):
    x = np.asarray(inputs["x"], np.float32)
    mem = np.asarray(inputs["mem"], np.float32)
    shared = host_inputs(**inputs)
    nseq = x.shape[0] // NCORES
    nc = Builder(nseq=nseq).build()
    in_maps = []
    for c in range(NCORES):
        m = dict(shared)
        m["x"] = np.ascontiguousarray(x[c * nseq:(c + 1) * nseq])
        m["mem"] = np.ascontiguousarray(mem[c * nseq:(c + 1) * nseq])
        in_maps.append(m)
    res = run_bass_kernel_spmd(nc, in_maps, core_ids=list(range(NCORES)))
    return np.concatenate([np.asarray(r["out"], np.float32) for r in res.results], axis=0)
```

```python
import numpy as np
from contextlib import ExitStack
import concourse.bass as bass
import concourse.mybir as mybir
from concourse.bass_utils import run_bass_kernel_spmd

F32 = mybir.dt.float32
BF16 = mybir.dt.bfloat16
AF = mybir.ActivationFunctionType
ALU = mybir.AluOpType

D = 1024
SEQ = 2048
NT = SEQ // 128
NMEM = 256
DFF = 2816
A_IN = 2572
EPS = 1e-6
NCORES = 8
ENGS = ["pe", "act", "dve", "pool", "sp"]
CENGS = ["pe", "act", "dve", "pool"]
FCH = [4, 4, 4, 4, 4, 2]


class Sched:
    def __init__(self, esem, dsem):
        self.esem = esem
        self.dsem = dsem
        self.prog = {e: [] for e in ENGS}
        self.cnt = {e: 0 for e in CENGS}
        self.dcnt = [0] * len(dsem)
        self.dpool = {"sp": list(range(0, 16)), "pool": list(range(16, 24)), "act": list(range(16, 24))}
        self.dnext = {"sp": 0, "pool": 0, "act": 0}
        self.known = {e: {} for e in ENGS}
        self.state = {}

    def _sem(self, src):
        return self.esem[src] if isinstance(src, str) else self.dsem[src]

    def _need(self, eng, reads, writes):
        need = {}

        def add(w, kind):
            src, val = w
            if src == eng and (eng == "pe" or kind != "RAW"):
                return
            if need.get(src, 0) < val:
                need[src] = val

        for r in reads:
            st = self.state.get(r)
            if st is not None and st[0] is not None:
                add(st[0], "RAW")
        for w in writes:
            st = self.state.get(w)
            if st is not None:
                if st[0] is not None:
                    add(st[0], "WAW")
                for s_, v_ in st[1].items():
                    add((s_, v_), "WAR")
        out = []
        kn = self.known[eng]
        for src, val in need.items():
            if kn.get(src, 0) < val:
                kn[src] = val
                out.append((self._sem(src), val))
        return out

    def _update(self, me, reads, writes):
        src, val = me
        for r in reads:
            st = self.state.get(r)
            if st is None:
                st = [None, {}]
                self.state[r] = st
            st[1][src] = val
        for w in writes:
            self.state[w] = [me, {}]

    def op(self, eng, fn, reads=(), writes=()):
        wl = self._need(eng, reads, writes)
        self.cnt[eng] += 1
        sem = self.esem[eng]

        def emit(e, wl=wl, fn=fn, sem=sem):
            for sm, v in wl:
                e.wait_ge(sm, v)
            fn(e).then_inc(sem, 1)

        self.prog[eng].append(emit)
        self._update((eng, self.cnt[eng]), reads, writes)

    def dma(self, q, out, in_, reads=(), writes=()):
        pl = self.dpool[q]
        j = pl[self.dnext[q] % len(pl)]
        self.dnext[q] += 1
        wl = self._need(q, reads, writes)
        prev = 16 * self.dcnt[j]
        if prev > 0 and self.known[q].get(j, 0) < prev:
            self.known[q][j] = prev
            wl.append((self.dsem[j], prev))
        self.dcnt[j] += 1
        sem = self.dsem[j]

        def emit(e, wl=wl, out=out, in_=in_, sem=sem):
            for sm, v in wl:
                e.wait_ge(sm, v)
            e.dma_start(out=out, in_=in_).then_inc(sem, 16)

        self.prog[q].append(emit)
        self._update((j, 16 * self.dcnt[j]), reads, writes)

    def fence(self):
        srcs = [(e, self.cnt[e]) for e in CENGS] + [(j, 16 * c) for j, c in enumerate(self.dcnt)]
        for e in ENGS:
            wl = []
            for src, val in srcs:
                if src == e and e == "pe":
                    continue
                if val > self.known[e].get(src, 0):
                    self.known[e][src] = val
                    wl.append((self._sem(src), val))

            def emit(en, wl=wl):
                for sm, v in wl:
                    en.wait_ge(sm, v)

            self.prog[e].append(emit)
        self.state = {}


class Builder:
    def __init__(self, nseq=2, stop=None, dbg=(), phases=None, opt=None):
        self.phases = phases
        self.opt = opt or {}
        self.nseq = nseq
        self.stop = stop
        self.dbg = set(dbg)
        self.nc = bass.Bass("TRN2", target_bir_lowering=False)
        self.rr = {}
        self.norm_pending = []
        self.psa_wide = False

    def nxt(self, name, n):
        v = self.rr.get(name, 0)
        self.rr[name] = (v + 1) % n
        return v

    def pe(self, mms, reads, writes):
        def fn(e, mms=mms):
            ins = None
            for m in mms:
                if m[0] == "tr":
                    ins = e.transpose(out=m[1], in_=m[2], identity=self.IDENT)
                elif len(m) == 6:
                    out, lhsT, rhs, start, stop, _ = m
                    ins = e.matmul(out, lhsT, rhs, start=start, stop=stop, skip_group_check=True)
                else:
                    out, lhsT, rhs, start, stop = m
                    ins = e.matmul(out, lhsT, rhs, start=start, stop=stop)
            return ins

        self.S.op("pe", fn, reads, writes)

    def act(self, out, in_, func, reads, writes, **kw):
        self.S.op("act", lambda e: e.activation(out=out, in_=in_, func=func, **kw), reads, writes)

    def ts(self, eng, out, in0, s1, s2, op0, op1, reads, writes):
        if op1 is None:
            self.S.op(eng, lambda e: e.tensor_scalar(out=out, in0=in0, scalar1=s1, scalar2=s2, op0=op0), reads, writes)
        else:
            self.S.op(eng, lambda e: e.tensor_scalar(out=out, in0=in0, scalar1=s1, scalar2=s2, op0=op0, op1=op1), reads, writes)

    def tt(self, eng, out, in0, in1, op, reads, writes):
        self.S.op(eng, lambda e: e.tensor_tensor(out=out, in0=in0, in1=in1, op=op), reads, writes)

    def stt(self, out, in0, scalar, in1, op0, op1, reads, writes):
        self.S.op("dve", lambda e: e.scalar_tensor_tensor(out=out, in0=in0, scalar=scalar, in1=in1, op0=op0, op1=op1), reads, writes)

    def cp(self, eng, out, in_, reads, writes):
        if eng == "act":
            self.S.op("act", lambda e: e.activation(out=out, in_=in_, func=AF.Copy), reads, writes)
        else:
            self.S.op(eng, lambda e: e.tensor_copy(out=out, in_=in_), reads, writes)

    def memset(self, eng, ap, val, writes):
        self.S.op(eng, lambda e: e.memset(ap, val), (), writes)

    def dma(self, q, out, in_, reads, writes):
        self.S.dma(q, out, in_, reads, writes)

    def psA(self):
        if self.psa_wide:
            k = self.nxt("psAw", 5)
            if k >= 3:
                return self.LACC[k - 3], ("acc", 2 + k - 3)
            return self.PSA[k], ("psA", k)
        k = self.nxt("psA", 3)
        return self.PSA[k], ("psA", k)

    def norm_tile(self, xin, xreg, gb, dst, dstreg, evac_eng):
        b = self.nxt("hn", 2)
        ss = self.STAT[:, 2 * b:2 * b + 1]
        rs = self.STAT[:, 2 * b + 1:2 * b + 2]
        hn = self.HN[b]
        self.act(hn[:], xin, AF.Square, [xreg], [("hn", b), ("ss", b)], scale=1.0 / 32.0, accum_out=ss)
        self.act(ss, ss, AF.Sqrt, [("ss", b)], [("ss", b)], bias=self.EPSC[:, 0:1])
        self.S.op("dve", lambda e: e.reciprocal(out=rs, in_=ss), [("ss", b)], [("rs", b)])
        if gb is None:
            self.ts("dve", hn[:], xin, rs, None, ALU.mult, None, [xreg, ("rs", b)], [("hn", b)])
        else:
            self.stt(hn[:], xin, rs, gb, ALU.mult, ALU.mult, [xreg, ("rs", b), ("gb",)], [("hn", b)])
        def part2():
            self.pe([("tr", self.PST[:, c, :], hn[:, c * 128:(c + 1) * 128]) for c in range(8)], [("hn", b)], [("psT",)])
            self.cp(evac_eng, dst, self.PST[:, :, :], [("psT",)], [dstreg])

        self.norm_flush()
        self.norm_pending.append(part2)

    def norm_flush(self):
        while self.norm_pending:
            self.norm_pending.pop(0)()

    def load_gb(self, idx):
        self.dma("sp", self.GB[:], self.d_gb[idx], [], [("gb",)])

    def wload(self, dst, src2d, reg, fold=None):
        self.dma("pool", dst, src2d.rearrange("(c p) n -> p c n", p=128), [], [reg])
        if fold is not None:
            for c in range(8):
                self.ts("pool", dst[:, c, :], dst[:, c, :], self.GC[:, fold, c:c + 1], 1.0, ALU.mult, ALU.mult, [reg], [reg])

    def attn_softmax_pair(self, QZ, KZ, VP, nkt, causal, oc, kreg, vreg):
        items = []
        for qb in range(4):
            for hh in range(2):
                kts = list(range(4 * qb + 4)) if causal else list(range(nkt))
                for kt in kts:
                    items.append((qb, hh, kt, kt == kts[0], kt == kts[-1]))
        n = len(items)
        pend = {}

        def stage_a(it):
            qb, hh, kt, first, last = it
            j0 = kt - 4 * qb if causal else -1
            col0 = max(0, j0) * 128
            S_, sreg = self.psA()
            qc = slice(qb * 512 + col0, (qb + 1) * 512)
            self.pe([(S_[:, col0:512], KZ[hh][:, kt * 128:(kt + 1) * 128], QZ[hh][:, qc], True, True)],
                    [kreg(hh, kt), ("QZ", hh, qb)], [sreg])
            pb = self.nxt("PT", 3)
            PT = self.PT[pb]
            self.act(PT[:, col0:512], S_[:, col0:512], AF.Exp, [sreg], [("PT", pb)])
            if j0 >= 0:
                self.tt("dve", PT[:, col0:col0 + 128], PT[:, col0:col0 + 128], self.TRI, ALU.mult, [("PT", pb)], [("PT", pb)])
            pend[it] = (pb, col0)

        def stage_c(it):
            qb, hh, kt, first, last = it
            pb, col0 = pend.pop(it)
            PT = self.PT[pb]
            if first:
                self.accb = self.nxt("accb", 2)
            ab = self.accb
            self.pe([(self.OACC[ab][:, col0:512], VP(kt), PT[:, col0:512], first, last),
                     (self.LACC[ab][:, col0:512], self.ONES, PT[:, col0:512], first, last)],
                    [("PT", pb), vreg(kt)], [("acc", ab), ("acc", 2 + ab)])
            if last:
                rows = slice(64 * hh, 64 * hh + 64)
                rb = self.RB[ab]
                self.S.op("dve", lambda e: e.reciprocal(out=rb[rows, :], in_=self.LACC[ab][rows, :]), [("acc", 2 + ab)], [("RB", ab)])
                self.tt("dve", self.OT[rows, oc, qb * 512:(qb + 1) * 512], self.OACC[ab][rows, :], rb[rows, :], ALU.mult,
                        [("acc", ab), ("RB", ab)], [("OT", oc, 4 * qb + j) for j in range(4)])

        for i in range(n + 1):
            if i < n:
                stage_a(items[i])
            if i >= 1:
                stage_c(items[i - 1])

    def attn_sb_pair(self, QZ, KP, VP, oc):
        items = []
        for qb in range(4):
            for hh in range(2):
                kts = list(reversed(range(4 * qb + 4)))
                for kt in kts:
                    items.append((qb, hh, kt, kt == kts[0], kt == kts[-1]))
        n = len(items)
        pend = {}

        def stage_a(it):
            qb, hh, kt, first, last = it
            j0 = kt - 4 * qb
            col0 = max(0, j0) * 128
            Z, zreg = self.psA()
            qc = slice(qb * 512 + col0, (qb + 1) * 512)
            self.pe([(Z[:, col0:512], KP[:, kt * 128:(kt + 1) * 128], QZ[hh][:, qc], True, True)],
                    [("KZ", 0, kt // 4), ("QZ", hh, qb)], [zreg])
            eb = self.nxt("E", 2)
            E = self.EX[eb]
            self.act(E[:, col0:512], Z[:, col0:512], AF.Exp, [zreg], [("E", eb)])
            sb = self.nxt("SP", 4)
            SP = self.SPT[sb]
            self.act(SP[:, col0:512], E[:, col0:512], AF.Ln, [("E", eb)], [("SP", sb)], bias=1.0)
            if j0 >= 0:
                self.tt("pool", SP[:, col0:col0 + 128], SP[:, col0:col0 + 128], self.TRIS, ALU.mult, [("SP", sb)], [("SP", sb)])
            pend[it] = (Z, zreg, sb, col0, j0)

        def stage_c(it):
            qb, hh, kt, first, last = it
            Z, zreg, sb, col0, j0 = pend[it]
            SP = self.SPT[sb]
            if first:
                self.memset("pool", self.SL[:, :], 0.0, [("SL",)])
            mms = [(Z[:, col0:512], self.NTRI, SP[:, col0:512], False, first, "nochk")]
            rd = [("SP", sb)]
            if not first:
                mms.append((Z[:, col0:512], self.NONES, self.SL[:, col0:512], False, True, "nochk"))
                rd.append(("SL",))
            self.pe(mms, rd, [zreg])
            if not last:
                self.tt("dve", self.SL[:, col0:512], self.SL[:, col0:512], SP[:, col0:512], ALU.add, [("SL",), ("SP", sb)], [("SL",)])
            pb = self.nxt("PT", 3)
            PT = self.PT[pb]
            self.act(PT[:, col0:512], Z[:, col0:512], AF.Exp, [zreg], [("PT", pb)])
            if j0 >= 0:
                self.tt("dve", PT[:, col0:col0 + 128], PT[:, col0:col0 + 128], self.TRIS, ALU.mult, [("PT", pb)], [("PT", pb)])
            pend[it] = (pb, col0)

        def stage_e(it):
            qb, hh, kt, first, last = it
            pb, col0 = pend.pop(it)
            PT = self.PT[pb]
            mms = []
            if first:
                self.accb = self.nxt("accb", 2)
                mms.append((self.OACC[self.accb][:, :], self.ZEROS, QZ[hh][:, qb * 512:(qb + 1) * 512], True, False))
            ab = self.accb
            mms.append((self.OACC[ab][:, col0:512], VP(kt), PT[:, col0:512], False, last))
            self.pe(mms, [("PT", pb), ("V", kt), ("QZ", hh, qb)], [("acc", ab)])
            if last:
                rows = slice(64 * hh, 64 * hh + 64)
                self.cp("dve", self.OT[rows, oc, qb * 512:(qb + 1) * 512], self.OACC[ab][rows, :],
                        [("acc", ab)], [("OT", oc, 4 * qb + j) for j in range(4)])

        LC, LE = 2, 3
        self.psa_wide = True
        for i in range(n + LE):
            if i < n:
                stage_a(items[i])
            if LC <= i < n + LC:
                stage_c(items[i - LC])
            if i >= LE:
                stage_e(items[i - LE])
        self.psa_wide = False

    def proj_fm(self, W, wsl, blk, wreg, dsts, scale, eng):
        ps, preg = self.psA()
        tsl = slice(blk * 512, (blk + 1) * 512)
        self.pe([(ps[:, :], W[:, c, wsl], self.HT[:, c, tsl], c == 0, c == 7) for c in range(8)],
                [wreg] + [("HT", 4 * blk + j) for j in range(4)], [preg])
        for dst, rows, dreg in dsts:
            if eng == "act":
                self.S.op("act", lambda e, dst=dst, rows=rows: e.activation(out=dst, in_=ps[rows, :], func=AF.Copy, scale=scale), [preg], [dreg])
            else:
                self.ts(eng, dst, ps[rows, :], scale, None, ALU.mult, None, [preg], [dreg])

    def v_proj(self, W, wsl, n, i, wreg, dst, src, sreg, vreg):
        ps, preg = self.psA()
        self.pe([(ps[:, 0:n], src[:, c, i * 128:(i + 1) * 128], W[:, c, wsl], c == 0, c == 7) for c in range(8)],
                [wreg, sreg], [preg])
        self.cp("dve", dst, ps[:, 0:n], [preg], [vreg])

    def qk_proj(self, W, b, l0):
        h0 = slice(0, 64)
        h1 = slice(64, 128)
        for blk in range(4):
            tsl = slice(blk * 512, (blk + 1) * 512)
            self.proj_fm(W, slice(0, 128), blk, ("WP", b),
                         [(self.QZ[0][h0, tsl], h0, ("QZ", 0, blk)), (self.QZ[1][h1, tsl], h1, ("QZ", 1, blk))], 0.125, "act")
        for blk in range(4):
            tsl = slice(blk * 512, (blk + 1) * 512)
            if l0:
                dsts = [(self.KZ[0][h0, tsl], h0, ("KZ", 0, blk)), (self.KZ[1][h1, tsl], h1, ("KZ", 1, blk))]
            else:
                dsts = [(self.KZ[0][:, tsl], slice(0, 128), ("KZ", 0, blk))]
            self.proj_fm(W, slice(128, 256), blk, ("WP", b), dsts, 1.0, "dve")

    def mem_branch(self, s, l):
        self.load_gb(1 + l)
        self.wload(self.WM, self.d_wmemkv[l], ("WM",))
        for mt in range(2):
            self.dma("sp", self.MTMP[:], self.d_mem[s, mt * 128:(mt + 1) * 128, :], [], [("mtmp",)])
            self.norm_tile(self.MTMP[:], ("mtmp",), self.GB[:], self.HMT[:, :, mt * 128:(mt + 1) * 128], ("HMT", mt), "act")
        self.norm_flush()
        for p in range(2):
            ps, preg = self.psA()
            self.pe([(ps[:, 0:256], self.WM[:, c, 128 * p:128 * p + 128], self.HMT[:, c, :], c == 0, c == 7) for c in range(8)],
                    [("WM",), ("HMT", 0), ("HMT", 1)], [preg])
            self.cp("act", self.MKT[:, p, :], ps[:, 0:256], [preg], [("MKT", p)])
        for mt in range(2):
            self.v_proj(self.WM, slice(256, 512), 256, mt, ("WM",), self.MVP[:, mt, :], self.HMT, ("HMT", mt), ("MV", mt))

    def mem_attn(self, l, qsrc, qcol0, fold):
        h0 = slice(0, 64)
        h1 = slice(64, 128)
        for p in range(2):
            b = self.nxt("WP", 2)
            self.wload(self.WP[b][:, :, 0:128], qsrc[:, qcol0 + 128 * p:qcol0 + 128 * p + 128], ("WP", b), fold)
            for blk in range(4):
                tsl = slice(blk * 512, (blk + 1) * 512)
                self.proj_fm(self.WP[b], slice(0, 128), blk, ("WP", b),
                             [(self.QZ[0][h0, tsl], h0, ("QZ", 0, blk)), (self.QZ[1][h1, tsl], h1, ("QZ", 1, blk))], 0.125, "act")
            MK = self.MKT[:, p, :]
            self.attn_softmax_pair(self.QZ, [MK, MK], lambda kt, p=p: self.MVP[:, kt, 128 * p:128 * p + 128], 2, False, 6 + p,
                                   lambda hh, kt, p=p: ("MKT", p), lambda kt: ("MV", kt))

    def out_proj(self, l, post=None):
        self.wload(self.WO, self.d_wout[l], ("WO",))
        for i in range(NT):
            a = self.nxt("acc2", 2)
            ps = self.ACC2[a]
            regs = [("acc", 2 * a), ("acc", 2 * a + 1)]
            mms = []
            for h in range(2):
                for c in range(8):
                    mms.append((ps[:, 512 * h:512 * h + 512], self.OT[:, c, i * 128:(i + 1) * 128], self.WO[:, c, 512 * h:512 * h + 512], c == 0, c == 7))
            self.pe(mms, [("WO",)] + [("OT", c, i) for c in range(8)], regs)
            self.tt("dve", self.X[:, i, :], ps[:, :], self.X[:, i, :], ALU.add, regs + [("X", i)], [("X", i)])
            if post is not None:
                post(i)
        self.norm_flush()

    def norm_x_tile(self, i, gb):
        self.norm_tile(self.X[:, i, :], ("X", i), gb, self.HT[:, :, i * 128:(i + 1) * 128], ("HT", i), "act" if i % 2 else "dve")

    def ffn(self, l, post=None):
        wup = self.d_wup[l]
        wdn = self.d_wdown[l]
        nch = len(FCH)

        def load_up(fi):
            b = fi % 2
            f0 = 128 * sum(FCH[:fi])
            nf = 128 * FCH[fi]
            self.wload(self.WUP[b][:, :, 0, 0:nf], wup[:, f0:f0 + nf], ("WUP", b, 0))
            self.wload(self.WUP[b][:, :, 1, 0:nf], wup[:, DFF + f0:DFF + f0 + nf], ("WUP", b, 1))

        def load_dn(fi):
            b = fi % 2
            f0 = 128 * sum(FCH[:fi])
            nf = 128 * FCH[fi]
            self.wload(self.WDN[b][:, 0:FCH[fi], :], wdn[f0:f0 + nf, :], ("WDN", b))

        def down_tile(fi, i, final):
            b = fi % 2
            a = self.nxt("acc2", 2)
            ps = self.ACC2[a]
            regs = [("acc", 2 * a), ("acc", 2 * a + 1)]
            mms = []
            for h in range(2):
                for k in range(FCH[fi]):
                    mms.append((ps[:, 512 * h:512 * h + 512], self.AT[b][:, k, i * 128:(i + 1) * 128],
                                self.WDN[b][:, k, 512 * h:512 * h + 512], k == 0, k == FCH[fi] - 1))
            self.pe(mms, [("WDN", b)] + [("AT", b, k, i // 4) for k in range(FCH[fi])], regs)
            self.tt("dve", self.X[:, i, :], ps[:, :], self.X[:, i, :], ALU.add, regs + [("X", i)], [("X", i)])
            if final and post is not None:
                post(i)

        def step(fi, k, tb):
            b = fi % 2
            cbase = sum(FCH[:fi])
            tsl = slice(tb * 512, (tb + 1) * 512)
            gT = None
            for gv in range(2):
                ci = cbase + k + (DFF // 128) * gv
                ps, preg = self.psA()
                self.pe([(ps[:, :], self.WUP[b][:, c, gv, 128 * k:128 * k + 128], self.HT[:, c, tsl], c == 0, c == 7) for c in range(8)],
                        [("WUP", b, gv)] + [("HT", 4 * tb + j) for j in range(4)], [preg])
                UB = self.UBS[gv][tb % 2]
                ureg = ("UB", gv, tb % 2)
                if tb == 0:
                    self.memset("pool", UB[:, 0:2], 0.0, [ureg])
                else:
                    self.cp("pool", UB[:, 0:2], self.UBS[gv][(tb + 1) % 2][:, 512:514], [("UB", gv, (tb + 1) % 2)], [ureg])
                self.cp("act", UB[:, 2:514], ps[:, :], [preg], [ureg])
                tbuf = self.nxt("T%d" % gv, 2)
                T = (self.TG if gv == 0 else self.TV)[tbuf]
                treg = ("T", gv, tbuf)
                self.act(T[:, :], ps[:, :], AF.Identity, [preg, ("cw",), ("cb",)], [treg],
                         scale=self.CW[:, ci, 2:3], bias=self.CB[:, ci:ci + 1])
                self.stt(T[:, :], UB[:, 1:513], self.CW[:, ci, 1:2], T[:, :], ALU.mult, ALU.add, [ureg, treg, ("cw",)], [treg])
                self.stt(T[:, :], UB[:, 0:512], self.CW[:, ci, 0:1], T[:, :], ALU.mult, ALU.add, [ureg, treg, ("cw",)], [treg])
                if gv == 0:
                    gT = (T, treg)
                else:
                    self.act(gT[0][:, :], gT[0][:, :], AF.Silu, [gT[1]], [gT[1]])
                    self.tt("dve", self.AT[b][:, k, tsl], gT[0][:, :], T[:, :], ALU.mult, [gT[1], treg], [("AT", b, k, tb)])

        load_up(0)
        load_dn(0)
        pending = []
        for fi in range(nch):
            if fi + 1 < nch:
                load_up(fi + 1)
            steps = [(k, tb) for k in range(FCH[fi]) for tb in range(4)]
            for si, (k, tb) in enumerate(steps):
                step(fi, k, tb)
                left = len(steps) - si
                ne = (len(pending) + left - 1) // left
                for _ in range(ne):
                    pending.pop(0)()
            assert not pending
            if fi + 1 < nch:
                load_dn(fi + 1)
            pending = [(lambda fi=fi, i=i: down_tile(fi, i, fi == nch - 1)) for i in range(NT)]
        for cl in pending:
            cl()
        self.norm_flush()

    def layer0_mixer(self, s):
        self.mem_branch(s, 0)
        self.S.fence()
        self.load_gb(0)
        for hh in range(2):
            self.memset("pool", self.QZ[hh][:, :], 0.0, [("QZ", hh, k) for k in range(4)])
            self.memset("pool", self.KZ[hh][:, :], 0.0, [("KZ", hh, k) for k in range(4)])
        brow = [64, 0]
        for hh in range(2):
            r = brow[hh]
            self.memset("pool", self.QZ[hh][r:r + 4, :], -1.0, [("QZ", hh, k) for k in range(4)])
            self.memset("pool", self.KZ[hh][r:r + 4, :], 1.0, [("KZ", hh, k) for k in range(4)])
        for i in range(NT):
            self.norm_x_tile(i, self.GB[:])
        self.norm_flush()
        b = self.nxt("WP", 2)
        WF = self.WP[b][:, :, 0:12]
        self.wload(WF, self.d_wina[:, 2304:2316], ("WP", b))
        for blk in range(4):
            ps, preg = self.psA()
            tsl = slice(blk * 512, (blk + 1) * 512)
            self.pe([(ps[0:12, :], WF[:, c, :], self.HT[:, c, tsl], c == 0, c == 7) for c in range(8)],
                    [("WP", b)] + [("HT", 4 * blk + j) for j in range(4)], [preg])
            self.act(self.CF[0:12, tsl], ps[0:12, :], AF.Exp, [preg], [("CF", blk)], scale=-1.0, bias=self.NB[:, 0:1])
            self.act(self.CF[0:12, tsl], self.CF[0:12, tsl], AF.Ln, [("CF", blk)], [("CF", blk)], bias=1.0)
            self.ts("dve", self.CF[0:12, tsl], self.CF[0:12, tsl], -0.5, None, ALU.mult, None, [("CF", blk)], [("CF", blk)])
        self.S.op("dve", lambda e: e.tensor_tensor_scan(out=self.CFO[0:12, :], data0=self.CF[0:12, :], data1=self.CF[0:12, :],
                                                         initial=0.0, op0=ALU.add, op1=ALU.add),
                  [("CF", k) for k in range(4)], [("CFO",)])
        self.cp("dve", self.CH[0:12, 0, :], self.CFO[0:12, :], [("CFO",)], [("CH", 0)])
        self.tt("dve", self.CH[0:12, 1, :], self.CFO[0:12, :], self.CH[0:12, 0, :], ALU.subtract, [("CFO",), ("CH", 0)], [("CH", 1)])
        if "cf" in self.d_dbg:
            self.dma("sp", self.d_dbg["cf"], self.CFO[0:12, :], [("CFO",)], [])
        allq = lambda hh: [("QZ", hh, k) for k in range(4)]
        allk = lambda hh: [("KZ", hh, k) for k in range(4)]
        for p in range(6):
            b = self.nxt("WP", 2)
            for k in range(3):
                self.wload(self.WP[b][:, :, 128 * k:128 * k + 128], self.d_wina[:, 768 * k + 128 * p:768 * k + 128 * p + 128], ("WP", b))
            for hh in range(2):
                h = 2 * p + hh
                r = brow[hh]
                self.dma("sp", self.QZ[hh][r:r + 1, :], self.CH[h:h + 1, 0, :], [("CH", 0)], allq(hh))
                self.dma("sp", self.QZ[hh][r + 1:r + 2, :], self.CH[h:h + 1, 1, :], [("CH", 1)], allq(hh))
                self.dma("sp", self.KZ[hh][r + 2:r + 3, :], self.CH[h:h + 1, 0, :], [("CH", 0)], allk(hh))
                self.dma("sp", self.KZ[hh][r + 3:r + 4, :], self.CH[h:h + 1, 1, :], [("CH", 1)], allk(hh))
            self.qk_proj(self.WP[b], b, True)
            for i in range(NT):
                self.v_proj(self.WP[b], slice(256, 384), 128, i, ("WP", b), self.VP[:, i, :], self.HT, ("HT", i), ("V", i))
            self.attn_softmax_pair(self.QZ, self.KZ, lambda kt: self.VP[:, kt, :], None, True, p,
                                   lambda hh, kt: ("KZ", hh, kt // 4), lambda kt: ("V", kt))
        for hh in range(2):
            r = brow[hh]
            self.memset("pool", self.QZ[hh][r:r + 4, :], 0.0, allq(hh))
        self.mem_attn(0, self.d_wina, 2316, None)

    def layer1_mixer(self, s):
        self.mem_branch(s, 1)
        self.S.fence()
        for hh in range(2):
            self.memset("pool", self.QZ[hh][:, :], 0.0, [("QZ", hh, k) for k in range(4)])
        if not self.prenormed:
            for i in range(NT):
                self.norm_x_tile(i, None)
            self.norm_flush()
        for p in range(self.opt.get("l1_pairs", 6)):
            b = self.nxt("WP", 2)
            self.wload(self.WP[b][:, :, 0:128], self.d_winb[:, 128 * p:128 * p + 128], ("WP", b), 1)
            self.wload(self.WP[b][:, :, 128:256], self.d_wkv[:, 128 * p:128 * p + 128], ("WP", b), 0)
            self.wload(self.WP[b][:, :, 256:384], self.d_wkv[:, 768 + 128 * p:768 + 128 * p + 128], ("WP", b), 0)
            self.qk_proj(self.WP[b], b, False)
            for i in range(NT):
                self.v_proj(self.WP[b], slice(256, 384), 128, i, ("WP", b), self.VP[:, i, :], self.HT, ("HT", i), ("V", i))
            if self.opt.get("l1_attn", True):
                self.attn_sb_pair(self.QZ, self.KZ[0], lambda kt: self.VP[:, kt, :], p)
        if self.opt.get("l1_mem", True):
            self.mem_attn(1, self.d_winb, 768, 1)

    def dump_x(self, name):
        if name in self.d_dbg:
            for i in range(NT):
                self.dma("sp", self.d_dbg[name][i * 128:(i + 1) * 128, :], self.X[:, i, :], [("X", i)], [])

    def final_tile(self, s, i, normed):
        ob = self.nxt("outt", 2)
        O = self.OUTT[ob]
        if normed:
            b = self.nxt("hn", 2)
            ss = self.STAT[:, 2 * b:2 * b + 1]
            rs = self.STAT[:, 2 * b + 1:2 * b + 2]
            xin = self.X[:, i, :]
            self.act(self.HN[b][:], xin, AF.Square, [("X", i)], [("hn", b), ("ss", b)], scale=1.0 / 32.0, accum_out=ss)
            self.act(ss, ss, AF.Sqrt, [("ss", b)], [("ss", b)], bias=self.EPSC[:, 0:1])
            self.S.op("dve", lambda e, rs=rs, ss=ss: e.reciprocal(out=rs, in_=ss), [("ss", b)], [("rs", b)])
            self.stt(O[:], xin, rs, self.GB[:], ALU.mult, ALU.mult, [("X", i), ("rs", b), ("gb",)], [("outt", ob)])
            self.dma("sp", self.d_out[s, i * 128:(i + 1) * 128, :], O[:], [("outt", ob)], [])
        else:
            self.dma("sp", self.d_out[s, i * 128:(i + 1) * 128, :], self.X[:, i, :], [("X", i)], [])

    def program(self):
        order = ["l0mix", "l0out", "l0ffn", "l1mix", "l1out", "l1ffn"]
        stop = self.stop
        nph = len(order) if stop is None else order.index(stop) + 1
        self.dma("pool", self.CONST[:], self.d_consts, [], [("const",)])
        self.dma("sp", self.GC[:], self.d_gc, [], [("gc",)])
        self.dma("sp", self.NB[:], self.d_bf, [], [("nb",)])
        self.memset("dve", self.EPSC[:], EPS, [("eps",)])
        self.ts("dve", self.NB[:], self.NB[:], -1.0, None, ALU.mult, None, [("nb",)], [("nb",)])
        self.S.fence()
        for s in range(self.nseq):
            for i in range(NT):
                self.dma("sp", self.X[:, i, :], self.d_x[s, i * 128:(i + 1) * 128, :], [], [("X", i)])
            phs = order[:nph] if self.phases is None else self.phases
            full = self.phases is None and stop is None
            self.prenormed = False
            for ph in phs:
                if ph == "l0mix":
                    self.layer0_mixer(s)
                elif ph == "l1mix":
                    self.layer1_mixer(s)
                elif ph in ("l0out", "l1out"):
                    l = int(ph[1])
                    self.load_gb(3 + l)
                    self.dma("sp", self.CW[:], self.d_cw[l], [], [("cw",)])
                    self.dma("sp", self.CB[:], self.d_cb[l], [], [("cb",)])
                    self.out_proj(l, lambda i: self.norm_x_tile(i, self.GB[:]))
                    if s == 0:
                        self.dump_x("x_l%dmix" % l)
                elif ph == "l0ffn":
                    self.ffn(0, (lambda i: self.norm_x_tile(i, None)) if full else None)
                    self.prenormed = full
                    if s == 0:
                        self.dump_x("x_l0")
                elif ph == "l1ffn":
                    if full:
                        self.load_gb(5)
                    self.ffn(1, (lambda i, s=s: self.final_tile(s, i, True)) if full else None)
                self.S.fence()
            if not full:
                for i in range(NT):
                    self.final_tile(s, i, False)
            self.S.fence()

    def build(self):
        nc = self.nc
        nseq = self.nseq
        dt_in = lambda name, shape: nc.dram_tensor(name, shape, F32, kind="ExternalInput").ap()
        self.d_x = dt_in("x", [nseq, SEQ, D])
        self.d_mem = dt_in("mem", [nseq, NMEM, D])
        self.d_wina = dt_in("w_in_a", [D, A_IN])
        self.d_winb = dt_in("w_in_b", [D, D])
        self.d_wkv = dt_in("w_kv", [D, 1536])
        self.d_wmemkv = dt_in("w_memkv", [2, D, 512])
        self.d_wout = dt_in("w_out", [2, D, D])
        self.d_wup = dt_in("w_up", [2, D, 2 * DFF])
        self.d_wdown = dt_in("w_down", [2, DFF, D])
        self.d_gb = dt_in("gb", [6, 128, D])
        self.d_gc = dt_in("gc", [128, 2, 8])
        self.d_bf = dt_in("bf", [12, 1])
        self.d_cw = dt_in("cw", [2, 128, 44, 3])
        self.d_cb = dt_in("cb", [2, 128, 44])
        self.d_consts = dt_in("consts", [128, 7, 128])
        self.d_out = nc.dram_tensor("out", [nseq, SEQ, D], F32, kind="ExternalOutput").ap()
        self.d_dbg = {}
        for name, shape in self.dbg_shapes().items():
            if name in self.dbg:
                self.d_dbg[name] = nc.dram_tensor("dbg_" + name, shape, F32, kind="ExternalOutput").ap()

        with ExitStack() as es:
            sb = lambda name, shape, dt: es.enter_context(nc.sbuf_tensor(name, shape, dt))
            self.X = sb("X", [128, NT, D], F32)
            self.HT = sb("HT", [128, 8, SEQ], BF16)
            self.CONST = sb("CONST", [128, 7, 128], BF16)
            self.IDENT = self.CONST[:, 0, :]
            self.TRI = self.CONST[:, 1, :]
            self.TRIS = self.CONST[:, 2, :]
            self.NTRI = self.CONST[:, 3, :]
            self.NONES = self.CONST[:, 4, :]
            self.ZEROS = self.CONST[:, 5, :]
            self.ONES = self.CONST[:, 6, :]
            self.GB = sb("GB", [128, D], F32)
            self.GC = sb("GC", [128, 2, 8], F32)
            self.NB = sb("NB", [12, 1], F32)
            self.CW = sb("CW", [128, 44, 3], F32)
            self.CB = sb("CB", [128, 44], F32)
            self.STAT = sb("STAT", [128, 16], F32)
            self.EPSC = sb("EPSC", [128, 1], F32)
            self.HN = [sb("HN%d" % i, [128, D], BF16) for i in range(2)]
            self.ARENA_B = 103000
            self.ARENA = sb("ARENA", [128, self.ARENA_B // 2], BF16)
            self.PSA = [es.enter_context(nc.psum_tensor("psA%d" % i, [128, 512], F32)) for i in range(3)]
            self.ACC2 = [es.enter_context(nc.psum_tensor("acc2_%d" % i, [128, 1024], F32)) for i in range(2)]
            self.OACC = [self.ACC2[0][:, 0:512], self.ACC2[0][:, 512:1024]]
            self.LACC = [self.ACC2[1][:, 0:512], self.ACC2[1][:, 512:1024]]
            self.PST = es.enter_context(nc.psum_tensor("psT", [128, 8, 128], BF16))
            esem = {e: es.enter_context(nc.semaphore("sem_" + e)) for e in CENGS}
            dsem = [es.enter_context(nc.semaphore("dsem%d" % i)) for i in range(24)]
            self.S = Sched(esem, dsem)
            self.carve_all()
            self.program()
            self.emit()
        return nc

    def dbg_shapes(self):
        return {"x_l0mix": [SEQ, D], "x_l0": [SEQ, D], "x_l1mix": [SEQ, D], "cf": [12, SEQ]}

    def carve(self, off, shape, dt):
        n = int(np.prod(shape[1:]))
        esz = 2 if dt == BF16 else 4
        assert off % 4 == 0, off
        a = self.ARENA[:, off // 2:off // 2 + n * esz // 2]
        if dt == F32:
            a = a.bitcast(F32)
        if len(shape) == 3:
            a = a.rearrange("p (a b) -> p a b", a=shape[1])
        elif len(shape) == 4:
            a = a.rearrange("p (a b c) -> p a b c", a=shape[1], b=shape[2])
        self._off = off + n * esz
        assert self._off <= self.ARENA_B, self._off
        return a

    def carve_all(self):
        al = lambda o: (o + 3) // 4 * 4
        self.OT = self.carve(0, [128, 8, SEQ], BF16)
        o = self._off
        self.MTMP = self.carve(0, [128, D], F32)
        self.WM = self.carve(self._off, [128, 8, 512], BF16)
        self.HMT = self.carve(self._off, [128, 8, NMEM], BF16)
        self.QZ = []
        self.KZ = []
        for i in range(2):
            self.QZ.append(self.carve(o, [128, SEQ], BF16)); o = self._off
        for i in range(2):
            self.KZ.append(self.carve(o, [128, SEQ], BF16)); o = self._off
        self.VP = self.carve(o, [128, NT, 128], BF16); o = self._off
        self.WP = []
        for i in range(2):
            self.WP.append(self.carve(o, [128, 8, 384], BF16)); o = self._off
        self.PT = []
        for i in range(3):
            self.PT.append(self.carve(o, [128, 512], BF16)); o = self._off
        self.MKT = self.carve(o, [128, 2, NMEM], BF16); o = self._off
        self.MVP = self.carve(o, [128, 2, 256], BF16); o = self._off
        self.RB = []
        for i in range(2):
            self.RB.append(self.carve(o, [128, 512], F32)); o = self._off
        o0 = o
        self.CH = self.carve(o, [128, 2, SEQ], BF16); o = self._off
        self.CF = self.carve(o, [128, SEQ], F32); o = self._off
        o = o0
        self.EX = []
        for i in range(2):
            self.EX.append(self.carve(o, [128, 512], F32)); o = self._off
        self.SPT = []
        for i in range(4):
            self.SPT.append(self.carve(o, [128, 512], BF16)); o = self._off
        self.SL = self.carve(o, [128, 512], BF16); o = self._off
        self.CFO = self.carve(self.ARENA_B - 8192 - 8, [128, SEQ], F32)
        self.WO = self.carve(32768, [128, 8, D], BF16)
        o = 0
        self.WUP = []
        for i in range(2):
            self.WUP.append(self.carve(o, [128, 8, 2, 512], BF16)); o = self._off
        self.WDN = []
        for i in range(2):
            self.WDN.append(self.carve(o, [128, 4, D], BF16)); o = self._off
        self.AT = []
        for i in range(2):
            self.AT.append(self.carve(o, [128, 4, SEQ], BF16)); o = self._off
        self.UBS = [[None, None], [None, None]]
        for g in range(2):
            for i in range(2):
                self.UBS[g][i] = self.carve(o, [128, 516], F32); o = self._off
        self.TG = []
        self.TV = []
        for i in range(2):
            self.TG.append(self.carve(o, [128, 512], F32)); o = self._off
            self.TV.append(self.carve(o, [128, 512], F32)); o = self._off
        self.OUTT = [self.carve(49152, [128, D], F32), self.carve(49152 + 4096, [128, D], F32)]

    def emit(self):
        nc = self.nc
        S = self.S
        with nc.Block() as block:
            @block.tensor
            def _(e):
                for f in S.prog["pe"]:
                    f(e)

            @block.scalar
            def _(e):
                for f in S.prog["act"]:
                    f(e)

            @block.vector
            def _(e):
                for f in S.prog["dve"]:
                    f(e)

            @block.gpsimd
            def _(e):
                for f in S.prog["pool"]:
                    f(e)

            @block.sync
            def _(e):
                for f in S.prog["sp"]:
                    f(e)


def host_inputs(x, mem, ln_mix_g, w_in_a, b_f_a, w_in_b, ln_kv_g, w_kv, ln_mem_g, w_memkv, w_out, ln_ffn_g,
                w_up, conv_w, conv_b, w_down, final_g):
    f = lambda a: np.ascontiguousarray(np.asarray(a, dtype=np.float32))
    rep = lambda g: np.broadcast_to(np.asarray(g, np.float32)[None, :], (128, D))
    gb = f(np.stack([rep(ln_mix_g[0]), rep(ln_mem_g[0]), rep(ln_mem_g[1]), rep(ln_ffn_g[0]), rep(ln_ffn_g[1]), rep(final_g)]))
    col = lambda g: np.asarray(g, np.float32).reshape(8, 128).T
    gc = f(np.stack([col(ln_kv_g), col(ln_mix_g[1])], axis=1))
    cw = f(np.asarray(conv_w, np.float32).reshape(2, 3, 44, 128).transpose(0, 3, 2, 1))
    cb = f(np.asarray(conv_b, np.float32).reshape(2, 44, 128).transpose(0, 2, 1))
    i = np.arange(128)
    ident = (i[:, None] == i[None, :]).astype(np.float32)
    tri = (i[:, None] <= i[None, :]).astype(np.float32)
    tris = (i[:, None] < i[None, :]).astype(np.float32)
    ntri = -(i[:, None] >= i[None, :]).astype(np.float32)
    nones = -np.ones((128, 128), np.float32)
    consts = f(np.stack([ident, tri, tris, ntri, nones, 0.0 * nones, -nones], axis=1))
    return {
        "w_in_a": f(w_in_a[0]), "w_in_b": f(w_in_b[0]), "w_kv": f(w_kv), "w_memkv": f(w_memkv), "w_out": f(w_out),
        "w_up": f(w_up), "w_down": f(w_down), "gb": gb, "gc": gc, "bf": f(np.asarray(b_f_a, np.float32).reshape(12, 1)),
        "cw": cw, "cb": cb, "consts": consts,
    }


def kernel(**inputs):
    x = np.asarray(inputs["x"], np.float32)
    mem = np.asarray(inputs["mem"], np.float32)
    shared = host_inputs(**inputs)
    nseq = x.shape[0] // NCORES
    nc = Builder(nseq=nseq).build()
    in_maps = []
    for c in range(NCORES):
        m = dict(shared)
        m["x"] = np.ascontiguousarray(x[c * nseq:(c + 1) * nseq])
        m["mem"] = np.ascontiguousarray(mem[c * nseq:(c + 1) * nseq])
        in_maps.append(m)
    res = run_bass_kernel_spmd(nc, in_maps, core_ids=list(range(NCORES)))
    return np.concatenate([np.asarray(r["out"], np.float32) for r in res.results], axis=0)
```

```python
import numpy as np
from contextlib import ExitStack
import concourse.bass as bass
import concourse.mybir as mybir
from concourse.bass_utils import run_bass_kernel_spmd

F32 = mybir.dt.float32
BF16 = mybir.dt.bfloat16
AF = mybir.ActivationFunctionType
ALU = mybir.AluOpType

D = 1024
SEQ = 2048
NT = SEQ // 128
NMEM = 256
DFF = 2816
A_IN = 2572
EPS = 1e-6
NCORES = 8
ENGS = ["pe", "act", "dve", "pool", "sp"]
CENGS = ["pe", "act", "dve", "pool"]
FCH = [4, 4, 4, 4, 4, 2]


class Sched:
    def __init__(self, esem, dsem):
        self.esem = esem
        self.dsem = dsem
        self.prog = {e: [] for e in ENGS}
        self.cnt = {e: 0 for e in CENGS}
        self.dcnt = [0] * len(dsem)
        self.dpool = {"sp": list(range(0, 16)), "pool": list(range(16, 24)), "act": list(range(16, 24))}
        self.dnext = {"sp": 0, "pool": 0, "act": 0}
        self.known = {e: {} for e in ENGS}
        self.state = {}

    def _sem(self, src):
        return self.esem[src] if isinstance(src, str) else self.dsem[src]

    def _need(self, eng, reads, writes):
        need = {}

        def add(w, kind):
            src, val = w
            if src == eng and (eng == "pe" or kind != "RAW"):
                return
            if need.get(src, 0) < val:
                need[src] = val

        for r in reads:
            st = self.state.get(r)
            if st is not None and st[0] is not None:
                add(st[0], "RAW")
        for w in writes:
            st = self.state.get(w)
            if st is not None:
                if st[0] is not None:
                    add(st[0], "WAW")
                for s_, v_ in st[1].items():
                    add((s_, v_), "WAR")
        out = []
        kn = self.known[eng]
        for src, val in need.items():
            if kn.get(src, 0) < val:
                kn[src] = val
                out.append((self._sem(src), val))
        return out

    def _update(self, me, reads, writes):
        src, val = me
        for r in reads:
            st = self.state.get(r)
            if st is None:
                st = [None, {}]
                self.state[r] = st
            st[1][src] = val
        for w in writes:
            self.state[w] = [me, {}]

    def op(self, eng, fn, reads=(), writes=()):
        wl = self._need(eng, reads, writes)
        self.cnt[eng] += 1
        sem = self.esem[eng]

        def emit(e, wl=wl, fn=fn, sem=sem):
            for sm, v in wl:
                e.wait_ge(sm, v)
            fn(e).then_inc(sem, 1)

        self.prog[eng].append(emit)
        self._update((eng, self.cnt[eng]), reads, writes)

    def dma(self, q, out, in_, reads=(), writes=()):
        pl = self.dpool[q]
        j = pl[self.dnext[q] % len(pl)]
        self.dnext[q] += 1
        wl = self._need(q, reads, writes)
        prev = 16 * self.dcnt[j]
        if prev > 0 and self.known[q].get(j, 0) < prev:
            self.known[q][j] = prev
            wl.append((self.dsem[j], prev))
        self.dcnt[j] += 1
        sem = self.dsem[j]

        def emit(e, wl=wl, out=out, in_=in_, sem=sem):
            for sm, v in wl:
                e.wait_ge(sm, v)
            e.dma_start(out=out, in_=in_).then_inc(sem, 16)

        self.prog[q].append(emit)
        self._update((j, 16 * self.dcnt[j]), reads, writes)

    def fence(self):
        srcs = [(e, self.cnt[e]) for e in CENGS] + [(j, 16 * c) for j, c in enumerate(self.dcnt)]
        for e in ENGS:
            wl = []
            for src, val in srcs:
                if src == e and e == "pe":
                    continue
                if val > self.known[e].get(src, 0):
                    self.known[e][src] = val
                    wl.append((self._sem(src), val))

            def emit(en, wl=wl):
                for sm, v in wl:
                    en.wait_ge(sm, v)

            self.prog[e].append(emit)
        self.state = {}


class Builder:
    def __init__(self, nseq=2, stop=None, dbg=(), phases=None, opt=None):
        self.phases = phases
        self.opt = opt or {}
        self.nseq = nseq
        self.stop = stop
        self.dbg = set(dbg)
        self.nc = bass.Bass("TRN2", target_bir_lowering=False)
        self.rr = {}
        self.norm_pending = []
        self.psa_wide = False
        self.ffn_preloaded = None

    def nxt(self, name, n):
        v = self.rr.get(name, 0)
        self.rr[name] = (v + 1) % n
        return v

    def pe(self, mms, reads, writes):
        def fn(e, mms=mms):
            ins = None
            for m in mms:
                if m[0] == "tr":
                    ins = e.transpose(out=m[1], in_=m[2], identity=self.IDENT)
                elif len(m) == 6:
                    out, lhsT, rhs, start, stop, _ = m
                    ins = e.matmul(out, lhsT, rhs, start=start, stop=stop, skip_group_check=True)
                else:
                    out, lhsT, rhs, start, stop = m
                    ins = e.matmul(out, lhsT, rhs, start=start, stop=stop)
            return ins

        self.S.op("pe", fn, reads, writes)

    def act(self, out, in_, func, reads, writes, **kw):
        self.S.op("act", lambda e: e.activation(out=out, in_=in_, func=func, **kw), reads, writes)

    def ts(self, eng, out, in0, s1, s2, op0, op1, reads, writes):
        if op1 is None:
            self.S.op(eng, lambda e: e.tensor_scalar(out=out, in0=in0, scalar1=s1, scalar2=s2, op0=op0), reads, writes)
        else:
            self.S.op(eng, lambda e: e.tensor_scalar(out=out, in0=in0, scalar1=s1, scalar2=s2, op0=op0, op1=op1), reads, writes)

    def tt(self, eng, out, in0, in1, op, reads, writes):
        self.S.op(eng, lambda e: e.tensor_tensor(out=out, in0=in0, in1=in1, op=op), reads, writes)

    def stt(self, out, in0, scalar, in1, op0, op1, reads, writes):
        self.S.op("dve", lambda e: e.scalar_tensor_tensor(out=out, in0=in0, scalar=scalar, in1=in1, op0=op0, op1=op1), reads, writes)

    def cp(self, eng, out, in_, reads, writes):
        if eng == "act":
            self.S.op("act", lambda e: e.activation(out=out, in_=in_, func=AF.Copy), reads, writes)
        else:
            self.S.op(eng, lambda e: e.tensor_copy(out=out, in_=in_), reads, writes)

    def memset(self, eng, ap, val, writes):
        self.S.op(eng, lambda e: e.memset(ap, val), (), writes)

    def dma(self, q, out, in_, reads, writes):
        self.S.dma(q, out, in_, reads, writes)

    def psA(self, z=False):
        if self.psa_wide:
            if z:
                k = self.nxt("psAz", 5)
                if k >= 3:
                    return self.LACC[k - 3], ("acc", 2 + k - 3)
                return self.PSA[k], ("psA", k)
            return self.PSTF, ("psT",)
        k = self.nxt("psA", 3)
        return self.PSA[k], ("psA", k)

    def norm_tile(self, xin, xreg, gb, dst, dstreg, evac_eng):
        b = self.nxt("hn", 2)
        ss = self.STAT[:, 2 * b:2 * b + 1]
        rs = self.STAT[:, 2 * b + 1:2 * b + 2]
        hn = self.HN[b]
        hreg = [("hn", b), ("PT", 3 + 2 * b), ("PT", 4 + 2 * b)]
        self.act(hn[:], xin, AF.Square, [xreg], hreg + [("ss", b)], scale=1.0 / 32.0, accum_out=ss)
        self.act(ss, ss, AF.Sqrt, [("ss", b)], [("ss", b)], bias=self.EPSC[:, 0:1])
        self.S.op("dve", lambda e: e.reciprocal(out=rs, in_=ss), [("ss", b)], [("rs", b)])
        if gb is None:
            self.ts("dve", hn[:], xin, rs, None, ALU.mult, None, [xreg, ("rs", b)], hreg)
        else:
            self.stt(hn[:], xin, rs, gb, ALU.mult, ALU.mult, [xreg, ("rs", b), ("gb",)], hreg)
        def part2():
            self.pe([("tr", self.PST[:, c, :], hn[:, c * 128:(c + 1) * 128]) for c in range(8)], hreg, [("psT",)])
            self.cp(evac_eng, dst, self.PST[:, :, :], [("psT",)], [dstreg])

        self.norm_flush()
        self.norm_pending.append(part2)

    def norm_flush(self):
        while self.norm_pending:
            self.norm_pending.pop(0)()

    def load_gb(self, idx):
        self.dma("sp", self.GB[:], self.d_gb[idx], [], [("gb",)])

    def wload(self, dst, src2d, reg, fold=None):
        self.dma("pool", dst, src2d.rearrange("(c p) n -> p c n", p=128), [], [reg])
        if fold is not None:
            for c in range(8):
                self.ts("pool", dst[:, c, :], dst[:, c, :], self.GC[:, fold, c:c + 1], 1.0, ALU.mult, ALU.mult, [reg], [reg])

    def bg_pop(self, bg, i, n):
        if not bg or not bg["list"]:
            return
        tot = len(bg["list"])
        den = max(1, int(0.8 * n))
        target = min(tot, -(-tot * (i + 1) // den))
        while bg["done"] < target:
            bg["list"][bg["done"]]()
            bg["done"] += 1

    def bg_flush(self, bg):
        if bg:
            while bg["done"] < len(bg["list"]):
                bg["list"][bg["done"]]()
                bg["done"] += 1

    def attn_softmax_pair(self, st, KZo, VP, nkt, causal, oc, kreg, vreg, bg=None):
        QZ = self.SETS[st]["QZ"]
        KZ = KZo if KZo is not None else self.SETS[st]["KZ"]
        items = []
        for qb in range(4):
            for hh in range(2):
                kts = list(range(4 * qb + 4)) if causal else list(range(nkt))
                for kt in kts:
                    items.append((qb, hh, kt, kt == kts[0], kt == kts[-1]))
        n = len(items)
        pend = {}

        def stage_a(it):
            qb, hh, kt, first, last = it
            j0 = kt - 4 * qb if causal else -1
            col0 = max(0, j0) * 128
            S_, sreg = self.psA()
            qc = slice(qb * 512 + col0, (qb + 1) * 512)
            self.pe([(S_[:, col0:512], KZ[hh][:, kt * 128:(kt + 1) * 128], QZ[hh][:, qc], True, True)],
                    [kreg(hh, kt), ("QZ", st, hh, qb)], [sreg])
            pb = self.nxt("PT", 7)
            PT = self.PT[pb]
            self.act(PT[:, col0:512], S_[:, col0:512], AF.Exp, [sreg], [("PT", pb)])
            if j0 >= 0:
                self.tt("dve", PT[:, col0:col0 + 128], PT[:, col0:col0 + 128], self.TRI, ALU.mult, [("PT", pb)], [("PT", pb)])
            pend[it] = (pb, col0)

        def stage_c(it):
            qb, hh, kt, first, last = it
            pb, col0 = pend.pop(it)
            PT = self.PT[pb]
            if first:
                self.accb = self.nxt("accb", 2)
            ab = self.accb
            self.pe([(self.OACC[ab][:, col0:512], VP(kt), PT[:, col0:512], first, last),
                     (self.LACC[ab][:, col0:512], self.ONES, PT[:, col0:512], first, last)],
                    [("PT", pb), vreg(kt)], [("acc", ab), ("acc", 2 + ab)])
            if last:
                rows = slice(64 * hh, 64 * hh + 64)
                rb = self.RB
                self.act(rb[rows, :], self.LACC[ab][rows, :], AF.Ln, [("acc", 2 + ab)], [("RB",)])
                self.act(rb[rows, :], rb[rows, :], AF.Exp, [("RB",)], [("RB",)], scale=-1.0)
                self.tt("dve", self.OT[rows, oc, qb * 512:(qb + 1) * 512], self.OACC[ab][rows, :], rb[rows, :], ALU.mult,
                        [("acc", ab), ("RB",)], [("OT", oc, 4 * qb + j) for j in range(4)])

        LG = self.opt.get("l0_lag", 3)
        for i in range(n + LG):
            if i < n:
                stage_a(items[i])
            if i >= LG:
                stage_c(items[i - LG])
            self.bg_pop(bg, i, n)
        self.bg_flush(bg)

    def attn_sb_pair(self, st, oc, bg=None):
        QZ = self.SETS[st]["QZ"]
        KP = self.SETS[st]["KZ"][0]
        VPt = self.SETS[st]["VP"]
        VP = lambda kt: VPt[:, kt, :]
        items = []
        for qb in range(4):
            for hh in range(2):
                kts = list(reversed(range(4 * qb + 4)))
                for kt in kts:
                    items.append((qb, hh, kt, kt == kts[0], kt == kts[-1]))
        n = len(items)
        pend = {}

        def stage_a(it):
            qb, hh, kt, first, last = it
            j0 = kt - 4 * qb
            col0 = max(0, j0) * 128
            Z, zreg = self.psA(z=True)
            qc = slice(qb * 512 + col0, (qb + 1) * 512)
            self.pe([(Z[:, col0:512], KP[:, kt * 128:(kt + 1) * 128], QZ[hh][:, qc], True, True)],
                    [("KZ", st, 0, kt // 4), ("QZ", st, hh, qb)], [zreg])
            eb = self.nxt("E", 2)
            E = self.EX[eb]
            self.act(E[:, col0:512], Z[:, col0:512], AF.Exp, [zreg], [("E", eb)])
            sb = self.nxt("SP", 4)
            SP = self.SPT[sb]
            self.act(SP[:, col0:512], E[:, col0:512], AF.Ln, [("E", eb)], [("SP", sb)], bias=1.0)
            if j0 >= 0:
                self.tt("pool", SP[:, col0:col0 + 128], SP[:, col0:col0 + 128], self.TRIS, ALU.mult, [("SP", sb)], [("SP", sb)])
            pend[it] = (Z, zreg, sb, col0, j0)

        def stage_c(it):
            qb, hh, kt, first, last = it
            Z, zreg, sb, col0, j0 = pend[it]
            SP = self.SPT[sb]
            if first:
                self.memset("pool", self.SL[:, :], 0.0, [("SL",)])
            mms = [(Z[:, col0:512], self.NTRI, SP[:, col0:512], False, first, "nochk")]
            rd = [("SP", sb)]
            if not first:
                mms.append((Z[:, col0:512], self.NONES, self.SL[:, col0:512], False, True, "nochk"))
                rd.append(("SL",))
            self.pe(mms, rd, [zreg])
            if not last:
                self.tt("dve", self.SL[:, col0:512], self.SL[:, col0:512], SP[:, col0:512], ALU.add, [("SL",), ("SP", sb)], [("SL",)])
            pb = self.nxt("PT", 7)
            PT = self.PT[pb]
            self.act(PT[:, col0:512], Z[:, col0:512], AF.Exp, [zreg], [("PT", pb)])
            if j0 >= 0:
                self.tt("dve", PT[:, col0:col0 + 128], PT[:, col0:col0 + 128], self.TRIS, ALU.mult, [("PT", pb)], [("PT", pb)])
            pend[it] = (pb, col0)

        def stage_e(it):
            qb, hh, kt, first, last = it
            pb, col0 = pend.pop(it)
            PT = self.PT[pb]
            mms = []
            if first:
                self.accb = self.nxt("accb", 2)
                mms.append((self.OACC[self.accb][:, :], self.ZEROS, QZ[hh][:, qb * 512:(qb + 1) * 512], True, False))
            ab = self.accb
            mms.append((self.OACC[ab][:, col0:512], VP(kt), PT[:, col0:512], False, last))
            self.pe(mms, [("PT", pb), ("V", st, kt), ("QZ", st, hh, qb)], [("acc", ab)])
            if last:
                rows = slice(64 * hh, 64 * hh + 64)
                self.cp("dve", self.OT[rows, oc, qb * 512:(qb + 1) * 512], self.OACC[ab][rows, :],
                        [("acc", ab)], [("OT", oc, 4 * qb + j) for j in range(4)])

        LC = self.opt.get("l1_lc", 2)
        LE = LC + 1
        self.psa_wide = True
        for i in range(n + LE):
            if i < n:
                stage_a(items[i])
            if LC <= i < n + LC:
                stage_c(items[i - LC])
            if i >= LE:
                stage_e(items[i - LE])
            self.bg_pop(bg, i, n)
        self.bg_flush(bg)
        self.psa_wide = False

    def proj_fm(self, W, wsl, blk, wreg, dsts, scale, eng):
        ps, preg = self.psA()
        tsl = slice(blk * 512, (blk + 1) * 512)
        self.pe([(ps[:, :], W[:, c, wsl], self.HT[:, c, tsl], c == 0, c == 7) for c in range(8)],
                [wreg] + [("HT", 4 * blk + j) for j in range(4)], [preg])
        for dst, rows, dreg in dsts:
            if eng == "act":
                self.S.op("act", lambda e, dst=dst, rows=rows: e.activation(out=dst, in_=ps[rows, :], func=AF.Copy, scale=scale), [preg], [dreg])
            else:
                self.ts(eng, dst, ps[rows, :], scale, None, ALU.mult, None, [preg], [dreg])

    def v_proj(self, W, wsl, n, i, wreg, dst, src, sreg, vreg):
        ps, preg = self.psA()
        self.pe([(ps[:, 0:n], src[:, c, i * 128:(i + 1) * 128], W[:, c, wsl], c == 0, c == 7) for c in range(8)],
                [wreg, sreg], [preg])
        self.cp("dve", dst, ps[:, 0:n], [preg], [vreg])

    def pair_wload(self, layer, kind, p, widx):
        W = self.WP[widx]
        wreg = ("WP", widx)
        if layer == 0:
            if kind == "main":
                for k in range(3):
                    self.wload(W[:, :, 128 * k:128 * k + 128], self.d_wina[:, 768 * k + 128 * p:768 * k + 128 * p + 128], wreg)
            else:
                self.wload(W[:, :, 0:128], self.d_wina[:, 2316 + 128 * p:2316 + 128 * p + 128], wreg)
        else:
            if kind == "main":
                self.wload(W[:, :, 0:128], self.d_winb[:, 128 * p:128 * p + 128], wreg, 1)
                self.wload(W[:, :, 128:256], self.d_wkv[:, 128 * p:128 * p + 128], wreg, 0)
                self.wload(W[:, :, 256:384], self.d_wkv[:, 768 + 128 * p:768 + 128 * p + 128], wreg, 0)
            else:
                self.wload(W[:, :, 0:128], self.d_winb[:, 768 + 128 * p:768 + 128 * p + 128], wreg, 1)

    def pair_closures(self, layer, kind, p, st, widx, nxt_spec):
        S_ = self.SETS[st]
        W = self.WP[widx]
        wreg = ("WP", widx)
        h0 = slice(0, 64)
        h1 = slice(64, 128)
        brow = [64, 0]
        allq = lambda hh: [("QZ", st, hh, k) for k in range(4)]
        allk = lambda hh: [("KZ", st, hh, k) for k in range(4)]
        cl = []
        if layer == 0 and kind == "main":
            def bias():
                for hh in range(2):
                    h = 2 * p + hh
                    r = brow[hh]
                    self.dma("sp", S_["QZ"][hh][r:r + 1, :], self.CH[h:h + 1, 0, :], [("CH", 0)], allq(hh))
                    self.dma("sp", S_["QZ"][hh][r + 1:r + 2, :], self.CH[h:h + 1, 1, :], [("CH", 1)], allq(hh))
                    self.dma("sp", S_["KZ"][hh][r + 2:r + 3, :], self.CH[h:h + 1, 0, :], [("CH", 0)], allk(hh))
                    self.dma("sp", S_["KZ"][hh][r + 3:r + 4, :], self.CH[h:h + 1, 1, :], [("CH", 1)], allk(hh))
            cl.append(bias)
        if layer == 0 and kind == "mem":
            def clr():
                for hh in range(2):
                    r = brow[hh]
                    self.memset("pool", S_["QZ"][hh][r:r + 4, :], 0.0, allq(hh))
            cl.append(clr)
        qeng = "act" if layer == 0 else "dve"
        for blk in range(4):
            def qp(blk=blk):
                tsl = slice(blk * 512, (blk + 1) * 512)
                self.proj_fm(W, slice(0, 128), blk, wreg,
                             [(S_["QZ"][0][h0, tsl], h0, ("QZ", st, 0, blk)), (S_["QZ"][1][h1, tsl], h1, ("QZ", st, 1, blk))], 0.125, qeng)
            cl.append(qp)
        if kind == "main":
            for blk in range(4):
                def kp(blk=blk):
                    tsl = slice(blk * 512, (blk + 1) * 512)
                    if layer == 0:
                        dsts = [(S_["KZ"][0][h0, tsl], h0, ("KZ", st, 0, blk)), (S_["KZ"][1][h1, tsl], h1, ("KZ", st, 1, blk))]
                    else:
                        dsts = [(S_["KZ"][0][:, tsl], slice(0, 128), ("KZ", st, 0, blk))]
                    self.proj_fm(W, slice(128, 256), blk, wreg, dsts, 1.0, "dve")
                cl.append(kp)
            for i in range(NT):
                cl.append(lambda i=i: self.v_proj(W, slice(256, 384), 128, i, wreg, S_["VP"][:, i, :], self.HT, ("HT", i), ("V", st, i)))
        if nxt_spec is not None:
            cl.append(lambda: self.pair_wload(layer, *nxt_spec))
        return cl

    def run_pairs(self, layer):
        nmain = 6 if layer == 0 else self.opt.get("l1_pairs", 6)
        plist = [("main", p) for p in range(nmain)]
        if layer == 0 or self.opt.get("l1_mem", True):
            plist += [("mem", p) for p in range(2)]
        spec = lambda idx: (plist[idx] + (idx % 2,)) if idx < len(plist) else None
        self.pair_wload(layer, *spec(0))
        for c in self.pair_closures(layer, plist[0][0], plist[0][1], 0, 0, spec(1)):
            c()
        for idx, (kind, p) in enumerate(plist):
            st = idx % 2
            bgl = []
            if idx + 1 < len(plist):
                bgl = self.pair_closures(layer, plist[idx + 1][0], plist[idx + 1][1], 1 - st, (idx + 1) % 2, spec(idx + 2))
            bg = {"list": bgl, "done": 0}
            if kind == "main":
                if layer == 0:
                    VPt = self.SETS[st]["VP"]
                    self.attn_softmax_pair(st, None, lambda kt, VPt=VPt: VPt[:, kt, :], None, True, p,
                                           lambda hh, kt, st=st: ("KZ", st, hh, kt // 4), lambda kt, st=st: ("V", st, kt), bg)
                elif self.opt.get("l1_attn", True):
                    self.attn_sb_pair(st, p, bg)
                else:
                    self.bg_flush(bg)
            else:
                MK = self.MKT[:, p, :]
                self.attn_softmax_pair(st, [MK, MK], lambda kt, p=p: self.MVP[:, kt, 128 * p:128 * p + 128], 2, False, 6 + p,
                                       lambda hh, kt, p=p: ("MKT", p), lambda kt: ("MV", kt), bg)

    def mem_branch(self, s, l):
        self.load_gb(1 + l)
        self.wload(self.WM, self.d_wmemkv[l], ("WM",))
        for mt in range(2):
            self.dma("sp", self.MTMP[:], self.d_mem[s, mt * 128:(mt + 1) * 128, :], [], [("mtmp",)])
            self.norm_tile(self.MTMP[:], ("mtmp",), self.GB[:], self.HMT[:, :, mt * 128:(mt + 1) * 128], ("HMT", mt), "act")
        self.norm_flush()
        for p in range(2):
            ps, preg = self.psA()
            self.pe([(ps[:, 0:256], self.WM[:, c, 128 * p:128 * p + 128], self.HMT[:, c, :], c == 0, c == 7) for c in range(8)],
                    [("WM",), ("HMT", 0), ("HMT", 1)], [preg])
            self.cp("act", self.MKT[:, p, :], ps[:, 0:256], [preg], [("MKT", p)])
        for mt in range(2):
            self.v_proj(self.WM, slice(256, 512), 256, mt, ("WM",), self.MVP[:, mt, :], self.HMT, ("HMT", mt), ("MV", mt))

    def out_proj(self, l, post=None, prefetch=False):
        self.wload(self.WO, self.d_wout[l], ("WO",))
        if prefetch:
            self.ffn_prefetch(l)
        for i in range(NT):
            a = self.nxt("acc2", 2)
            ps = self.ACC2[a]
            regs = [("acc", 2 * a), ("acc", 2 * a + 1)]
            mms = []
            for h in range(2):
                for c in range(8):
                    mms.append((ps[:, 512 * h:512 * h + 512], self.OT[:, c, i * 128:(i + 1) * 128], self.WO[:, c, 512 * h:512 * h + 512], c == 0, c == 7))
            self.pe(mms, [("WO",)] + [("OT", c, i) for c in range(8)], regs)
            self.tt("dve", self.X[:, i, :], ps[:, :], self.X[:, i, :], ALU.add, regs + [("X", i)], [("X", i)])
            if post is not None:
                post(i)
        self.norm_flush()

    def norm_x_tile(self, i, gb):
        self.norm_tile(self.X[:, i, :], ("X", i), gb, self.HT[:, :, i * 128:(i + 1) * 128], ("HT", i), "act" if i % 2 else "dve")

    def ffn_prefetch(self, l):
        wup = self.d_wup[l]
        nf = 128 * FCH[0]
        self.wload(self.WUP[0][:, :, 0, 0:nf], wup[:, 0:nf], ("WUP", 0, 0))
        self.wload(self.WUP[0][:, :, 1, 0:nf], wup[:, DFF:DFF + nf], ("WUP", 0, 1))
        self.wload(self.WDN[0][:, 0:FCH[0], :], self.d_wdown[l][0:nf, :], ("WDN", 0))
        self.ffn_preloaded = l

    def ffn(self, l, post=None):
        wup = self.d_wup[l]
        wdn = self.d_wdown[l]
        nch = len(FCH)

        def load_up(fi):
            b = fi % 2
            f0 = 128 * sum(FCH[:fi])
            nf = 128 * FCH[fi]
            self.wload(self.WUP[b][:, :, 0, 0:nf], wup[:, f0:f0 + nf], ("WUP", b, 0))
            self.wload(self.WUP[b][:, :, 1, 0:nf], wup[:, DFF + f0:DFF + f0 + nf], ("WUP", b, 1))

        def load_dn(fi):
            b = fi % 2
            f0 = 128 * sum(FCH[:fi])
            nf = 128 * FCH[fi]
            self.wload(self.WDN[b][:, 0:FCH[fi], :], wdn[f0:f0 + nf, :], ("WDN", b))

        def down_tile(fi, i, final):
            b = fi % 2
            a = self.nxt("acc2", 2)
            ps = self.ACC2[a]
            regs = [("acc", 2 * a), ("acc", 2 * a + 1)]
            mms = []
            for h in range(2):
                for k in range(FCH[fi]):
                    mms.append((ps[:, 512 * h:512 * h + 512], self.AT[b][:, k, i * 128:(i + 1) * 128],
                                self.WDN[b][:, k, 512 * h:512 * h + 512], k == 0, k == FCH[fi] - 1))
            self.pe(mms, [("WDN", b)] + [("AT", b, k, i // 4) for k in range(FCH[fi])], regs)
            self.tt("dve", self.X[:, i, :], ps[:, :], self.X[:, i, :], ALU.add, regs + [("X", i)], [("X", i)])
            if final and post is not None:
                post(i)

        def step(fi, k, tb):
            b = fi % 2
            cbase = sum(FCH[:fi])
            tsl = slice(tb * 512, (tb + 1) * 512)
            gT = None
            for gv in range(2):
                ci = cbase + k + (DFF // 128) * gv
                ps, preg = self.psA()
                self.pe([(ps[:, :], self.WUP[b][:, c, gv, 128 * k:128 * k + 128], self.HT[:, c, tsl], c == 0, c == 7) for c in range(8)],
                        [("WUP", b, gv)] + [("HT", 4 * tb + j) for j in range(4)], [preg])
                UB = self.UBS[gv][tb % 2]
                ureg = ("UB", gv, tb % 2)
                if tb == 0:
                    self.memset("pool", UB[:, 0:2], 0.0, [ureg])
                else:
                    self.cp("pool", UB[:, 0:2], self.UBS[gv][(tb + 1) % 2][:, 512:514], [("UB", gv, (tb + 1) % 2)], [ureg])
                self.cp("act", UB[:, 2:514], ps[:, :], [preg], [ureg])
                tbuf = self.nxt("T%d" % gv, 2)
                T = (self.TG if gv == 0 else self.TV)[tbuf]
                treg = ("T", gv, tbuf)
                self.act(T[:, :], ps[:, :], AF.Identity, [preg, ("cw",), ("cb",)], [treg],
                         scale=self.CW[:, ci, 2:3], bias=self.CB[:, ci:ci + 1])
                self.stt(T[:, :], UB[:, 1:513], self.CW[:, ci, 1:2], T[:, :], ALU.mult, ALU.add, [ureg, treg, ("cw",)], [treg])
                self.stt(T[:, :], UB[:, 0:512], self.CW[:, ci, 0:1], T[:, :], ALU.mult, ALU.add, [ureg, treg, ("cw",)], [treg])
                if gv == 0:
                    gT = (T, treg)
                else:
                    self.act(gT[0][:, :], gT[0][:, :], AF.Silu, [gT[1]], [gT[1]])
                    self.tt("dve", self.AT[b][:, k, tsl], gT[0][:, :], T[:, :], ALU.mult, [gT[1], treg], [("AT", b, k, tb)])

        if self.ffn_preloaded != l:
            load_up(0)
            load_dn(0)
        self.ffn_preloaded = None
        pending = []
        for fi in range(nch):
            if fi + 1 < nch:
                load_up(fi + 1)
            steps = [(k, tb) for k in range(FCH[fi]) for tb in range(4)]
            for si, (k, tb) in enumerate(steps):
                step(fi, k, tb)
                left = len(steps) - si
                ne = (len(pending) + left - 1) // left
                for _ in range(ne):
                    pending.pop(0)()
            assert not pending
            if fi + 1 < nch:
                load_dn(fi + 1)
            pending = [(lambda fi=fi, i=i: down_tile(fi, i, fi == nch - 1)) for i in range(NT)]
        for cl in pending:
            cl()
        self.norm_flush()

    def layer0_mixer(self, s):
        self.mem_branch(s, 0)
        self.S.fence()
        self.load_gb(0)
        brow = [64, 0]

        def init_set(st):
            S_ = self.SETS[st]
            for hh in range(2):
                r = brow[hh]
                self.memset("pool", S_["QZ"][hh][:, :], 0.0, [("QZ", st, hh, k) for k in range(4)])
                self.memset("pool", S_["KZ"][hh][:, :], 0.0, [("KZ", st, hh, k) for k in range(4)])
                self.memset("pool", S_["QZ"][hh][r:r + 4, :], -1.0, [("QZ", st, hh, k) for k in range(4)])
                self.memset("pool", S_["KZ"][hh][r:r + 4, :], 1.0, [("KZ", st, hh, k) for k in range(4)])

        init_set(0)
        cf_alias = [("QZ", 1, hh, k) for hh in range(2) for k in range(4)]
        for i in range(NT):
            self.norm_x_tile(i, self.GB[:])
        self.norm_flush()
        WF = self.WP[0][:, :, 0:12]
        self.wload(WF, self.d_wina[:, 2304:2316], ("WP", 0))
        cfo_alias = [("WP", 1), ("PT", 0), ("PT", 1)]
        for blk in range(4):
            ps, preg = self.psA()
            tsl = slice(blk * 512, (blk + 1) * 512)
            self.pe([(ps[0:12, :], WF[:, c, :], self.HT[:, c, tsl], c == 0, c == 7) for c in range(8)],
                    [("WP", 0)] + [("HT", 4 * blk + j) for j in range(4)], [preg])
            self.act(self.CF[0:12, tsl], ps[0:12, :], AF.Exp, [preg], [("CF", blk)] + cf_alias, scale=-1.0, bias=self.NB[:, 0:1])
            self.act(self.CF[0:12, tsl], self.CF[0:12, tsl], AF.Ln, [("CF", blk)], [("CF", blk)], bias=1.0)
            self.ts("dve", self.CF[0:12, tsl], self.CF[0:12, tsl], -0.5, None, ALU.mult, None, [("CF", blk)], [("CF", blk)])
        self.S.op("dve", lambda e: e.tensor_tensor_scan(out=self.CFO[0:12, :], data0=self.CF[0:12, :], data1=self.CF[0:12, :],
                                                         initial=0.0, op0=ALU.add, op1=ALU.add),
                  [("CF", k) for k in range(4)] + cf_alias, [("CFO",)] + cfo_alias)
        self.cp("dve", self.CH[0:12, 0, :], self.CFO[0:12, :], [("CFO",)] + cfo_alias, [("CH", 0)])
        self.tt("dve", self.CH[0:12, 1, :], self.CFO[0:12, :], self.CH[0:12, 0, :], ALU.subtract, [("CFO",), ("CH", 0)] + cfo_alias, [("CH", 1)])
        if "cf" in self.d_dbg:
            self.dma("sp", self.d_dbg["cf"], self.CFO[0:12, :], [("CFO",)] + cfo_alias, [])
        init_set(1)
        self.run_pairs(0)

    def layer1_mixer(self, s):
        self.mem_branch(s, 1)
        self.S.fence()
        for st in range(2):
            for hh in range(2):
                self.memset("pool", self.SETS[st]["QZ"][hh][:, :], 0.0, [("QZ", st, hh, k) for k in range(4)])
        if not self.prenormed:
            for i in range(NT):
                self.norm_x_tile(i, None)
            self.norm_flush()
        self.run_pairs(1)

    def dump_x(self, name):
        if name in self.d_dbg:
            for i in range(NT):
                self.dma("sp", self.d_dbg[name][i * 128:(i + 1) * 128, :], self.X[:, i, :], [("X", i)], [])

    def final_tile(self, s, i, normed):
        ob = self.nxt("outt", 2)
        O = self.OUTT[ob]
        if normed:
            b = self.nxt("hn", 2)
            ss = self.STAT[:, 2 * b:2 * b + 1]
            rs = self.STAT[:, 2 * b + 1:2 * b + 2]
            xin = self.X[:, i, :]
            self.act(self.HN[b][:], xin, AF.Square, [("X", i)], [("hn", b), ("PT", 3 + 2 * b), ("PT", 4 + 2 * b), ("ss", b)], scale=1.0 / 32.0, accum_out=ss)
            self.act(ss, ss, AF.Sqrt, [("ss", b)], [("ss", b)], bias=self.EPSC[:, 0:1])
            self.S.op("dve", lambda e, rs=rs, ss=ss: e.reciprocal(out=rs, in_=ss), [("ss", b)], [("rs", b)])
            self.stt(O[:], xin, rs, self.GB[:], ALU.mult, ALU.mult, [("X", i), ("rs", b), ("gb",)], [("outt", ob)])
            self.dma("sp", self.d_out[s, i * 128:(i + 1) * 128, :], O[:], [("outt", ob)], [])
        else:
            self.dma("sp", self.d_out[s, i * 128:(i + 1) * 128, :], self.X[:, i, :], [("X", i)], [])

    def program(self):
        order = ["l0mix", "l0out", "l0ffn", "l1mix", "l1out", "l1ffn"]
        stop = self.stop
        nph = len(order) if stop is None else order.index(stop) + 1
        self.dma("pool", self.CONST[:], self.d_consts, [], [("const",)])
        self.dma("sp", self.GC[:], self.d_gc, [], [("gc",)])
        self.dma("sp", self.NB[:], self.d_bf, [], [("nb",)])
        self.memset("dve", self.EPSC[:], EPS, [("eps",)])
        self.ts("dve", self.NB[:], self.NB[:], -1.0, None, ALU.mult, None, [("nb",)], [("nb",)])
        self.S.fence()
        for s in range(self.nseq):
            for i in range(NT):
                self.dma("sp", self.X[:, i, :], self.d_x[s, i * 128:(i + 1) * 128, :], [], [("X", i)])
            phs = order[:nph] if self.phases is None else self.phases
            full = self.phases is None and stop is None
            self.prenormed = False
            for ph in phs:
                if ph == "l0mix":
                    self.layer0_mixer(s)
                elif ph == "l1mix":
                    self.layer1_mixer(s)
                elif ph in ("l0out", "l1out"):
                    l = int(ph[1])
                    self.load_gb(3 + l)
                    self.dma("sp", self.CW[:], self.d_cw[l], [], [("cw",)])
                    self.dma("sp", self.CB[:], self.d_cb[l], [], [("cb",)])
                    self.out_proj(l, lambda i: self.norm_x_tile(i, self.GB[:]), prefetch=("l%dffn" % l) in phs)
                    if s == 0:
                        self.dump_x("x_l%dmix" % l)
                elif ph == "l0ffn":
                    self.ffn(0, (lambda i: self.norm_x_tile(i, None)) if full else None)
                    self.prenormed = full
                    if s == 0:
                        self.dump_x("x_l0")
                elif ph == "l1ffn":
                    if full:
                        self.load_gb(5)
                    self.ffn(1, (lambda i, s=s: self.final_tile(s, i, True)) if full else None)
                self.S.fence()
            if not full:
                for i in range(NT):
                    self.final_tile(s, i, False)
            self.S.fence()

    def build(self):
        nc = self.nc
        nseq = self.nseq
        dt_in = lambda name, shape: nc.dram_tensor(name, shape, F32, kind="ExternalInput").ap()
        self.d_x = dt_in("x", [nseq, SEQ, D])
        self.d_mem = dt_in("mem", [nseq, NMEM, D])
        self.d_wina = dt_in("w_in_a", [D, A_IN])
        self.d_winb = dt_in("w_in_b", [D, D])
        self.d_wkv = dt_in("w_kv", [D, 1536])
        self.d_wmemkv = dt_in("w_memkv", [2, D, 512])
        self.d_wout = dt_in("w_out", [2, D, D])
        self.d_wup = dt_in("w_up", [2, D, 2 * DFF])
        self.d_wdown = dt_in("w_down", [2, DFF, D])
        self.d_gb = dt_in("gb", [6, 128, D])
        self.d_gc = dt_in("gc", [128, 2, 8])
        self.d_bf = dt_in("bf", [12, 1])
        self.d_cw = dt_in("cw", [2, 128, 44, 3])
        self.d_cb = dt_in("cb", [2, 128, 44])
        self.d_consts = dt_in("consts", [128, 7, 128])
        self.d_out = nc.dram_tensor("out", [nseq, SEQ, D], F32, kind="ExternalOutput").ap()
        self.d_dbg = {}
        for name, shape in self.dbg_shapes().items():
            if name in self.dbg:
                self.d_dbg[name] = nc.dram_tensor("dbg_" + name, shape, F32, kind="ExternalOutput").ap()

        with ExitStack() as es:
            sb = lambda name, shape, dt: es.enter_context(nc.sbuf_tensor(name, shape, dt))
            self.X = sb("X", [128, NT, D], F32)
            self.HT = sb("HT", [128, 8, SEQ], BF16)
            self.CONST = sb("CONST", [128, 7, 128], BF16)
            self.IDENT = self.CONST[:, 0, :]
            self.TRI = self.CONST[:, 1, :]
            self.TRIS = self.CONST[:, 2, :]
            self.NTRI = self.CONST[:, 3, :]
            self.NONES = self.CONST[:, 4, :]
            self.ZEROS = self.CONST[:, 5, :]
            self.ONES = self.CONST[:, 6, :]
            self.GB = sb("GB", [128, D], F32)
            self.GC = sb("GC", [128, 2, 8], F32)
            self.NB = sb("NB", [12, 1], F32)
            self.CW = sb("CW", [128, 44, 3], F32)
            self.CB = sb("CB", [128, 44], F32)
            self.STAT = sb("STAT", [128, 16], F32)
            self.EPSC = sb("EPSC", [128, 1], F32)
            self.HN = [sb("HN%d" % i, [128, D], BF16) for i in range(2)]
            self.ARENA_B = 103000
            self.ARENA = sb("ARENA", [128, self.ARENA_B // 2], BF16)
            self.PSA = [es.enter_context(nc.psum_tensor("psA%d" % i, [128, 512], F32)) for i in range(3)]
            self.ACC2 = [es.enter_context(nc.psum_tensor("acc2_%d" % i, [128, 1024], F32)) for i in range(2)]
            self.OACC = [self.ACC2[0][:, 0:512], self.ACC2[0][:, 512:1024]]
            self.LACC = [self.ACC2[1][:, 0:512], self.ACC2[1][:, 512:1024]]
            self.PST = es.enter_context(nc.psum_tensor("psT", [128, 8, 128], BF16))
            self.PSTF = self.PST[:, :, :].rearrange("p a b -> p (a b)").bitcast(F32)
            esem = {e: es.enter_context(nc.semaphore("sem_" + e)) for e in CENGS}
            dsem = [es.enter_context(nc.semaphore("dsem%d" % i)) for i in range(24)]
            self.S = Sched(esem, dsem)
            self.carve_all()
            self.program()
            self.emit()
        return nc

    def dbg_shapes(self):
        return {"x_l0mix": [SEQ, D], "x_l0": [SEQ, D], "x_l1mix": [SEQ, D], "cf": [12, SEQ]}

    def carve(self, off, shape, dt):
        n = int(np.prod(shape[1:]))
        esz = 2 if dt == BF16 else 4
        assert off % 4 == 0, off
        a = self.ARENA[:, off // 2:off // 2 + n * esz // 2]
        if dt == F32:
            a = a.bitcast(F32)
        if len(shape) == 3:
            a = a.rearrange("p (a b) -> p a b", a=shape[1])
        elif len(shape) == 4:
            a = a.rearrange("p (a b c) -> p a b c", a=shape[1], b=shape[2])
        self._off = off + n * esz
        assert self._off <= self.ARENA_B, self._off
        return a

    def carve_all(self):
        self.OT = self.carve(0, [128, 8, SEQ], BF16)
        o = self._off
        self.MTMP = self.carve(0, [128, D], F32)
        self.WM = self.carve(self._off, [128, 8, 512], BF16)
        self.HMT = self.carve(self._off, [128, 8, NMEM], BF16)
        self.SETS = []
        for st in range(2):
            d = {"QZ": [], "KZ": []}
            for i in range(2):
                d["QZ"].append(self.carve(o, [128, SEQ], BF16)); o = self._off
            for i in range(2):
                d["KZ"].append(self.carve(o, [128, SEQ], BF16)); o = self._off
            d["VP"] = self.carve(o, [128, NT, 128], BF16); o = self._off
            self.SETS.append(d)
        self.WP = []
        for i in range(2):
            self.WP.append(self.carve(o, [128, 8, 384], BF16)); o = self._off
        self.PT = []
        for i in range(3):
            self.PT.append(self.carve(o, [128, 512], BF16)); o = self._off
        for b in range(2):
            self.PT.append(self.HN[b][:, 0:512])
            self.PT.append(self.HN[b][:, 512:1024])
        self.MKT = self.carve(o, [128, 2, NMEM], BF16); o = self._off
        self.MVP = self.carve(o, [128, 2, 256], BF16); o = self._off
        self.RB = self.carve(o, [128, 512], F32); o = self._off
        o0 = o
        self.CH = self.carve(o, [128, 2, SEQ], BF16); o = self._off
        o = o0
        self.EX = []
        for i in range(2):
            self.EX.append(self.carve(o, [128, 512], F32)); o = self._off
        self.SPT = []
        for i in range(4):
            self.SPT.append(self.carve(o, [128, 512], BF16)); o = self._off
        self.SL = self.carve(o, [128, 512], BF16); o = self._off
        self.CF = self.SETS[1]["QZ"][0].bitcast(F32)[:, 0:1024] if False else self.carve(53248, [128, SEQ], F32)
        self.CFO = self.carve(79872, [128, SEQ], F32)
        self.WO = self.carve(32768, [128, 8, D], BF16)
        top = self.ARENA_B - 24576
        self.WUP = [self.carve(top, [128, 8, 2, 512], BF16), self.carve(0, [128, 8, 2, 512], BF16)]
        self.WDN = [self.carve(top + 16384, [128, 4, D], BF16), self.carve(16384, [128, 4, D], BF16)]
        o = 24576
        self.AT = []
        for i in range(2):
            self.AT.append(self.carve(o, [128, 4, SEQ], BF16)); o = self._off
        self.UBS = [[None, None], [None, None]]
        for g in range(2):
            for i in range(2):
                self.UBS[g][i] = self.carve(o, [128, 516], F32); o = self._off
        self.TG = []
        self.TV = []
        for i in range(2):
            self.TG.append(self.carve(o, [128, 512], F32)); o = self._off
            self.TV.append(self.carve(o, [128, 512], F32)); o = self._off
        assert o <= top, (o, top)
        self.OUTT = [self.carve(24576, [128, D], F32), self.carve(24576 + 4096, [128, D], F32)]

    def emit(self):
        nc = self.nc
        S = self.S
        with nc.Block() as block:
            @block.tensor
            def _(e):
                for f in S.prog["pe"]:
                    f(e)

            @block.scalar
            def _(e):
                for f in S.prog["act"]:
                    f(e)

            @block.vector
            def _(e):
                for f in S.prog["dve"]:
                    f(e)

            @block.gpsimd
            def _(e):
                for f in S.prog["pool"]:
                    f(e)

            @block.sync
            def _(e):
                for f in S.prog["sp"]:
                    f(e)


def host_inputs(x, mem, ln_mix_g, w_in_a, b_f_a, w_in_b, ln_kv_g, w_kv, ln_mem_g, w_memkv, w_out, ln_ffn_g,
                w_up, conv_w, conv_b, w_down, final_g):
    f = lambda a: np.ascontiguousarray(np.asarray(a, dtype=np.float32))
    rep = lambda g: np.broadcast_to(np.asarray(g, np.float32)[None, :], (128, D))
    gb = f(np.stack([rep(ln_mix_g[0]), rep(ln_mem_g[0]), rep(ln_mem_g[1]), rep(ln_ffn_g[0]), rep(ln_ffn_g[1]), rep(final_g)]))
    col = lambda g: np.asarray(g, np.float32).reshape(8, 128).T
    gc = f(np.stack([col(ln_kv_g), col(ln_mix_g[1])], axis=1))
    cw = f(np.asarray(conv_w, np.float32).reshape(2, 3, 44, 128).transpose(0, 3, 2, 1))
    cb = f(np.asarray(conv_b, np.float32).reshape(2, 44, 128).transpose(0, 2, 1))
    i = np.arange(128)
    ident = (i[:, None] == i[None, :]).astype(np.float32)
    tri = (i[:, None] <= i[None, :]).astype(np.float32)
    tris = (i[:, None] < i[None, :]).astype(np.float32)
    ntri = -(i[:, None] >= i[None, :]).astype(np.float32)
    nones = -np.ones((128, 128), np.float32)
    consts = f(np.stack([ident, tri, tris, ntri, nones, 0.0 * nones, -nones], axis=1))
    return {
        "w_in_a": f(w_in_a[0]), "w_in_b": f(w_in_b[0]), "w_kv": f(w_kv), "w_memkv": f(w_memkv), "w_out": f(w_out),
        "w_up": f(w_up), "w_down": f(w_down), "gb": gb, "gc": gc, "bf": f(np.asarray(b_f_a, np.float32).reshape(12, 1)),
        "cw": cw, "cb": cb, "consts": consts,
    }


def kernel(**inputs):
    x = np.asarray(inputs["x"], np.float32)
    mem = np.asarray(inputs["mem"], np.float32)
    shared = host_inputs(**inputs)
    nseq = x.shape[0] // NCORES
    nc = Builder(nseq=nseq).build()
    in_maps = []
    for c in range(NCORES):
        m = dict(shared)
        m["x"] = np.ascontiguousarray(x[c * nseq:(c + 1) * nseq])
        m["mem"] = np.ascontiguousarray(mem[c * nseq:(c + 1) * nseq])
        in_maps.append(m)
    res = run_bass_kernel_spmd(nc, in_maps, core_ids=list(range(NCORES)))
    return np.concatenate([np.asarray(r["out"], np.float32) for r in res.results], axis=0)
```

```python
import numpy as np
from contextlib import ExitStack
import concourse.bass as bass
import concourse.mybir as mybir
from concourse.bass_utils import run_bass_kernel_spmd

F32 = mybir.dt.float32
BF16 = mybir.dt.bfloat16
AF = mybir.ActivationFunctionType
ALU = mybir.AluOpType

D = 1024
SEQ = 2048
NT = SEQ // 128
NMEM = 256
DFF = 2816
A_IN = 2572
EPS = 1e-6
NCORES = 8
ENGS = ["pe", "act", "dve", "pool", "sp"]
CENGS = ["pe", "act", "dve", "pool"]
FCH = [4, 4, 4, 4, 4, 2]


class Sched:
    def __init__(self, esem, dsem):
        self.esem = esem
        self.dsem = dsem
        self.prog = {e: [] for e in ENGS}
        self.cnt = {e: 0 for e in CENGS}
        self.dcnt = [0] * len(dsem)
        self.dpool = {"sp": list(range(0, 16)), "pool": list(range(16, 24)), "act": list(range(16, 24))}
        self.dnext = {"sp": 0, "pool": 0, "act": 0}
        self.known = {e: {} for e in ENGS}
        self.state = {}

    def _sem(self, src):
        return self.esem[src] if isinstance(src, str) else self.dsem[src]

    def _need(self, eng, reads, writes):
        need = {}

        def add(w, kind):
            src, val = w
            if src == eng and (eng == "pe" or kind != "RAW"):
                return
            if need.get(src, 0) < val:
                need[src] = val

        for r in reads:
            st = self.state.get(r)
            if st is not None and st[0] is not None:
                add(st[0], "RAW")
        for w in writes:
            st = self.state.get(w)
            if st is not None:
                if st[0] is not None:
                    add(st[0], "WAW")
                for s_, v_ in st[1].items():
                    add((s_, v_), "WAR")
        out = []
        kn = self.known[eng]
        for src, val in need.items():
            if kn.get(src, 0) < val:
                kn[src] = val
                out.append((self._sem(src), val))
        return out

    def _update(self, me, reads, writes):
        src, val = me
        for r in reads:
            st = self.state.get(r)
            if st is None:
                st = [None, {}]
                self.state[r] = st
            st[1][src] = val
        for w in writes:
            self.state[w] = [me, {}]

    def op(self, eng, fn, reads=(), writes=()):
        wl = self._need(eng, reads, writes)
        self.cnt[eng] += 1
        sem = self.esem[eng]

        def emit(e, wl=wl, fn=fn, sem=sem):
            for sm, v in wl:
                e.wait_ge(sm, v)
            fn(e).then_inc(sem, 1)

        self.prog[eng].append(emit)
        self._update((eng, self.cnt[eng]), reads, writes)

    def dma(self, q, out, in_, reads=(), writes=()):
        pl = self.dpool[q]
        j = pl[self.dnext[q] % len(pl)]
        self.dnext[q] += 1
        wl = self._need(q, reads, writes)
        prev = 16 * self.dcnt[j]
        if prev > 0 and self.known[q].get(j, 0) < prev:
            self.known[q][j] = prev
            wl.append((self.dsem[j], prev))
        self.dcnt[j] += 1
        sem = self.dsem[j]

        def emit(e, wl=wl, out=out, in_=in_, sem=sem):
            for sm, v in wl:
                e.wait_ge(sm, v)
            e.dma_start(out=out, in_=in_).then_inc(sem, 16)

        self.prog[q].append(emit)
        self._update((j, 16 * self.dcnt[j]), reads, writes)

    def fence(self):
        srcs = [(e, self.cnt[e]) for e in CENGS] + [(j, 16 * c) for j, c in enumerate(self.dcnt)]
        for e in ENGS:
            wl = []
            for src, val in srcs:
                if src == e and e == "pe":
                    continue
                if val > self.known[e].get(src, 0):
                    self.known[e][src] = val
                    wl.append((self._sem(src), val))

            def emit(en, wl=wl):
                for sm, v in wl:
                    en.wait_ge(sm, v)

            self.prog[e].append(emit)
        self.state = {}


class Builder:
    def __init__(self, nseq=2, stop=None, dbg=(), phases=None, opt=None):
        self.phases = phases
        self.opt = opt or {}
        self.nseq = nseq
        self.stop = stop
        self.dbg = set(dbg)
        self.nc = bass.Bass("TRN2", target_bir_lowering=False)
        self.rr = {}
        self.norm_pending = []
        self.psa_wide = False
        self.ffn_preloaded = None
        self.wo_loaded = None
        self.x_prefetched = None

    def nxt(self, name, n):
        v = self.rr.get(name, 0)
        self.rr[name] = (v + 1) % n
        return v

    def pe(self, mms, reads, writes):
        def fn(e, mms=mms):
            ins = None
            for m in mms:
                if m[0] == "tr":
                    ins = e.transpose(out=m[1], in_=m[2], identity=self.IDENT)
                elif len(m) == 6:
                    out, lhsT, rhs, start, stop, _ = m
                    ins = e.matmul(out, lhsT, rhs, start=start, stop=stop, skip_group_check=True)
                else:
                    out, lhsT, rhs, start, stop = m
                    ins = e.matmul(out, lhsT, rhs, start=start, stop=stop)
            return ins

        self.S.op("pe", fn, reads, writes)

    def act(self, out, in_, func, reads, writes, **kw):
        self.S.op("act", lambda e: e.activation(out=out, in_=in_, func=func, **kw), reads, writes)

    def ts(self, eng, out, in0, s1, s2, op0, op1, reads, writes):
        if op1 is None:
            self.S.op(eng, lambda e: e.tensor_scalar(out=out, in0=in0, scalar1=s1, scalar2=s2, op0=op0), reads, writes)
        else:
            self.S.op(eng, lambda e: e.tensor_scalar(out=out, in0=in0, scalar1=s1, scalar2=s2, op0=op0, op1=op1), reads, writes)

    def tt(self, eng, out, in0, in1, op, reads, writes):
        self.S.op(eng, lambda e: e.tensor_tensor(out=out, in0=in0, in1=in1, op=op), reads, writes)

    def stt(self, out, in0, scalar, in1, op0, op1, reads, writes):
        self.S.op("dve", lambda e: e.scalar_tensor_tensor(out=out, in0=in0, scalar=scalar, in1=in1, op0=op0, op1=op1), reads, writes)

    def cp(self, eng, out, in_, reads, writes):
        if eng == "act":
            self.S.op("act", lambda e: e.activation(out=out, in_=in_, func=AF.Copy), reads, writes)
        else:
            self.S.op(eng, lambda e: e.tensor_copy(out=out, in_=in_), reads, writes)

    def memset(self, eng, ap, val, writes):
        self.S.op(eng, lambda e: e.memset(ap, val), (), writes)

    def dma(self, q, out, in_, reads, writes):
        self.S.dma(q, out, in_, reads, writes)

    def psA(self, z=False):
        if self.psa_wide:
            if z:
                k = self.nxt("psAz", 5)
                if k >= 3:
                    return self.LACC[k - 3], ("acc", 2 + k - 3)
                return self.PSA[k], ("psA", k)
            return self.PSTF, ("psT",)
        k = self.nxt("psA", 3)
        return self.PSA[k], ("psA", k)

    def norm_tile(self, xin, xreg, gb, dst, dstreg, evac_eng):
        b = self.nxt("hn", 2)
        ss = self.STAT[:, 2 * b:2 * b + 1]
        rs = self.STAT[:, 2 * b + 1:2 * b + 2]
        hn = self.HN[b]
        hreg = [("hn", b), ("PT", 3 + 2 * b), ("PT", 4 + 2 * b)]
        self.act(hn[:], xin, AF.Square, [xreg], hreg + [("ss", b)], scale=1.0 / 32.0, accum_out=ss)
        self.act(ss, ss, AF.Sqrt, [("ss", b)], [("ss", b)], bias=self.EPSC[:, 0:1])
        self.S.op("dve", lambda e: e.reciprocal(out=rs, in_=ss), [("ss", b)], [("rs", b)])
        if gb is None:
            self.ts("dve", hn[:], xin, rs, None, ALU.mult, None, [xreg, ("rs", b)], hreg)
        else:
            self.stt(hn[:], xin, rs, gb, ALU.mult, ALU.mult, [xreg, ("rs", b), ("gb",)], hreg)
        def part2():
            self.pe([("tr", self.PST[:, c, :], hn[:, c * 128:(c + 1) * 128]) for c in range(8)], hreg, [("psT",)])
            self.cp(evac_eng, dst, self.PST[:, :, :], [("psT",)], [dstreg])

        self.norm_flush()
        self.norm_pending.append(part2)

    def norm_flush(self):
        while self.norm_pending:
            self.norm_pending.pop(0)()

    def load_gb(self, idx):
        self.dma("sp", self.GB[:], self.d_gb[idx], [], [("gb",)])

    def wload(self, dst, src2d, reg, fold=None):
        self.dma("pool", dst, src2d.rearrange("(c p) n -> p c n", p=128), [], [reg])
        if fold is not None:
            for c in range(8):
                self.ts("pool", dst[:, c, :], dst[:, c, :], self.GC[:, fold, c:c + 1], 1.0, ALU.mult, ALU.mult, [reg], [reg])

    def bg_pop(self, bg, i, n):
        if not bg or not bg["list"]:
            return
        tot = len(bg["list"])
        den = max(1, int(0.8 * n))
        target = min(tot, -(-tot * (i + 1) // den))
        while bg["done"] < target:
            bg["list"][bg["done"]]()
            bg["done"] += 1

    def bg_flush(self, bg):
        if bg:
            while bg["done"] < len(bg["list"]):
                bg["list"][bg["done"]]()
                bg["done"] += 1

    def attn_softmax_pair(self, st, KZo, VP, nkt, causal, oc, kreg, vreg, bg=None):
        QZ = self.SETS[st]["QZ"]
        KZ = KZo if KZo is not None else self.SETS[st]["KZ"]
        items = []
        for qb in range(4):
            for hh in range(2):
                kts = list(range(4 * qb + 4)) if causal else list(range(nkt))
                for kt in kts:
                    items.append((qb, hh, kt, kt == kts[0], kt == kts[-1]))
        n = len(items)
        pend = {}

        def stage_a(it):
            qb, hh, kt, first, last = it
            j0 = kt - 4 * qb if causal else -1
            col0 = max(0, j0) * 128
            S_, sreg = self.psA()
            qc = slice(qb * 512 + col0, (qb + 1) * 512)
            self.pe([(S_[:, col0:512], KZ[hh][:, kt * 128:(kt + 1) * 128], QZ[hh][:, qc], True, True)],
                    [kreg(hh, kt), ("QZ", st, hh, qb)], [sreg])
            pb = self.nxt("PT", 7)
            PT = self.PT[pb]
            self.act(PT[:, col0:512], S_[:, col0:512], AF.Exp, [sreg], [("PT", pb)])
            if j0 >= 0:
                self.tt("dve", PT[:, col0:col0 + 128], PT[:, col0:col0 + 128], self.TRI, ALU.mult, [("PT", pb)], [("PT", pb)])
            pend[it] = (pb, col0)

        def stage_c(it):
            qb, hh, kt, first, last = it
            pb, col0 = pend.pop(it)
            PT = self.PT[pb]
            if first:
                self.accb = self.nxt("accb", 2)
            ab = self.accb
            self.pe([(self.OACC[ab][:, col0:512], VP(kt), PT[:, col0:512], first, last),
                     (self.LACC[ab][:, col0:512], self.ONES, PT[:, col0:512], first, last)],
                    [("PT", pb), vreg(kt)], [("acc", ab), ("acc", 2 + ab)])
            if last:
                rows = slice(64 * hh, 64 * hh + 64)
                rb = self.RB
                self.act(rb[rows, :], self.LACC[ab][rows, :], AF.Ln, [("acc", 2 + ab)], [("RB",)])
                self.act(rb[rows, :], rb[rows, :], AF.Exp, [("RB",)], [("RB",)], scale=-1.0)
                self.tt("dve", self.OT[rows, oc, qb * 512:(qb + 1) * 512], self.OACC[ab][rows, :], rb[rows, :], ALU.mult,
                        [("acc", ab), ("RB",)], [("OT", oc, 4 * qb + j) for j in range(4)])

        LG = self.opt.get("l0_lag", 3)
        for i in range(n + LG):
            if i < n:
                stage_a(items[i])
            if i >= LG:
                stage_c(items[i - LG])
            self.bg_pop(bg, i, n)
        self.bg_flush(bg)

    def attn_sb_pair(self, st, oc, bg=None):
        QZ = self.SETS[st]["QZ"]
        KP = self.SETS[st]["KZ"][0]
        VPt = self.SETS[st]["VP"]
        VP = lambda kt: VPt[:, kt, :]
        items = []
        for qb in range(4):
            for hh in range(2):
                kts = list(reversed(range(4 * qb + 4)))
                for kt in kts:
                    items.append((qb, hh, kt, kt == kts[0], kt == kts[-1]))
        n = len(items)
        pend = {}

        def stage_a(it):
            qb, hh, kt, first, last = it
            j0 = kt - 4 * qb
            col0 = max(0, j0) * 128
            Z, zreg = self.psA(z=True)
            qc = slice(qb * 512 + col0, (qb + 1) * 512)
            self.pe([(Z[:, col0:512], KP[:, kt * 128:(kt + 1) * 128], QZ[hh][:, qc], True, True)],
                    [("KZ", st, 0, kt // 4), ("QZ", st, hh, qb)], [zreg])
            eb = self.nxt("E", 2)
            E = self.EX[eb]
            self.act(E[:, col0:512], Z[:, col0:512], AF.Exp, [zreg], [("E", eb)])
            pend[it] = (Z, zreg, eb, col0, j0)

        def stage_a2(it):
            Z, zreg, eb, col0, j0 = pend[it]
            E = self.EX[eb]
            sb = self.nxt("SP", 4)
            SP = self.SPT[sb]
            self.act(SP[:, col0:512], E[:, col0:512], AF.Ln, [("E", eb)], [("SP", sb)], bias=1.0)
            if j0 >= 0:
                self.tt("pool", SP[:, col0:col0 + 128], SP[:, col0:col0 + 128], self.TRIS, ALU.mult, [("SP", sb)], [("SP", sb)])
            pend[it] = (Z, zreg, sb, col0, j0)

        def stage_c(it):
            qb, hh, kt, first, last = it
            Z, zreg, sb, col0, j0 = pend[it]
            SP = self.SPT[sb]
            if first:
                self.memset("pool", self.SL[:, :], 0.0, [("SL",)])
            mms = [(Z[:, col0:512], self.NTRI, SP[:, col0:512], False, first, "nochk")]
            rd = [("SP", sb)]
            if not first:
                mms.append((Z[:, col0:512], self.NONES, self.SL[:, col0:512], False, True, "nochk"))
                rd.append(("SL",))
            self.pe(mms, rd, [zreg])
            if not last:
                self.tt("dve", self.SL[:, col0:512], self.SL[:, col0:512], SP[:, col0:512], ALU.add, [("SL",), ("SP", sb)], [("SL",)])
            pb = self.nxt("PT", 7)
            PT = self.PT[pb]
            self.act(PT[:, col0:512], Z[:, col0:512], AF.Exp, [zreg], [("PT", pb)])
            if j0 >= 0:
                self.tt("dve", PT[:, col0:col0 + 128], PT[:, col0:col0 + 128], self.TRIS, ALU.mult, [("PT", pb)], [("PT", pb)])
            pend[it] = (pb, col0)

        def stage_e(it):
            qb, hh, kt, first, last = it
            pb, col0 = pend.pop(it)
            PT = self.PT[pb]
            mms = []
            if first:
                self.accb = self.nxt("accb", 2)
                mms.append((self.OACC[self.accb][:, :], self.ZEROS, QZ[hh][:, qb * 512:(qb + 1) * 512], True, False))
            ab = self.accb
            mms.append((self.OACC[ab][:, col0:512], VP(kt), PT[:, col0:512], False, last))
            self.pe(mms, [("PT", pb), ("V", st, kt), ("QZ", st, hh, qb)], [("acc", ab)])
            if last:
                rows = slice(64 * hh, 64 * hh + 64)
                self.cp("dve", self.OT[rows, oc, qb * 512:(qb + 1) * 512], self.OACC[ab][rows, :],
                        [("acc", ab)], [("OT", oc, 4 * qb + j) for j in range(4)])

        LC = self.opt.get("l1_lc", 2)
        LE = LC + 1
        self.psa_wide = True
        for i in range(n + LE):
            if i < n:
                stage_a(items[i])
            if LC <= i < n + LC:
                stage_c(items[i - LC])
            if i < n:
                stage_a2(items[i])
            if i >= LE:
                stage_e(items[i - LE])
            self.bg_pop(bg, i, n)
        self.bg_flush(bg)
        self.psa_wide = False

    def proj_fm(self, W, wsl, blk, wreg, dsts, scale, eng):
        ps, preg = self.psA()
        tsl = slice(blk * 512, (blk + 1) * 512)
        self.pe([(ps[:, :], W[:, c, wsl], self.HT[:, c, tsl], c == 0, c == 7) for c in range(8)],
                [wreg] + [("HT", 4 * blk + j) for j in range(4)], [preg])
        for dst, rows, dreg in dsts:
            if eng == "act":
                self.S.op("act", lambda e, dst=dst, rows=rows: e.activation(out=dst, in_=ps[rows, :], func=AF.Copy, scale=scale), [preg], [dreg])
            else:
                self.ts(eng, dst, ps[rows, :], scale, None, ALU.mult, None, [preg], [dreg])

    def v_proj(self, W, wsl, n, i, wreg, dst, src, sreg, vreg):
        ps, preg = self.psA()
        self.pe([(ps[:, 0:n], src[:, c, i * 128:(i + 1) * 128], W[:, c, wsl], c == 0, c == 7) for c in range(8)],
                [wreg, sreg], [preg])
        self.cp("dve", dst, ps[:, 0:n], [preg], [vreg])

    def pair_wload(self, layer, kind, p, widx):
        W = self.WP[widx]
        wreg = ("WP", widx)
        if layer == 0:
            if kind == "main":
                for k in range(3):
                    self.wload(W[:, :, 128 * k:128 * k + 128], self.d_wina[:, 768 * k + 128 * p:768 * k + 128 * p + 128], wreg)
            else:
                self.wload(W[:, :, 0:128], self.d_wina[:, 2316 + 128 * p:2316 + 128 * p + 128], wreg)
        else:
            if kind == "main":
                self.wload(W[:, :, 0:128], self.d_winb[:, 128 * p:128 * p + 128], wreg, 1)
                self.wload(W[:, :, 128:256], self.d_wkv[:, 128 * p:128 * p + 128], wreg, 0)
                self.wload(W[:, :, 256:384], self.d_wkv[:, 768 + 128 * p:768 + 128 * p + 128], wreg, 0)
            else:
                self.wload(W[:, :, 0:128], self.d_winb[:, 768 + 128 * p:768 + 128 * p + 128], wreg, 1)

    def pair_closures(self, layer, kind, p, st, widx, nxt_spec):
        S_ = self.SETS[st]
        W = self.WP[widx]
        wreg = ("WP", widx)
        h0 = slice(0, 64)
        h1 = slice(64, 128)
        brow = [64, 0]
        allq = lambda hh: [("QZ", st, hh, k) for k in range(4)]
        allk = lambda hh: [("KZ", st, hh, k) for k in range(4)]
        cl = []
        if layer == 0 and kind == "main":
            def bias():
                for hh in range(2):
                    h = 2 * p + hh
                    r = brow[hh]
                    self.dma("sp", S_["QZ"][hh][r:r + 1, :], self.CH[h:h + 1, 0, :], [("CH", 0)], allq(hh))
                    self.dma("sp", S_["QZ"][hh][r + 1:r + 2, :], self.CH[h:h + 1, 1, :], [("CH", 1)], allq(hh))
                    self.dma("sp", S_["KZ"][hh][r + 2:r + 3, :], self.CH[h:h + 1, 0, :], [("CH", 0)], allk(hh))
                    self.dma("sp", S_["KZ"][hh][r + 3:r + 4, :], self.CH[h:h + 1, 1, :], [("CH", 1)], allk(hh))
            cl.append(bias)
        if layer == 0 and kind == "mem":
            def clr():
                for hh in range(2):
                    r = brow[hh]
                    self.memset("pool", S_["QZ"][hh][r:r + 4, :], 0.0, allq(hh))
            cl.append(clr)
        qeng = "act" if layer == 0 else "dve"
        for blk in range(4):
            def qp(blk=blk):
                tsl = slice(blk * 512, (blk + 1) * 512)
                self.proj_fm(W, slice(0, 128), blk, wreg,
                             [(S_["QZ"][0][h0, tsl], h0, ("QZ", st, 0, blk)), (S_["QZ"][1][h1, tsl], h1, ("QZ", st, 1, blk))], 0.125, qeng)
            cl.append(qp)
        if kind == "main":
            for blk in range(4):
                def kp(blk=blk):
                    tsl = slice(blk * 512, (blk + 1) * 512)
                    if layer == 0:
                        dsts = [(S_["KZ"][0][h0, tsl], h0, ("KZ", st, 0, blk)), (S_["KZ"][1][h1, tsl], h1, ("KZ", st, 1, blk))]
                    else:
                        dsts = [(S_["KZ"][0][:, tsl], slice(0, 128), ("KZ", st, 0, blk))]
                    self.proj_fm(W, slice(128, 256), blk, wreg, dsts, 1.0, "dve")
                cl.append(kp)
            for i in range(NT):
                cl.append(lambda i=i: self.v_proj(W, slice(256, 384), 128, i, wreg, S_["VP"][:, i, :], self.HT, ("HT", i), ("V", st, i)))
        if nxt_spec is not None:
            cl.append(lambda: self.pair_wload(layer, *nxt_spec))
        return cl

    def run_pairs(self, layer):
        nmain = 6 if layer == 0 else self.opt.get("l1_pairs", 6)
        plist = [("main", p) for p in range(nmain)]
        if layer == 0 or self.opt.get("l1_mem", True):
            plist += [("mem", p) for p in range(2)]
        spec = lambda idx: (plist[idx] + (idx % 2,)) if idx < len(plist) else None
        self.pair_wload(layer, *spec(0))
        for c in self.pair_closures(layer, plist[0][0], plist[0][1], 0, 0, spec(1)):
            c()
        for idx, (kind, p) in enumerate(plist):
            st = idx % 2
            bgl = []
            if idx + 1 < len(plist):
                bgl = self.pair_closures(layer, plist[idx + 1][0], plist[idx + 1][1], 1 - st, (idx + 1) % 2, spec(idx + 2))
            if idx == len(plist) - 1 and st == 1 and self.opt.get("wo_prefetch", True):
                set0 = [("QZ", 0, hh, k) for hh in range(2) for k in range(4)] + [("KZ", 0, hh, k) for hh in range(2) for k in range(4)] \
                    + [("V", 0, i) for i in range(NT)]
                bgl = [lambda: self.dma("pool", self.WO, self.d_wout[layer].rearrange("(c p) n -> p c n", p=128), [], [("WO",)] + set0)]
                self.wo_loaded = layer
            bg = {"list": bgl, "done": 0}
            if kind == "main":
                if layer == 0:
                    VPt = self.SETS[st]["VP"]
                    self.attn_softmax_pair(st, None, lambda kt, VPt=VPt: VPt[:, kt, :], None, True, p,
                                           lambda hh, kt, st=st: ("KZ", st, hh, kt // 4), lambda kt, st=st: ("V", st, kt), bg)
                elif self.opt.get("l1_attn", True):
                    self.attn_sb_pair(st, p, bg)
                else:
                    self.bg_flush(bg)
            else:
                MK = self.MKT[:, p, :]
                self.attn_softmax_pair(st, [MK, MK], lambda kt, p=p: self.MVP[:, kt, 128 * p:128 * p + 128], 2, False, 6 + p,
                                       lambda hh, kt, p=p: ("MKT", p), lambda kt: ("MV", kt), bg)

    def mem_branch(self, s, l):
        self.load_gb(1 + l)
        self.wload(self.WM, self.d_wmemkv[l], ("WM",))
        for mt in range(2):
            self.dma("sp", self.MTMP[:], self.d_mem[s, mt * 128:(mt + 1) * 128, :], [], [("mtmp",)])
            self.norm_tile(self.MTMP[:], ("mtmp",), self.GB[:], self.HMT[:, :, mt * 128:(mt + 1) * 128], ("HMT", mt), "act")
        self.norm_flush()
        for p in range(2):
            ps, preg = self.psA()
            self.pe([(ps[:, 0:256], self.WM[:, c, 128 * p:128 * p + 128], self.HMT[:, c, :], c == 0, c == 7) for c in range(8)],
                    [("WM",), ("HMT", 0), ("HMT", 1)], [preg])
            self.cp("act", self.MKT[:, p, :], ps[:, 0:256], [preg], [("MKT", p)])
        for mt in range(2):
            self.v_proj(self.WM, slice(256, 512), 256, mt, ("WM",), self.MVP[:, mt, :], self.HMT, ("HMT", mt), ("MV", mt))

    def out_proj(self, l, post=None, prefetch=False):
        if self.wo_loaded != l:
            self.wload(self.WO, self.d_wout[l], ("WO",))
        self.wo_loaded = None
        if prefetch:
            self.ffn_prefetch(l)
        for i in range(NT):
            a = self.nxt("acc2", 2)
            ps = self.ACC2[a]
            regs = [("acc", 2 * a), ("acc", 2 * a + 1)]
            mms = []
            for h in range(2):
                for c in range(8):
                    mms.append((ps[:, 512 * h:512 * h + 512], self.OT[:, c, i * 128:(i + 1) * 128], self.WO[:, c, 512 * h:512 * h + 512], c == 0, c == 7))
            self.pe(mms, [("WO",)] + [("OT", c, i) for c in range(8)], regs)
            self.tt("dve", self.X[:, i, :], ps[:, :], self.X[:, i, :], ALU.add, regs + [("X", i)], [("X", i)])
            if post is not None:
                post(i)
        self.norm_flush()

    def norm_x_tile(self, i, gb):
        self.norm_tile(self.X[:, i, :], ("X", i), gb, self.HT[:, :, i * 128:(i + 1) * 128], ("HT", i), "act" if i % 2 else "dve")

    def ffn_prefetch(self, l):
        wup = self.d_wup[l]
        nf = 128 * FCH[0]
        self.wload(self.WUP[0][:, :, 0, 0:nf], wup[:, 0:nf], ("WUP", 0, 0))
        self.wload(self.WUP[0][:, :, 1, 0:nf], wup[:, DFF:DFF + nf], ("WUP", 0, 1))
        self.wload(self.WDN[0][:, 0:FCH[0], :], self.d_wdown[l][0:nf, :], ("WDN", 0))
        self.ffn_preloaded = l

    def ffn(self, l, post=None):
        wup = self.d_wup[l]
        wdn = self.d_wdown[l]
        nch = len(FCH)

        def load_up(fi):
            b = fi % 2
            f0 = 128 * sum(FCH[:fi])
            nf = 128 * FCH[fi]
            self.wload(self.WUP[b][:, :, 0, 0:nf], wup[:, f0:f0 + nf], ("WUP", b, 0))
            self.wload(self.WUP[b][:, :, 1, 0:nf], wup[:, DFF + f0:DFF + f0 + nf], ("WUP", b, 1))

        def load_dn(fi):
            b = fi % 2
            f0 = 128 * sum(FCH[:fi])
            nf = 128 * FCH[fi]
            self.wload(self.WDN[b][:, 0:FCH[fi], :], wdn[f0:f0 + nf, :], ("WDN", b))

        def down_tile(fi, i, final):
            b = fi % 2
            a = self.nxt("acc2", 2)
            ps = self.ACC2[a]
            regs = [("acc", 2 * a), ("acc", 2 * a + 1)]
            mms = []
            for h in range(2):
                for k in range(FCH[fi]):
                    mms.append((ps[:, 512 * h:512 * h + 512], self.AT[b][:, k, i * 128:(i + 1) * 128],
                                self.WDN[b][:, k, 512 * h:512 * h + 512], k == 0, k == FCH[fi] - 1))
            self.pe(mms, [("WDN", b)] + [("AT", b, k, i // 4) for k in range(FCH[fi])], regs)
            self.tt("dve", self.X[:, i, :], ps[:, :], self.X[:, i, :], ALU.add, regs + [("X", i)], [("X", i)])
            if final and post is not None:
                post(i)

        def step(fi, k, tb):
            b = fi % 2
            cbase = sum(FCH[:fi])
            tsl = slice(tb * 512, (tb + 1) * 512)
            gT = None
            for gv in range(2):
                ci = cbase + k + (DFF // 128) * gv
                ps, preg = self.psA()
                self.pe([(ps[:, :], self.WUP[b][:, c, gv, 128 * k:128 * k + 128], self.HT[:, c, tsl], c == 0, c == 7) for c in range(8)],
                        [("WUP", b, gv)] + [("HT", 4 * tb + j) for j in range(4)], [preg])
                UB = self.UBS[gv][tb % 2]
                ureg = ("UB", gv, tb % 2)
                if tb == 0:
                    self.memset("pool", UB[:, 0:2], 0.0, [ureg])
                else:
                    self.cp("pool", UB[:, 0:2], self.UBS[gv][(tb + 1) % 2][:, 512:514], [("UB", gv, (tb + 1) % 2)], [ureg])
                self.cp("act", UB[:, 2:514], ps[:, :], [preg], [ureg])
                tbuf = self.nxt("T%d" % gv, 2)
                T = (self.TG if gv == 0 else self.TV)[tbuf]
                treg = ("T", gv, tbuf)
                self.act(T[:, :], ps[:, :], AF.Identity, [preg, ("cw",), ("cb",)], [treg],
                         scale=self.CW[:, ci, 2:3], bias=self.CB[:, ci:ci + 1])
                self.stt(T[:, :], UB[:, 1:513], self.CW[:, ci, 1:2], T[:, :], ALU.mult, ALU.add, [ureg, treg, ("cw",)], [treg])
                self.stt(T[:, :], UB[:, 0:512], self.CW[:, ci, 0:1], T[:, :], ALU.mult, ALU.add, [ureg, treg, ("cw",)], [treg])
                if gv == 0:
                    gT = (T, treg)
                else:
                    self.act(gT[0][:, :], gT[0][:, :], AF.Silu, [gT[1]], [gT[1]])
                    self.tt("dve", self.AT[b][:, k, tsl], gT[0][:, :], T[:, :], ALU.mult, [gT[1], treg], [("AT", b, k, tb)])

        if self.ffn_preloaded != l:
            load_up(0)
            load_dn(0)
        self.ffn_preloaded = None
        pending = []
        for fi in range(nch):
            if fi + 1 < nch:
                load_up(fi + 1)
            steps = [(k, tb) for k in range(FCH[fi]) for tb in range(4)]
            for si, (k, tb) in enumerate(steps):
                step(fi, k, tb)
                left = len(steps) - si
                ne = (len(pending) + left - 1) // left
                for _ in range(ne):
                    pending.pop(0)()
            assert not pending
            if fi + 1 < nch:
                load_dn(fi + 1)
            pending = [(lambda fi=fi, i=i: down_tile(fi, i, fi == nch - 1)) for i in range(NT)]
        for cl in pending:
            cl()
        self.norm_flush()

    def layer0_mixer(self, s):
        self.mem_branch(s, 0)
        self.S.fence()
        self.load_gb(0)
        brow = [64, 0]

        def init_set(st):
            S_ = self.SETS[st]
            for hh in range(2):
                r = brow[hh]
                self.memset("pool", S_["QZ"][hh][:, :], 0.0, [("QZ", st, hh, k) for k in range(4)])
                self.memset("pool", S_["KZ"][hh][:, :], 0.0, [("KZ", st, hh, k) for k in range(4)])
                self.memset("pool", S_["QZ"][hh][r:r + 4, :], -1.0, [("QZ", st, hh, k) for k in range(4)])
                self.memset("pool", S_["KZ"][hh][r:r + 4, :], 1.0, [("KZ", st, hh, k) for k in range(4)])

        init_set(0)
        cf_alias = [("QZ", 1, hh, k) for hh in range(2) for k in range(4)]
        for i in range(NT):
            self.norm_x_tile(i, self.GB[:])
        self.norm_flush()
        WF = self.WP[0][:, :, 0:12]
        self.wload(WF, self.d_wina[:, 2304:2316], ("WP", 0))
        cfo_alias = [("WP", 1), ("PT", 0), ("PT", 1)]
        for blk in range(4):
            ps, preg = self.psA()
            tsl = slice(blk * 512, (blk + 1) * 512)
            self.pe([(ps[0:12, :], WF[:, c, :], self.HT[:, c, tsl], c == 0, c == 7) for c in range(8)],
                    [("WP", 0)] + [("HT", 4 * blk + j) for j in range(4)], [preg])
            self.act(self.CF[0:12, tsl], ps[0:12, :], AF.Exp, [preg], [("CF", blk)] + cf_alias, scale=-1.0, bias=self.NB[:, 0:1])
            self.act(self.CF[0:12, tsl], self.CF[0:12, tsl], AF.Ln, [("CF", blk)], [("CF", blk)], bias=1.0)
            self.ts("dve", self.CF[0:12, tsl], self.CF[0:12, tsl], -0.5, None, ALU.mult, None, [("CF", blk)], [("CF", blk)])
        self.S.op("dve", lambda e: e.tensor_tensor_scan(out=self.CFO[0:12, :], data0=self.CF[0:12, :], data1=self.CF[0:12, :],
                                                         initial=0.0, op0=ALU.add, op1=ALU.add),
                  [("CF", k) for k in range(4)] + cf_alias, [("CFO",)] + cfo_alias)
        self.cp("dve", self.CH[0:12, 0, :], self.CFO[0:12, :], [("CFO",)] + cfo_alias, [("CH", 0)])
        self.tt("dve", self.CH[0:12, 1, :], self.CFO[0:12, :], self.CH[0:12, 0, :], ALU.subtract, [("CFO",), ("CH", 0)] + cfo_alias, [("CH", 1)])
        if "cf" in self.d_dbg:
            self.dma("sp", self.d_dbg["cf"], self.CFO[0:12, :], [("CFO",)] + cfo_alias, [])
        init_set(1)
        self.run_pairs(0)

    def layer1_mixer(self, s):
        self.mem_branch(s, 1)
        self.S.fence()
        for st in range(2):
            for hh in range(2):
                self.memset("pool", self.SETS[st]["QZ"][hh][:, :], 0.0, [("QZ", st, hh, k) for k in range(4)])
        if not self.prenormed:
            for i in range(NT):
                self.norm_x_tile(i, None)
            self.norm_flush()
        self.run_pairs(1)

    def dump_x(self, name):
        if name in self.d_dbg:
            for i in range(NT):
                self.dma("sp", self.d_dbg[name][i * 128:(i + 1) * 128, :], self.X[:, i, :], [("X", i)], [])

    def final_tile(self, s, i, normed):
        ob = self.nxt("outt", 2)
        O = self.OUTT[ob]
        if normed:
            b = self.nxt("hn", 2)
            ss = self.STAT[:, 2 * b:2 * b + 1]
            rs = self.STAT[:, 2 * b + 1:2 * b + 2]
            xin = self.X[:, i, :]
            self.act(self.HN[b][:], xin, AF.Square, [("X", i)], [("hn", b), ("PT", 3 + 2 * b), ("PT", 4 + 2 * b), ("ss", b)], scale=1.0 / 32.0, accum_out=ss)
            self.act(ss, ss, AF.Sqrt, [("ss", b)], [("ss", b)], bias=self.EPSC[:, 0:1])
            self.S.op("dve", lambda e, rs=rs, ss=ss: e.reciprocal(out=rs, in_=ss), [("ss", b)], [("rs", b)])
            self.stt(O[:], xin, rs, self.GB[:], ALU.mult, ALU.mult, [("X", i), ("rs", b), ("gb",)], [("outt", ob)])
            self.dma("sp", self.d_out[s, i * 128:(i + 1) * 128, :], O[:], [("outt", ob)], [])
            if s + 1 < self.nseq:
                self.dma("sp", self.X[:, i, :], self.d_x[s + 1, i * 128:(i + 1) * 128, :], [], [("X", i)])
                self.x_prefetched = s + 1
        else:
            self.dma("sp", self.d_out[s, i * 128:(i + 1) * 128, :], self.X[:, i, :], [("X", i)], [])

    def program(self):
        order = ["l0mix", "l0out", "l0ffn", "l1mix", "l1out", "l1ffn"]
        stop = self.stop
        nph = len(order) if stop is None else order.index(stop) + 1
        self.dma("pool", self.CONST[:], self.d_consts, [], [("const",)])
        self.dma("sp", self.GC[:], self.d_gc, [], [("gc",)])
        self.dma("sp", self.NB[:], self.d_bf, [], [("nb",)])
        self.memset("dve", self.EPSC[:], EPS, [("eps",)])
        self.ts("dve", self.NB[:], self.NB[:], -1.0, None, ALU.mult, None, [("nb",)], [("nb",)])
        self.S.fence()
        for s in range(self.nseq):
            if self.x_prefetched != s:
                for i in range(NT):
                    self.dma("sp", self.X[:, i, :], self.d_x[s, i * 128:(i + 1) * 128, :], [], [("X", i)])
            phs = order[:nph] if self.phases is None else self.phases
            full = self.phases is None and stop is None
            self.prenormed = False
            for ph in phs:
                if ph == "l0mix":
                    self.layer0_mixer(s)
                elif ph == "l1mix":
                    self.layer1_mixer(s)
                elif ph in ("l0out", "l1out"):
                    l = int(ph[1])
                    self.load_gb(3 + l)
                    self.dma("sp", self.CW[:], self.d_cw[l], [], [("cw",)])
                    self.dma("sp", self.CB[:], self.d_cb[l], [], [("cb",)])
                    self.out_proj(l, lambda i: self.norm_x_tile(i, self.GB[:]), prefetch=("l%dffn" % l) in phs)
                    if s == 0:
                        self.dump_x("x_l%dmix" % l)
                elif ph == "l0ffn":
                    self.ffn(0, (lambda i: self.norm_x_tile(i, None)) if full else None)
                    self.prenormed = full
                    if s == 0:
                        self.dump_x("x_l0")
                elif ph == "l1ffn":
                    if full:
                        self.load_gb(5)
                    self.ffn(1, (lambda i, s=s: self.final_tile(s, i, True)) if full else None)
                self.S.fence()
            if not full:
                for i in range(NT):
                    self.final_tile(s, i, False)
            self.S.fence()

    def build(self):
        nc = self.nc
        nseq = self.nseq
        dt_in = lambda name, shape: nc.dram_tensor(name, shape, F32, kind="ExternalInput").ap()
        self.d_x = dt_in("x", [nseq, SEQ, D])
        self.d_mem = dt_in("mem", [nseq, NMEM, D])
        self.d_wina = dt_in("w_in_a", [D, A_IN])
        self.d_winb = dt_in("w_in_b", [D, D])
        self.d_wkv = dt_in("w_kv", [D, 1536])
        self.d_wmemkv = dt_in("w_memkv", [2, D, 512])
        self.d_wout = dt_in("w_out", [2, D, D])
        self.d_wup = dt_in("w_up", [2, D, 2 * DFF])
        self.d_wdown = dt_in("w_down", [2, DFF, D])
        self.d_gb = dt_in("gb", [6, 128, D])
        self.d_gc = dt_in("gc", [128, 2, 8])
        self.d_bf = dt_in("bf", [12, 1])
        self.d_cw = dt_in("cw", [2, 128, 44, 3])
        self.d_cb = dt_in("cb", [2, 128, 44])
        self.d_consts = dt_in("consts", [128, 7, 128])
        self.d_out = nc.dram_tensor("out", [nseq, SEQ, D], F32, kind="ExternalOutput").ap()
        self.d_dbg = {}
        for name, shape in self.dbg_shapes().items():
            if name in self.dbg:
                self.d_dbg[name] = nc.dram_tensor("dbg_" + name, shape, F32, kind="ExternalOutput").ap()

        with ExitStack() as es:
            sb = lambda name, shape, dt: es.enter_context(nc.sbuf_tensor(name, shape, dt))
            self.X = sb("X", [128, NT, D], F32)
            self.HT = sb("HT", [128, 8, SEQ], BF16)
            self.CONST = sb("CONST", [128, 7, 128], BF16)
            self.IDENT = self.CONST[:, 0, :]
            self.TRI = self.CONST[:, 1, :]
            self.TRIS = self.CONST[:, 2, :]
            self.NTRI = self.CONST[:, 3, :]
            self.NONES = self.CONST[:, 4, :]
            self.ZEROS = self.CONST[:, 5, :]
            self.ONES = self.CONST[:, 6, :]
            self.GB = sb("GB", [128, D], F32)
            self.GC = sb("GC", [128, 2, 8], F32)
            self.NB = sb("NB", [12, 1], F32)
            self.CW = sb("CW", [128, 44, 3], F32)
            self.CB = sb("CB", [128, 44], F32)
            self.STAT = sb("STAT", [128, 16], F32)
            self.EPSC = sb("EPSC", [128, 1], F32)
            self.HN = [sb("HN%d" % i, [128, D], BF16) for i in range(2)]
            self.ARENA_B = 103000
            self.ARENA = sb("ARENA", [128, self.ARENA_B // 2], BF16)
            self.PSA = [es.enter_context(nc.psum_tensor("psA%d" % i, [128, 512], F32)) for i in range(3)]
            self.ACC2 = [es.enter_context(nc.psum_tensor("acc2_%d" % i, [128, 1024], F32)) for i in range(2)]
            self.OACC = [self.ACC2[0][:, 0:512], self.ACC2[0][:, 512:1024]]
            self.LACC = [self.ACC2[1][:, 0:512], self.ACC2[1][:, 512:1024]]
            self.PST = es.enter_context(nc.psum_tensor("psT", [128, 8, 128], BF16))
            self.PSTF = self.PST[:, :, :].rearrange("p a b -> p (a b)").bitcast(F32)
            esem = {e: es.enter_context(nc.semaphore("sem_" + e)) for e in CENGS}
            dsem = [es.enter_context(nc.semaphore("dsem%d" % i)) for i in range(24)]
            self.S = Sched(esem, dsem)
            self.carve_all()
            self.program()
            self.emit()
        return nc

    def dbg_shapes(self):
        return {"x_l0mix": [SEQ, D], "x_l0": [SEQ, D], "x_l1mix": [SEQ, D], "cf": [12, SEQ]}

    def carve(self, off, shape, dt):
        n = int(np.prod(shape[1:]))
        esz = 2 if dt == BF16 else 4
        assert off % 4 == 0, off
        a = self.ARENA[:, off // 2:off // 2 + n * esz // 2]
        if dt == F32:
            a = a.bitcast(F32)
        if len(shape) == 3:
            a = a.rearrange("p (a b) -> p a b", a=shape[1])
        elif len(shape) == 4:
            a = a.rearrange("p (a b c) -> p a b c", a=shape[1], b=shape[2])
        self._off = off + n * esz
        assert self._off <= self.ARENA_B, self._off
        return a

    def carve_all(self):
        self.OT = self.carve(0, [128, 8, SEQ], BF16)
        o = self._off
        self.MTMP = self.carve(0, [128, D], F32)
        self.WM = self.carve(self._off, [128, 8, 512], BF16)
        self.HMT = self.carve(self._off, [128, 8, NMEM], BF16)
        self.SETS = []
        for st in range(2):
            d = {"QZ": [], "KZ": []}
            for i in range(2):
                d["QZ"].append(self.carve(o, [128, SEQ], BF16)); o = self._off
            for i in range(2):
                d["KZ"].append(self.carve(o, [128, SEQ], BF16)); o = self._off
            d["VP"] = self.carve(o, [128, NT, 128], BF16); o = self._off
            self.SETS.append(d)
        self.WP = []
        for i in range(2):
            self.WP.append(self.carve(o, [128, 8, 384], BF16)); o = self._off
        self.PT = []
        for i in range(3):
            self.PT.append(self.carve(o, [128, 512], BF16)); o = self._off
        for b in range(2):
            self.PT.append(self.HN[b][:, 0:512])
            self.PT.append(self.HN[b][:, 512:1024])
        self.MKT = self.carve(o, [128, 2, NMEM], BF16); o = self._off
        self.MVP = self.carve(o, [128, 2, 256], BF16); o = self._off
        self.RB = self.carve(o, [128, 512], F32); o = self._off
        o0 = o
        self.CH = self.carve(o, [128, 2, SEQ], BF16); o = self._off
        o = o0
        self.EX = []
        for i in range(2):
            self.EX.append(self.carve(o, [128, 512], F32)); o = self._off
        self.SPT = []
        for i in range(4):
            self.SPT.append(self.carve(o, [128, 512], BF16)); o = self._off
        self.SL = self.carve(o, [128, 512], BF16); o = self._off
        self.CF = self.SETS[1]["QZ"][0].bitcast(F32)[:, 0:1024] if False else self.carve(53248, [128, SEQ], F32)
        self.CFO = self.carve(79872, [128, SEQ], F32)
        self.WO = self.carve(32768, [128, 8, D], BF16)
        top = self.ARENA_B - 24576
        self.WUP = [self.carve(top, [128, 8, 2, 512], BF16), self.carve(0, [128, 8, 2, 512], BF16)]
        self.WDN = [self.carve(top + 16384, [128, 4, D], BF16), self.carve(16384, [128, 4, D], BF16)]
        o = 24576
        self.AT = []
        for i in range(2):
            self.AT.append(self.carve(o, [128, 4, SEQ], BF16)); o = self._off
        self.UBS = [[None, None], [None, None]]
        for g in range(2):
            for i in range(2):
                self.UBS[g][i] = self.carve(o, [128, 516], F32); o = self._off
        self.TG = []
        self.TV = []
        for i in range(2):
            self.TG.append(self.carve(o, [128, 512], F32)); o = self._off
            self.TV.append(self.carve(o, [128, 512], F32)); o = self._off
        assert o <= top, (o, top)
        self.OUTT = [self.carve(24576, [128, D], F32), self.carve(24576 + 4096, [128, D], F32)]

    def emit(self):
        nc = self.nc
        S = self.S
        with nc.Block() as block:
            @block.tensor
            def _(e):
                for f in S.prog["pe"]:
                    f(e)

            @block.scalar
            def _(e):
                for f in S.prog["act"]:
                    f(e)

            @block.vector
            def _(e):
                for f in S.prog["dve"]:
                    f(e)

            @block.gpsimd
            def _(e):
                for f in S.prog["pool"]:
                    f(e)

            @block.sync
            def _(e):
                for f in S.prog["sp"]:
                    f(e)


def host_inputs(x, mem, ln_mix_g, w_in_a, b_f_a, w_in_b, ln_kv_g, w_kv, ln_mem_g, w_memkv, w_out, ln_ffn_g,
                w_up, conv_w, conv_b, w_down, final_g):
    f = lambda a: np.ascontiguousarray(np.asarray(a, dtype=np.float32))
    rep = lambda g: np.broadcast_to(np.asarray(g, np.float32)[None, :], (128, D))
    gb = f(np.stack([rep(ln_mix_g[0]), rep(ln_mem_g[0]), rep(ln_mem_g[1]), rep(ln_ffn_g[0]), rep(ln_ffn_g[1]), rep(final_g)]))
    col = lambda g: np.asarray(g, np.float32).reshape(8, 128).T
    gc = f(np.stack([col(ln_kv_g), col(ln_mix_g[1])], axis=1))
    cw = f(np.asarray(conv_w, np.float32).reshape(2, 3, 44, 128).transpose(0, 3, 2, 1))
    cb = f(np.asarray(conv_b, np.float32).reshape(2, 44, 128).transpose(0, 2, 1))
    i = np.arange(128)
    ident = (i[:, None] == i[None, :]).astype(np.float32)
    tri = (i[:, None] <= i[None, :]).astype(np.float32)
    tris = (i[:, None] < i[None, :]).astype(np.float32)
    ntri = -(i[:, None] >= i[None, :]).astype(np.float32)
    nones = -np.ones((128, 128), np.float32)
    consts = f(np.stack([ident, tri, tris, ntri, nones, 0.0 * nones, -nones], axis=1))
    return {
        "w_in_a": f(w_in_a[0]), "w_in_b": f(w_in_b[0]), "w_kv": f(w_kv), "w_memkv": f(w_memkv), "w_out": f(w_out),
        "w_up": f(w_up), "w_down": f(w_down), "gb": gb, "gc": gc, "bf": f(np.asarray(b_f_a, np.float32).reshape(12, 1)),
        "cw": cw, "cb": cb, "consts": consts,
    }


def kernel(**inputs):
    x = np.asarray(inputs["x"], np.float32)
    mem = np.asarray(inputs["mem"], np.float32)
    shared = host_inputs(**inputs)
    nseq = x.shape[0] // NCORES
    nc = Builder(nseq=nseq).build()
    in_maps = []
    for c in range(NCORES):
        m = dict(shared)
        m["x"] = np.ascontiguousarray(x[c * nseq:(c + 1) * nseq])
        m["mem"] = np.ascontiguousarray(mem[c * nseq:(c + 1) * nseq])
        in_maps.append(m)
    res = run_bass_kernel_spmd(nc, in_maps, core_ids=list(range(NCORES)))
    return np.concatenate([np.asarray(r["out"], np.float32) for r in res.results], axis=0)
```

```python
import numpy as np
from contextlib import ExitStack
import concourse.bass as bass
import concourse.mybir as mybir
from concourse.bass_utils import run_bass_kernel_spmd

F32 = mybir.dt.float32
BF16 = mybir.dt.bfloat16
AF = mybir.ActivationFunctionType
ALU = mybir.AluOpType

D = 1024
SEQ = 2048
NT = SEQ // 128
NMEM = 256
DFF = 2816
A_IN = 2572
EPS = 1e-6
NCORES = 8
ENGS = ["pe", "act", "dve", "pool", "sp"]
CENGS = ["pe", "act", "dve", "pool"]
FCH = [4, 4, 4, 4, 4, 2]


class Sched:
    def __init__(self, esem, dsem):
        self.esem = esem
        self.dsem = dsem
        self.prog = {e: [] for e in ENGS}
        self.cnt = {e: 0 for e in CENGS}
        self.dcnt = [0] * len(dsem)
        self.dpool = {"sp": list(range(0, 16)), "pool": list(range(16, 24)), "act": list(range(16, 24))}
        self.dnext = {"sp": 0, "pool": 0, "act": 0}
        self.known = {e: {} for e in ENGS}
        self.state = {}

    def _sem(self, src):
        return self.esem[src] if isinstance(src, str) else self.dsem[src]

    def _need(self, eng, reads, writes):
        need = {}

        def add(w, kind):
            src, val = w
            if src == eng and (eng == "pe" or kind != "RAW"):
                return
            if need.get(src, 0) < val:
                need[src] = val

        for r in reads:
            st = self.state.get(r)
            if st is not None and st[0] is not None:
                add(st[0], "RAW")
        for w in writes:
            st = self.state.get(w)
            if st is not None:
                if st[0] is not None:
                    add(st[0], "WAW")
                for s_, v_ in st[1].items():
                    add((s_, v_), "WAR")
        out = []
        kn = self.known[eng]
        for src, val in need.items():
            if kn.get(src, 0) < val:
                kn[src] = val
                out.append((self._sem(src), val))
        return out

    def _update(self, me, reads, writes):
        src, val = me
        for r in reads:
            st = self.state.get(r)
            if st is None:
                st = [None, {}]
                self.state[r] = st
            st[1][src] = val
        for w in writes:
            self.state[w] = [me, {}]

    def op(self, eng, fn, reads=(), writes=()):
        wl = self._need(eng, reads, writes)
        self.cnt[eng] += 1
        sem = self.esem[eng]

        def emit(e, wl=wl, fn=fn, sem=sem):
            for sm, v in wl:
                e.wait_ge(sm, v)
            fn(e).then_inc(sem, 1)

        self.prog[eng].append(emit)
        self._update((eng, self.cnt[eng]), reads, writes)

    def dma(self, q, out, in_, reads=(), writes=()):
        pl = self.dpool[q]
        j = pl[self.dnext[q] % len(pl)]
        self.dnext[q] += 1
        wl = self._need(q, reads, writes)
        prev = 16 * self.dcnt[j]
        if prev > 0 and self.known[q].get(j, 0) < prev:
            self.known[q][j] = prev
            wl.append((self.dsem[j], prev))
        self.dcnt[j] += 1
        sem = self.dsem[j]

        def emit(e, wl=wl, out=out, in_=in_, sem=sem):
            for sm, v in wl:
                e.wait_ge(sm, v)
            e.dma_start(out=out, in_=in_).then_inc(sem, 16)

        self.prog[q].append(emit)
        self._update((j, 16 * self.dcnt[j]), reads, writes)

    def fence(self):
        srcs = [(e, self.cnt[e]) for e in CENGS] + [(j, 16 * c) for j, c in enumerate(self.dcnt)]
        for e in ENGS:
            wl = []
            for src, val in srcs:
                if src == e and e == "pe":
                    continue
                if val > self.known[e].get(src, 0):
                    self.known[e][src] = val
                    wl.append((self._sem(src), val))

            def emit(en, wl=wl):
                for sm, v in wl:
                    en.wait_ge(sm, v)

            self.prog[e].append(emit)
        self.state = {}


class Builder:
    def __init__(self, nseq=2, stop=None, dbg=(), phases=None, opt=None):
        self.phases = phases
        self.opt = opt or {}
        self.nseq = nseq
        self.stop = stop
        self.dbg = set(dbg)
        self.nc = bass.Bass("TRN2", target_bir_lowering=False)
        self.rr = {}
        self.norm_pending = []
        self.psa_wide = False
        self.ffn_preloaded = None
        self.wo_loaded = None
        self.x_prefetched = None

    def nxt(self, name, n):
        v = self.rr.get(name, 0)
        self.rr[name] = (v + 1) % n
        return v

    def pe(self, mms, reads, writes):
        def fn(e, mms=mms):
            ins = None
            for m in mms:
                if m[0] == "tr":
                    ins = e.transpose(out=m[1], in_=m[2], identity=self.IDENT)
                elif len(m) == 6:
                    out, lhsT, rhs, start, stop, _ = m
                    ins = e.matmul(out, lhsT, rhs, start=start, stop=stop, skip_group_check=True)
                else:
                    out, lhsT, rhs, start, stop = m
                    ins = e.matmul(out, lhsT, rhs, start=start, stop=stop)
            return ins

        self.S.op("pe", fn, reads, writes)

    def act(self, out, in_, func, reads, writes, **kw):
        self.S.op("act", lambda e: e.activation(out=out, in_=in_, func=func, **kw), reads, writes)

    def ts(self, eng, out, in0, s1, s2, op0, op1, reads, writes):
        if op1 is None:
            self.S.op(eng, lambda e: e.tensor_scalar(out=out, in0=in0, scalar1=s1, scalar2=s2, op0=op0), reads, writes)
        else:
            self.S.op(eng, lambda e: e.tensor_scalar(out=out, in0=in0, scalar1=s1, scalar2=s2, op0=op0, op1=op1), reads, writes)

    def tt(self, eng, out, in0, in1, op, reads, writes):
        self.S.op(eng, lambda e: e.tensor_tensor(out=out, in0=in0, in1=in1, op=op), reads, writes)

    def stt(self, out, in0, scalar, in1, op0, op1, reads, writes):
        self.S.op("dve", lambda e: e.scalar_tensor_tensor(out=out, in0=in0, scalar=scalar, in1=in1, op0=op0, op1=op1), reads, writes)

    def cp(self, eng, out, in_, reads, writes):
        if eng == "act":
            self.S.op("act", lambda e: e.activation(out=out, in_=in_, func=AF.Copy), reads, writes)
        else:
            self.S.op(eng, lambda e: e.tensor_copy(out=out, in_=in_), reads, writes)

    def memset(self, eng, ap, val, writes):
        self.S.op(eng, lambda e: e.memset(ap, val), (), writes)

    def dma(self, q, out, in_, reads, writes):
        self.S.dma(q, out, in_, reads, writes)

    def psA(self, z=False):
        if self.psa_wide:
            if z:
                k = self.nxt("psAz", 5)
                if k >= 3:
                    return self.LACC[k - 3], ("acc", 2 + k - 3)
                return self.PSA[k], ("psA", k)
            return self.PSTF, ("psT",)
        k = self.nxt("psA", 3)
        return self.PSA[k], ("psA", k)

    def norm_tile(self, xin, xreg, gb, dst, dstreg, evac_eng):
        b = self.nxt("hn", 2)
        ss = self.STAT[:, 2 * b:2 * b + 1]
        rs = self.STAT[:, 2 * b + 1:2 * b + 2]
        hn = self.HN[b]
        hreg = [("hn", b), ("PT", 3 + 2 * b), ("PT", 4 + 2 * b)]
        self.act(hn[:], xin, AF.Square, [xreg], hreg + [("ss", b)], scale=1.0 / 32.0, accum_out=ss)
        self.act(ss, ss, AF.Sqrt, [("ss", b)], [("ss", b)], bias=self.EPSC[:, 0:1])
        self.S.op("dve", lambda e: e.reciprocal(out=rs, in_=ss), [("ss", b)], [("rs", b)])
        if gb is None:
            self.ts("dve", hn[:], xin, rs, None, ALU.mult, None, [xreg, ("rs", b)], hreg)
        else:
            self.stt(hn[:], xin, rs, gb, ALU.mult, ALU.mult, [xreg, ("rs", b), ("gb",)], hreg)
        def part2():
            self.pe([("tr", self.PST[:, c, :], hn[:, c * 128:(c + 1) * 128]) for c in range(8)], hreg, [("psT",)])
            self.cp(evac_eng, dst, self.PST[:, :, :], [("psT",)], [dstreg])

        self.norm_flush()
        self.norm_pending.append(part2)

    def norm_flush(self):
        while self.norm_pending:
            self.norm_pending.pop(0)()

    def load_gb(self, idx):
        self.dma("sp", self.GB[:], self.d_gb[idx], [], [("gb",)])

    def wload(self, dst, src2d, reg, fold=None, part="all"):
        if part in ("all", "dma"):
            self.dma("pool", dst, src2d.rearrange("(c p) n -> p c n", p=128), [], [reg])
        if fold is not None and part in ("all", "fold"):
            for c in range(8):
                self.ts("pool", dst[:, c, :], dst[:, c, :], self.GC[:, fold, c:c + 1], 1.0, ALU.mult, ALU.mult, [reg], [reg])

    def bg_pop(self, bg, i, n):
        if not bg or not bg["list"]:
            return
        tot = len(bg["list"])
        den = max(1, int(0.8 * n))
        target = min(tot, -(-tot * (i + 1) // den))
        while bg["done"] < target:
            bg["list"][bg["done"]]()
            bg["done"] += 1

    def bg_flush(self, bg):
        if bg:
            while bg["done"] < len(bg["list"]):
                bg["list"][bg["done"]]()
                bg["done"] += 1

    def attn_softmax_pair(self, st, KZo, VP, nkt, causal, oc, kreg, vreg, bg=None):
        QZ = self.SETS[st]["QZ"]
        KZ = KZo if KZo is not None else self.SETS[st]["KZ"]
        items = []
        for qb in range(4):
            for hh in range(2):
                kts = list(range(4 * qb + 4)) if causal else list(range(nkt))
                for kt in kts:
                    items.append((qb, hh, kt, kt == kts[0], kt == kts[-1]))
        n = len(items)
        pend = {}

        def stage_a(it):
            qb, hh, kt, first, last = it
            j0 = kt - 4 * qb if causal else -1
            col0 = max(0, j0) * 128
            S_, sreg = self.psA()
            qc = slice(qb * 512 + col0, (qb + 1) * 512)
            self.pe([(S_[:, col0:512], KZ[hh][:, kt * 128:(kt + 1) * 128], QZ[hh][:, qc], True, True)],
                    [kreg(hh, kt), ("QZ", st, hh, qb)], [sreg])
            pb = self.nxt("PT", 7)
            PT = self.PT[pb]
            self.act(PT[:, col0:512], S_[:, col0:512], AF.Exp, [sreg], [("PT", pb)])
            if j0 >= 0:
                self.tt("dve", PT[:, col0:col0 + 128], PT[:, col0:col0 + 128], self.TRI, ALU.mult, [("PT", pb)], [("PT", pb)])
            pend[it] = (pb, col0)

        def stage_c(it):
            qb, hh, kt, first, last = it
            pb, col0 = pend.pop(it)
            PT = self.PT[pb]
            if first:
                self.accb = self.nxt("accb", 2)
            ab = self.accb
            self.pe([(self.OACC[ab][:, col0:512], VP(kt), PT[:, col0:512], first, last),
                     (self.LACC[ab][:, col0:512], self.ONES, PT[:, col0:512], first, last)],
                    [("PT", pb), vreg(kt)], [("acc", ab), ("acc", 2 + ab)])
            if last:
                rows = slice(64 * hh, 64 * hh + 64)
                rb = self.RB
                self.act(rb[rows, :], self.LACC[ab][rows, :], AF.Ln, [("acc", 2 + ab)], [("RB",)])
                self.act(rb[rows, :], rb[rows, :], AF.Exp, [("RB",)], [("RB",)], scale=-1.0)
                self.tt("dve", self.OT[rows, oc, qb * 512:(qb + 1) * 512], self.OACC[ab][rows, :], rb[rows, :], ALU.mult,
                        [("acc", ab), ("RB",)], [("OT", oc, 4 * qb + j) for j in range(4)])

        LG = self.opt.get("l0_lag", 3)
        for i in range(n + LG):
            if i < n:
                stage_a(items[i])
            if i >= LG:
                stage_c(items[i - LG])
            self.bg_pop(bg, i, n)
        self.bg_flush(bg)

    def attn_sb_pair(self, st, oc, bg=None):
        QZ = self.SETS[st]["QZ"]
        KP = self.SETS[st]["KZ"][0]
        VPt = self.SETS[st]["VP"]
        VP = lambda kt: VPt[:, kt, :]
        items = []
        for qb in range(4):
            for hh in range(2):
                kts = list(reversed(range(4 * qb + 4)))
                for kt in kts:
                    items.append((qb, hh, kt, kt == kts[0], kt == kts[-1]))
        n = len(items)
        pend = {}

        def stage_a(it):
            qb, hh, kt, first, last = it
            j0 = kt - 4 * qb
            col0 = max(0, j0) * 128
            Z, zreg = self.psA(z=True)
            qc = slice(qb * 512 + col0, (qb + 1) * 512)
            self.pe([(Z[:, col0:512], KP[:, kt * 128:(kt + 1) * 128], QZ[hh][:, qc], True, True)],
                    [("KZ", st, 0, kt // 4), ("QZ", st, hh, qb)], [zreg])
            eb = self.nxt("E", 2)
            E = self.EX[eb]
            self.act(E[:, col0:512], Z[:, col0:512], AF.Exp, [zreg], [("E", eb)])
            pend[it] = (Z, zreg, eb, col0, j0)

        def stage_a2(it):
            Z, zreg, eb, col0, j0 = pend[it]
            E = self.EX[eb]
            sb = self.nxt("SP", 4)
            SP = self.SPT[sb]
            self.act(SP[:, col0:512], E[:, col0:512], AF.Ln, [("E", eb)], [("SP", sb)], bias=1.0)
            if j0 >= 0:
                self.tt("dve", SP[:, col0:col0 + 128], SP[:, col0:col0 + 128], self.TRIS, ALU.mult, [("SP", sb)], [("SP", sb)])
            pend[it] = (Z, zreg, sb, col0, j0)

        def stage_c(it):
            qb, hh, kt, first, last = it
            Z, zreg, sb, col0, j0 = pend[it]
            SP = self.SPT[sb]
            if first:
                self.memset("dve", self.SL[:, :], 0.0, [("SL",)])
            mms = [(Z[:, col0:512], self.NTRI, SP[:, col0:512], False, first, "nochk")]
            rd = [("SP", sb)]
            if not first:
                mms.append((Z[:, col0:512], self.NONES, self.SL[:, col0:512], False, True, "nochk"))
                rd.append(("SL",))
            self.pe(mms, rd, [zreg])
            if not last:
                self.tt("dve", self.SL[:, col0:512], self.SL[:, col0:512], SP[:, col0:512], ALU.add, [("SL",), ("SP", sb)], [("SL",)])
            pb = self.nxt("PT", 7)
            PT = self.PT[pb]
            self.act(PT[:, col0:512], Z[:, col0:512], AF.Exp, [zreg], [("PT", pb)])
            if j0 >= 0:
                self.tt("dve", PT[:, col0:col0 + 128], PT[:, col0:col0 + 128], self.TRIS, ALU.mult, [("PT", pb)], [("PT", pb)])
            pend[it] = (pb, col0)

        def stage_e(it):
            qb, hh, kt, first, last = it
            pb, col0 = pend.pop(it)
            PT = self.PT[pb]
            mms = []
            if first:
                self.accb = self.nxt("accb", 2)
                mms.append((self.OACC[self.accb][:, :], self.ZEROS, QZ[hh][:, qb * 512:(qb + 1) * 512], True, False))
            ab = self.accb
            mms.append((self.OACC[ab][:, col0:512], VP(kt), PT[:, col0:512], False, last))
            self.pe(mms, [("PT", pb), ("V", st, kt), ("QZ", st, hh, qb)], [("acc", ab)])
            if last:
                rows = slice(64 * hh, 64 * hh + 64)
                self.cp("dve", self.OT[rows, oc, qb * 512:(qb + 1) * 512], self.OACC[ab][rows, :],
                        [("acc", ab)], [("OT", oc, 4 * qb + j) for j in range(4)])

        LC = self.opt.get("l1_lc", 2)
        LE = LC + 1
        self.psa_wide = True
        for i in range(n + LE):
            if i < n:
                stage_a(items[i])
            if LC <= i < n + LC:
                stage_c(items[i - LC])
            if i < n:
                stage_a2(items[i])
            if i >= LE:
                stage_e(items[i - LE])
            self.bg_pop(bg, i, n)
        self.bg_flush(bg)
        self.psa_wide = False

    def proj_fm(self, W, wsl, blk, wreg, dsts, scale, eng):
        ps, preg = self.psA()
        tsl = slice(blk * 512, (blk + 1) * 512)
        self.pe([(ps[:, :], W[:, c, wsl], self.HT[:, c, tsl], c == 0, c == 7) for c in range(8)],
                [wreg] + [("HT", 4 * blk + j) for j in range(4)], [preg])
        for dst, rows, dreg in dsts:
            if eng == "act":
                self.S.op("act", lambda e, dst=dst, rows=rows: e.activation(out=dst, in_=ps[rows, :], func=AF.Copy, scale=scale), [preg], [dreg])
            else:
                self.ts(eng, dst, ps[rows, :], scale, None, ALU.mult, None, [preg], [dreg])

    def v_proj(self, W, wsl, n, i, wreg, dst, src, sreg, vreg):
        ps, preg = self.psA()
        self.pe([(ps[:, 0:n], src[:, c, i * 128:(i + 1) * 128], W[:, c, wsl], c == 0, c == 7) for c in range(8)],
                [wreg, sreg], [preg])
        self.cp("dve", dst, ps[:, 0:n], [preg], [vreg])

    def pair_wload(self, layer, kind, p, widx, part="all"):
        W = self.WP[widx]
        wreg = ("WP", widx)
        idx = p if kind == "main" else 6 + p
        ncol = 384 if kind == "main" else 128
        src = (self.d_wa if layer == 0 else self.d_wb)[idx]
        if part in ("all", "dma"):
            self.dma("pool", W[:, :, 0:ncol], src[:, :, 0:ncol], [], [wreg])
        if layer == 1 and part in ("all", "fold"):
            for c in range(8):
                self.ts("pool", W[:, c, 0:128], W[:, c, 0:128], self.GC[:, 1, c:c + 1], 1.0, ALU.mult, ALU.mult, [wreg], [wreg])
                if kind == "main":
                    self.ts("pool", W[:, c, 128:384], W[:, c, 128:384], self.GC[:, 0, c:c + 1], 1.0, ALU.mult, ALU.mult, [wreg], [wreg])

    def pair_closures(self, layer, kind, p, st, widx, nxt_spec):
        S_ = self.SETS[st]
        W = self.WP[widx]
        wreg = ("WP", widx)
        h0 = slice(0, 64)
        h1 = slice(64, 128)
        brow = [64, 0]
        allq = lambda hh: [("QZ", st, hh, k) for k in range(4)]
        allk = lambda hh: [("KZ", st, hh, k) for k in range(4)]
        cl = []
        if layer == 0 and kind == "main":
            def bias():
                for hh in range(2):
                    h = 2 * p + hh
                    r = brow[hh]
                    self.dma("sp", S_["QZ"][hh][r:r + 1, :], self.CH[h:h + 1, 0, :], [("CH", 0)], allq(hh))
                    self.dma("sp", S_["QZ"][hh][r + 1:r + 2, :], self.CH[h:h + 1, 1, :], [("CH", 1)], allq(hh))
                    self.dma("sp", S_["KZ"][hh][r + 2:r + 3, :], self.CH[h:h + 1, 0, :], [("CH", 0)], allk(hh))
                    self.dma("sp", S_["KZ"][hh][r + 3:r + 4, :], self.CH[h:h + 1, 1, :], [("CH", 1)], allk(hh))
            cl.append(bias)
        if layer == 0 and kind == "mem":
            def clr():
                for hh in range(2):
                    r = brow[hh]
                    self.memset("pool", S_["QZ"][hh][r:r + 4, :], 0.0, allq(hh))
            cl.append(clr)
        qeng = "act" if layer == 0 else "dve"
        for blk in range(4):
            def qp(blk=blk):
                tsl = slice(blk * 512, (blk + 1) * 512)
                self.proj_fm(W, slice(0, 128), blk, wreg,
                             [(S_["QZ"][0][h0, tsl], h0, ("QZ", st, 0, blk)), (S_["QZ"][1][h1, tsl], h1, ("QZ", st, 1, blk))], 0.125, qeng)
            cl.append(qp)
        if kind == "main":
            for blk in range(4):
                def kp(blk=blk):
                    tsl = slice(blk * 512, (blk + 1) * 512)
                    if layer == 0:
                        dsts = [(S_["KZ"][0][h0, tsl], h0, ("KZ", st, 0, blk)), (S_["KZ"][1][h1, tsl], h1, ("KZ", st, 1, blk))]
                    else:
                        dsts = [(S_["KZ"][0][:, tsl], slice(0, 128), ("KZ", st, 0, blk))]
                    self.proj_fm(W, slice(128, 256), blk, wreg, dsts, 1.0, "dve")
                cl.append(kp)
            for i in range(NT):
                cl.append(lambda i=i: self.v_proj(W, slice(256, 384), 128, i, wreg, S_["VP"][:, i, :], self.HT, ("HT", i), ("V", st, i)))
        if nxt_spec is not None:
            cl.insert(0, lambda: self.pair_wload(layer, *nxt_spec, part="dma"))
            if layer == 1:
                cl.append(lambda: self.pair_wload(layer, *nxt_spec, part="fold"))
        return cl

    def run_pairs(self, layer, preloaded=False):
        nmain = 6 if layer == 0 else self.opt.get("l1_pairs", 6)
        plist = [("main", p) for p in range(nmain)]
        if layer == 0 or self.opt.get("l1_mem", True):
            plist += [("mem", p) for p in range(2)]
        spec = lambda idx: (plist[idx] + (idx % 2,)) if idx < len(plist) else None
        if not preloaded:
            self.pair_wload(layer, *spec(0))
        for c in self.pair_closures(layer, plist[0][0], plist[0][1], 0, 0, spec(1)):
            c()
        for idx, (kind, p) in enumerate(plist):
            st = idx % 2
            bgl = []
            if idx + 1 < len(plist):
                bgl = self.pair_closures(layer, plist[idx + 1][0], plist[idx + 1][1], 1 - st, (idx + 1) % 2, spec(idx + 2))
            if idx == len(plist) - 1 and st == 1 and self.opt.get("wo_prefetch", True):
                set0 = [("QZ", 0, hh, k) for hh in range(2) for k in range(4)] + [("KZ", 0, hh, k) for hh in range(2) for k in range(4)] \
                    + [("V", 0, i) for i in range(NT)]
                bgl = [lambda: self.dma("pool", self.WO, self.d_wout[layer].rearrange("(c p) n -> p c n", p=128), [], [("WO",)] + set0)]
                self.wo_loaded = layer
            bg = {"list": bgl, "done": 0}
            if kind == "main":
                if layer == 0:
                    VPt = self.SETS[st]["VP"]
                    self.attn_softmax_pair(st, None, lambda kt, VPt=VPt: VPt[:, kt, :], None, True, p,
                                           lambda hh, kt, st=st: ("KZ", st, hh, kt // 4), lambda kt, st=st: ("V", st, kt), bg)
                elif self.opt.get("l1_attn", True):
                    self.attn_sb_pair(st, p, bg)
                else:
                    self.bg_flush(bg)
            else:
                MK = self.MKT[:, p, :]
                self.attn_softmax_pair(st, [MK, MK], lambda kt, p=p: self.MVP[:, kt, 128 * p:128 * p + 128], 2, False, 6 + p,
                                       lambda hh, kt, p=p: ("MKT", p), lambda kt: ("MV", kt), bg)

    def mem_branch(self, s, l):
        self.load_gb(1 + l)
        self.wload(self.WM, self.d_wmemkv[l], ("WM",))
        for mt in range(2):
            self.dma("sp", self.MTMP[:], self.d_mem[s, mt * 128:(mt + 1) * 128, :], [], [("mtmp",)])
            self.norm_tile(self.MTMP[:], ("mtmp",), self.GB[:], self.HMT[:, :, mt * 128:(mt + 1) * 128], ("HMT", mt), "act")
        self.norm_flush()
        for p in range(2):
            ps, preg = self.psA()
            self.pe([(ps[:, 0:256], self.WM[:, c, 128 * p:128 * p + 128], self.HMT[:, c, :], c == 0, c == 7) for c in range(8)],
                    [("WM",), ("HMT", 0), ("HMT", 1)], [preg])
            self.cp("act", self.MKT[:, p, :], ps[:, 0:256], [preg], [("MKT", p)])
        for mt in range(2):
            self.v_proj(self.WM, slice(256, 512), 256, mt, ("WM",), self.MVP[:, mt, :], self.HMT, ("HMT", mt), ("MV", mt))

    def out_proj(self, l, post=None, prefetch=False):
        if self.wo_loaded != l:
            self.wload(self.WO, self.d_wout[l], ("WO",))
        self.wo_loaded = None
        if prefetch:
            self.ffn_prefetch(l)
        for i in range(NT):
            a = self.nxt("acc2", 2)
            ps = self.ACC2[a]
            regs = [("acc", 2 * a), ("acc", 2 * a + 1)]
            mms = []
            for h in range(2):
                for c in range(8):
                    mms.append((ps[:, 512 * h:512 * h + 512], self.OT[:, c, i * 128:(i + 1) * 128], self.WO[:, c, 512 * h:512 * h + 512], c == 0, c == 7))
            self.pe(mms, [("WO",)] + [("OT", c, i) for c in range(8)], regs)
            self.tt("dve", self.X[:, i, :], ps[:, :], self.X[:, i, :], ALU.add, regs + [("X", i)], [("X", i)])
            if post is not None:
                post(i)
        self.norm_flush()

    def norm_x_tile(self, i, gb):
        self.norm_tile(self.X[:, i, :], ("X", i), gb, self.HT[:, :, i * 128:(i + 1) * 128], ("HT", i), "act" if i % 2 else "dve")

    def ffn_prefetch(self, l):
        wup = self.d_wup[l]
        nf = 128 * FCH[0]
        self.wload(self.WUP[0][:, :, 0, 0:nf], wup[:, 0:nf], ("WUP", 0, 0))
        self.wload(self.WUP[0][:, :, 1, 0:nf], wup[:, DFF:DFF + nf], ("WUP", 0, 1))
        self.wload(self.WDN[0][:, 0:FCH[0], :], self.d_wdown[l][0:nf, :], ("WDN", 0))
        self.ffn_preloaded = l

    def ffn(self, l, post=None):
        wup = self.d_wup[l]
        wdn = self.d_wdown[l]
        nch = len(FCH)

        def load_up(fi):
            b = fi % 2
            f0 = 128 * sum(FCH[:fi])
            nf = 128 * FCH[fi]
            self.wload(self.WUP[b][:, :, 0, 0:nf], wup[:, f0:f0 + nf], ("WUP", b, 0))
            self.wload(self.WUP[b][:, :, 1, 0:nf], wup[:, DFF + f0:DFF + f0 + nf], ("WUP", b, 1))

        def load_dn(fi):
            b = fi % 2
            f0 = 128 * sum(FCH[:fi])
            nf = 128 * FCH[fi]
            self.wload(self.WDN[b][:, 0:FCH[fi], :], wdn[f0:f0 + nf, :], ("WDN", b))

        def down_tile(fi, i, final):
            b = fi % 2
            a = self.nxt("acc2", 2)
            ps = self.ACC2[a]
            regs = [("acc", 2 * a), ("acc", 2 * a + 1)]
            mms = []
            for h in range(2):
                for k in range(FCH[fi]):
                    mms.append((ps[:, 512 * h:512 * h + 512], self.AT[b][:, k, i * 128:(i + 1) * 128],
                                self.WDN[b][:, k, 512 * h:512 * h + 512], k == 0, k == FCH[fi] - 1))
            self.pe(mms, [("WDN", b)] + [("AT", b, k, i // 4) for k in range(FCH[fi])], regs)
            self.tt("dve", self.X[:, i, :], ps[:, :], self.X[:, i, :], ALU.add, regs + [("X", i)], [("X", i)])
            if final and post is not None:
                post(i)

        def step(fi, k, tb):
            b = fi % 2
            cbase = sum(FCH[:fi])
            tsl = slice(tb * 512, (tb + 1) * 512)
            gT = None
            for gv in range(2):
                ci = cbase + k + (DFF // 128) * gv
                ps, preg = self.psA()
                self.pe([(ps[:, :], self.WUP[b][:, c, gv, 128 * k:128 * k + 128], self.HT[:, c, tsl], c == 0, c == 7) for c in range(8)],
                        [("WUP", b, gv)] + [("HT", 4 * tb + j) for j in range(4)], [preg])
                UB = self.UBS[gv][tb % 2]
                ureg = ("UB", gv, tb % 2)
                if tb == 0:
                    self.memset("pool", UB[:, 0:2], 0.0, [ureg])
                else:
                    self.cp("pool", UB[:, 0:2], self.UBS[gv][(tb + 1) % 2][:, 512:514], [("UB", gv, (tb + 1) % 2)], [ureg])
                self.cp("act", UB[:, 2:514], ps[:, :], [preg], [ureg])
                tbuf = self.nxt("T%d" % gv, 2)
                T = (self.TG if gv == 0 else self.TV)[tbuf]
                treg = ("T", gv, tbuf)
                self.act(T[:, :], ps[:, :], AF.Identity, [preg, ("cw",), ("cb",)], [treg],
                         scale=self.CW[:, ci, 2:3], bias=self.CB[:, ci:ci + 1])
                self.stt(T[:, :], UB[:, 1:513], self.CW[:, ci, 1:2], T[:, :], ALU.mult, ALU.add, [ureg, treg, ("cw",)], [treg])
                self.stt(T[:, :], UB[:, 0:512], self.CW[:, ci, 0:1], T[:, :], ALU.mult, ALU.add, [ureg, treg, ("cw",)], [treg])
                if gv == 0:
                    gT = (T, treg)
                else:
                    self.act(gT[0][:, :], gT[0][:, :], AF.Silu, [gT[1]], [gT[1]])
                    self.tt("dve", self.AT[b][:, k, tsl], gT[0][:, :], T[:, :], ALU.mult, [gT[1], treg], [("AT", b, k, tb)])

        if self.ffn_preloaded != l:
            load_up(0)
            load_dn(0)
        self.ffn_preloaded = None
        pending = []
        for fi in range(nch):
            if fi + 1 < nch:
                load_up(fi + 1)
            steps = [(k, tb) for k in range(FCH[fi]) for tb in range(4)]
            for si, (k, tb) in enumerate(steps):
                step(fi, k, tb)
                left = len(steps) - si
                ne = (len(pending) + left - 1) // left
                for _ in range(ne):
                    pending.pop(0)()
            assert not pending
            if fi + 1 < nch:
                load_dn(fi + 1)
            pending = [(lambda fi=fi, i=i: down_tile(fi, i, fi == nch - 1)) for i in range(NT)]
        for cl in pending:
            cl()
        self.norm_flush()

    def layer0_mixer(self, s):
        self.dma("pool", self.WP[1][:, :, 0:12], self.d_wf, [], [("WP", 1)])
        self.pair_wload(0, "main", 0, 0)
        self.mem_branch(s, 0)
        self.load_gb(0)
        brow = [64, 0]

        def init_set(st):
            S_ = self.SETS[st]
            for hh in range(2):
                r = brow[hh]
                self.memset("dve", S_["QZ"][hh][:, :], 0.0, [("QZ", st, hh, k) for k in range(4)])
                self.memset("dve", S_["KZ"][hh][:, :], 0.0, [("KZ", st, hh, k) for k in range(4)])
                self.memset("dve", S_["QZ"][hh][r:r + 4, :], -1.0, [("QZ", st, hh, k) for k in range(4)])
                self.memset("dve", S_["KZ"][hh][r:r + 4, :], 1.0, [("KZ", st, hh, k) for k in range(4)])

        init_set(0)
        cf_alias = [("QZ", 1, hh, k) for hh in range(2) for k in range(4)]
        for i in range(NT):
            self.norm_x_tile(i, self.GB[:])
        self.norm_flush()
        WF = self.WP[1][:, :, 0:12]
        cfo_alias = [("WP", 1), ("PT", 0), ("PT", 1)]
        for blk in range(4):
            ps, preg = self.psA()
            tsl = slice(blk * 512, (blk + 1) * 512)
            self.pe([(ps[0:12, :], WF[:, c, :], self.HT[:, c, tsl], c == 0, c == 7) for c in range(8)],
                    [("WP", 1)] + [("HT", 4 * blk + j) for j in range(4)], [preg])
            self.act(self.CF[0:12, tsl], ps[0:12, :], AF.Exp, [preg], [("CF", blk)] + cf_alias, scale=-1.0, bias=self.NB[:, 0:1])
            self.act(self.CF[0:12, tsl], self.CF[0:12, tsl], AF.Ln, [("CF", blk)], [("CF", blk)], bias=1.0)
            self.ts("dve", self.CF[0:12, tsl], self.CF[0:12, tsl], -0.5, None, ALU.mult, None, [("CF", blk)], [("CF", blk)])
        self.S.op("dve", lambda e: e.tensor_tensor_scan(out=self.CFO[0:12, :], data0=self.CF[0:12, :], data1=self.CF[0:12, :],
                                                         initial=0.0, op0=ALU.add, op1=ALU.add),
                  [("CF", k) for k in range(4)] + cf_alias, [("CFO",)] + cfo_alias)
        self.cp("dve", self.CH[0:12, 0, :], self.CFO[0:12, :], [("CFO",)] + cfo_alias, [("CH", 0)])
        self.tt("dve", self.CH[0:12, 1, :], self.CFO[0:12, :], self.CH[0:12, 0, :], ALU.subtract, [("CFO",), ("CH", 0)] + cfo_alias, [("CH", 1)])
        if "cf" in self.d_dbg:
            self.dma("sp", self.d_dbg["cf"], self.CFO[0:12, :], [("CFO",)] + cfo_alias, [])
        init_set(1)
        self.run_pairs(0, preloaded=True)

    def layer1_mixer(self, s):
        for st in range(2):
            for hh in range(2):
                self.memset("dve", self.SETS[st]["QZ"][hh][:, :], 0.0, [("QZ", st, hh, k) for k in range(4)])
        if self.opt.get("l1_pairs", 6) > 0:
            self.pair_wload(1, "main", 0, 0)
        self.mem_branch(s, 1)
        if not self.prenormed:
            for i in range(NT):
                self.norm_x_tile(i, None)
            self.norm_flush()
        self.run_pairs(1, preloaded=self.opt.get("l1_pairs", 6) > 0)

    def dump_x(self, name):
        if name in self.d_dbg:
            for i in range(NT):
                self.dma("sp", self.d_dbg[name][i * 128:(i + 1) * 128, :], self.X[:, i, :], [("X", i)], [])

    def final_tile(self, s, i, normed):
        ob = self.nxt("outt", 2)
        O = self.OUTT[ob]
        if normed:
            b = self.nxt("hn", 2)
            ss = self.STAT[:, 2 * b:2 * b + 1]
            rs = self.STAT[:, 2 * b + 1:2 * b + 2]
            xin = self.X[:, i, :]
            self.act(self.HN[b][:], xin, AF.Square, [("X", i)], [("hn", b), ("PT", 3 + 2 * b), ("PT", 4 + 2 * b), ("ss", b)], scale=1.0 / 32.0, accum_out=ss)
            self.act(ss, ss, AF.Sqrt, [("ss", b)], [("ss", b)], bias=self.EPSC[:, 0:1])
            self.S.op("dve", lambda e, rs=rs, ss=ss: e.reciprocal(out=rs, in_=ss), [("ss", b)], [("rs", b)])
            self.stt(O[:], xin, rs, self.GB[:], ALU.mult, ALU.mult, [("X", i), ("rs", b), ("gb",)], [("outt", ob)])
            self.dma("sp", self.d_out[s, i * 128:(i + 1) * 128, :], O[:], [("outt", ob)], [])
            if s + 1 < self.nseq:
                self.dma("sp", self.X[:, i, :], self.d_x[s + 1, i * 128:(i + 1) * 128, :], [], [("X", i)])
                self.x_prefetched = s + 1
        else:
            self.dma("sp", self.d_out[s, i * 128:(i + 1) * 128, :], self.X[:, i, :], [("X", i)], [])

    def program(self):
        order = ["l0mix", "l0out", "l0ffn", "l1mix", "l1out", "l1ffn"]
        stop = self.stop
        nph = len(order) if stop is None else order.index(stop) + 1
        self.dma("pool", self.CONST[:], self.d_consts, [], [("const",)])
        self.dma("sp", self.GC[:], self.d_gc, [], [("gc",)])
        self.dma("sp", self.NB[:], self.d_bf, [], [("nb",)])
        self.memset("dve", self.EPSC[:], EPS, [("eps",)])
        self.ts("dve", self.NB[:], self.NB[:], -1.0, None, ALU.mult, None, [("nb",)], [("nb",)])
        self.S.fence()
        for s in range(self.nseq):
            if self.x_prefetched != s:
                for i in range(NT):
                    self.dma("sp", self.X[:, i, :], self.d_x[s, i * 128:(i + 1) * 128, :], [], [("X", i)])
            phs = order[:nph] if self.phases is None else self.phases
            full = self.phases is None and stop is None
            self.prenormed = False
            for ph in phs:
                if ph == "l0mix":
                    self.layer0_mixer(s)
                elif ph == "l1mix":
                    self.layer1_mixer(s)
                elif ph in ("l0out", "l1out"):
                    l = int(ph[1])
                    self.load_gb(3 + l)
                    self.dma("sp", self.CW[:], self.d_cw[l], [], [("cw",)])
                    self.dma("sp", self.CB[:], self.d_cb[l], [], [("cb",)])
                    self.out_proj(l, lambda i: self.norm_x_tile(i, self.GB[:]), prefetch=("l%dffn" % l) in phs)
                    if s == 0:
                        self.dump_x("x_l%dmix" % l)
                elif ph == "l0ffn":
                    self.ffn(0, (lambda i: self.norm_x_tile(i, None)) if full else None)
                    self.prenormed = full
                    if s == 0:
                        self.dump_x("x_l0")
                elif ph == "l1ffn":
                    if full:
                        self.load_gb(5)
                    self.ffn(1, (lambda i, s=s: self.final_tile(s, i, True)) if full else None)
                self.S.fence()
            if not full:
                for i in range(NT):
                    self.final_tile(s, i, False)
            self.S.fence()

    def build(self):
        nc = self.nc
        nseq = self.nseq
        dt_in = lambda name, shape: nc.dram_tensor(name, shape, F32, kind="ExternalInput").ap()
        self.d_x = dt_in("x", [nseq, SEQ, D])
        self.d_mem = dt_in("mem", [nseq, NMEM, D])
        self.d_wa = dt_in("wa", [8, 128, 8, 384])
        self.d_wb = dt_in("wb", [8, 128, 8, 384])
        self.d_wf = dt_in("wf", [128, 8, 12])
        self.d_wmemkv = dt_in("w_memkv", [2, D, 512])
        self.d_wout = dt_in("w_out", [2, D, D])
        self.d_wup = dt_in("w_up", [2, D, 2 * DFF])
        self.d_wdown = dt_in("w_down", [2, DFF, D])
        self.d_gb = dt_in("gb", [6, 128, D])
        self.d_gc = dt_in("gc", [128, 2, 8])
        self.d_bf = dt_in("bf", [12, 1])
        self.d_cw = dt_in("cw", [2, 128, 44, 3])
        self.d_cb = dt_in("cb", [2, 128, 44])
        self.d_consts = dt_in("consts", [128, 7, 128])
        self.d_out = nc.dram_tensor("out", [nseq, SEQ, D], F32, kind="ExternalOutput").ap()
        self.d_dbg = {}
        for name, shape in self.dbg_shapes().items():
            if name in self.dbg:
                self.d_dbg[name] = nc.dram_tensor("dbg_" + name, shape, F32, kind="ExternalOutput").ap()

        with ExitStack() as es:
            sb = lambda name, shape, dt: es.enter_context(nc.sbuf_tensor(name, shape, dt))
            self.X = sb("X", [128, NT, D], F32)
            self.HT = sb("HT", [128, 8, SEQ], BF16)
            self.CONST = sb("CONST", [128, 7, 128], BF16)
            self.IDENT = self.CONST[:, 0, :]
            self.TRI = self.CONST[:, 1, :]
            self.TRIS = self.CONST[:, 2, :]
            self.NTRI = self.CONST[:, 3, :]
            self.NONES = self.CONST[:, 4, :]
            self.ZEROS = self.CONST[:, 5, :]
            self.ONES = self.CONST[:, 6, :]
            self.GB = sb("GB", [128, D], F32)
            self.GC = sb("GC", [128, 2, 8], F32)
            self.NB = sb("NB", [12, 1], F32)
            self.CW = sb("CW", [128, 44, 3], F32)
            self.CB = sb("CB", [128, 44], F32)
            self.STAT = sb("STAT", [128, 16], F32)
            self.EPSC = sb("EPSC", [128, 1], F32)
            self.HN = [sb("HN%d" % i, [128, D], BF16) for i in range(2)]
            self.ARENA_B = 103000
            self.ARENA = sb("ARENA", [128, self.ARENA_B // 2], BF16)
            self.PSA = [es.enter_context(nc.psum_tensor("psA%d" % i, [128, 512], F32)) for i in range(3)]
            self.ACC2 = [es.enter_context(nc.psum_tensor("acc2_%d" % i, [128, 1024], F32)) for i in range(2)]
            self.OACC = [self.ACC2[0][:, 0:512], self.ACC2[0][:, 512:1024]]
            self.LACC = [self.ACC2[1][:, 0:512], self.ACC2[1][:, 512:1024]]
            self.PST = es.enter_context(nc.psum_tensor("psT", [128, 8, 128], BF16))
            self.PSTF = self.PST[:, :, :].rearrange("p a b -> p (a b)").bitcast(F32)
            esem = {e: es.enter_context(nc.semaphore("sem_" + e)) for e in CENGS}
            dsem = [es.enter_context(nc.semaphore("dsem%d" % i)) for i in range(24)]
            self.S = Sched(esem, dsem)
            self.carve_all()
            self.program()
            self.emit()
        return nc

    def dbg_shapes(self):
        return {"x_l0mix": [SEQ, D], "x_l0": [SEQ, D], "x_l1mix": [SEQ, D], "cf": [12, SEQ]}

    def carve(self, off, shape, dt):
        n = int(np.prod(shape[1:]))
        esz = 2 if dt == BF16 else 4
        assert off % 4 == 0, off
        a = self.ARENA[:, off // 2:off // 2 + n * esz // 2]
        if dt == F32:
            a = a.bitcast(F32)
        if len(shape) == 3:
            a = a.rearrange("p (a b) -> p a b", a=shape[1])
        elif len(shape) == 4:
            a = a.rearrange("p (a b c) -> p a b c", a=shape[1], b=shape[2])
        self._off = off + n * esz
        assert self._off <= self.ARENA_B, self._off
        return a

    def carve_all(self):
        self.OT = self.carve(0, [128, 8, SEQ], BF16)
        o = self._off
        self.MTMP = self.carve(0, [128, D], F32)
        self.WM = self.carve(self._off, [128, 8, 512], BF16)
        self.HMT = self.carve(self._off, [128, 8, NMEM], BF16)
        self.SETS = []
        for st in range(2):
            d = {"QZ": [], "KZ": []}
            for i in range(2):
                d["QZ"].append(self.carve(o, [128, SEQ], BF16)); o = self._off
            for i in range(2):
                d["KZ"].append(self.carve(o, [128, SEQ], BF16)); o = self._off
            d["VP"] = self.carve(o, [128, NT, 128], BF16); o = self._off
            self.SETS.append(d)
        self.WP = []
        for i in range(2):
            self.WP.append(self.carve(o, [128, 8, 384], BF16)); o = self._off
        self.PT = []
        for i in range(3):
            self.PT.append(self.carve(o, [128, 512], BF16)); o = self._off
        for b in range(2):
            self.PT.append(self.HN[b][:, 0:512])
            self.PT.append(self.HN[b][:, 512:1024])
        self.MKT = self.carve(o, [128, 2, NMEM], BF16); o = self._off
        self.MVP = self.carve(o, [128, 2, 256], BF16); o = self._off
        self.RB = self.carve(o, [128, 512], F32); o = self._off
        o0 = o
        self.CH = self.carve(o, [128, 2, SEQ], BF16); o = self._off
        o = o0
        self.EX = []
        for i in range(2):
            self.EX.append(self.carve(o, [128, 512], F32)); o = self._off
        self.SPT = []
        for i in range(4):
            self.SPT.append(self.carve(o, [128, 512], BF16)); o = self._off
        self.SL = self.carve(o, [128, 512], BF16); o = self._off
        self.CF = self.SETS[1]["QZ"][0].bitcast(F32)[:, 0:1024] if False else self.carve(53248, [128, SEQ], F32)
        self.CFO = self.carve(79872, [128, SEQ], F32)
        self.WO = self.carve(32768, [128, 8, D], BF16)
        top = self.ARENA_B - 24576
        self.WUP = [self.carve(top, [128, 8, 2, 512], BF16), self.carve(0, [128, 8, 2, 512], BF16)]
        self.WDN = [self.carve(top + 16384, [128, 4, D], BF16), self.carve(16384, [128, 4, D], BF16)]
        o = 24576
        self.AT = []
        for i in range(2):
            self.AT.append(self.carve(o, [128, 4, SEQ], BF16)); o = self._off
        self.UBS = [[None, None], [None, None]]
        for g in range(2):
            for i in range(2):
                self.UBS[g][i] = self.carve(o, [128, 516], F32); o = self._off
        self.TG = []
        self.TV = []
        for i in range(2):
            self.TG.append(self.carve(o, [128, 512], F32)); o = self._off
            self.TV.append(self.carve(o, [128, 512], F32)); o = self._off
        assert o <= top, (o, top)
        self.OUTT = [self.carve(24576, [128, D], F32), self.carve(24576 + 4096, [128, D], F32)]

    def emit(self):
        nc = self.nc
        S = self.S
        with nc.Block() as block:
            @block.tensor
            def _(e):
                for f in S.prog["pe"]:
                    f(e)

            @block.scalar
            def _(e):
                for f in S.prog["act"]:
                    f(e)

            @block.vector
            def _(e):
                for f in S.prog["dve"]:
                    f(e)

            @block.gpsimd
            def _(e):
                for f in S.prog["pool"]:
                    f(e)

            @block.sync
            def _(e):
                for f in S.prog["sp"]:
                    f(e)


def host_inputs(x, mem, ln_mix_g, w_in_a, b_f_a, w_in_b, ln_kv_g, w_kv, ln_mem_g, w_memkv, w_out, ln_ffn_g,
                w_up, conv_w, conv_b, w_down, final_g):
    f = lambda a: np.ascontiguousarray(np.asarray(a, dtype=np.float32))
    rep = lambda g: np.broadcast_to(np.asarray(g, np.float32)[None, :], (128, D))
    gb = f(np.stack([rep(ln_mix_g[0]), rep(ln_mem_g[0]), rep(ln_mem_g[1]), rep(ln_ffn_g[0]), rep(ln_ffn_g[1]), rep(final_g)]))
    col = lambda g: np.asarray(g, np.float32).reshape(8, 128).T
    gc = f(np.stack([col(ln_kv_g), col(ln_mix_g[1])], axis=1))
    cw = f(np.asarray(conv_w, np.float32).reshape(2, 3, 44, 128).transpose(0, 3, 2, 1))
    cb = f(np.asarray(conv_b, np.float32).reshape(2, 44, 128).transpose(0, 2, 1))
    i = np.arange(128)
    ident = (i[:, None] == i[None, :]).astype(np.float32)
    tri = (i[:, None] <= i[None, :]).astype(np.float32)
    tris = (i[:, None] < i[None, :]).astype(np.float32)
    ntri = -(i[:, None] >= i[None, :]).astype(np.float32)
    nones = -np.ones((128, 128), np.float32)
    consts = f(np.stack([ident, tri, tris, ntri, nones, 0.0 * nones, -nones], axis=1))
    pc = lambda w: np.asarray(w, np.float32).reshape(8, 128, -1).transpose(1, 0, 2)
    wia = np.asarray(w_in_a[0], np.float32)
    wib = np.asarray(w_in_b[0], np.float32)
    wkv = np.asarray(w_kv, np.float32)
    wa = np.zeros((8, 128, 8, 384), np.float32)
    wb = np.zeros((8, 128, 8, 384), np.float32)
    for p in range(6):
        for k in range(3):
            wa[p, :, :, 128 * k:128 * k + 128] = pc(wia[:, 768 * k + 128 * p:768 * k + 128 * p + 128])
        wb[p, :, :, 0:128] = pc(wib[:, 128 * p:128 * p + 128])
        wb[p, :, :, 128:256] = pc(wkv[:, 128 * p:128 * p + 128])
        wb[p, :, :, 256:384] = pc(wkv[:, 768 + 128 * p:768 + 128 * p + 128])
    for m_ in range(2):
        wa[6 + m_, :, :, 0:128] = pc(wia[:, 2316 + 128 * m_:2316 + 128 * m_ + 128])
        wb[6 + m_, :, :, 0:128] = pc(wib[:, 768 + 128 * m_:768 + 128 * m_ + 128])
    wf = f(pc(wia[:, 2304:2316]))
    return {
        "wa": wa, "wb": wb, "wf": wf, "w_memkv": f(w_memkv), "w_out": f(w_out),
        "w_up": f(w_up), "w_down": f(w_down), "gb": gb, "gc": gc, "bf": f(np.asarray(b_f_a, np.float32).reshape(12, 1)),
        "cw": cw, "cb": cb, "consts": consts,
    }


def kernel(**inputs):
    x = np.asarray(inputs["x"], np.float32)
    mem = np.asarray(inputs["mem"], np.float32)
    shared = host_inputs(**inputs)
    nseq = x.shape[0] // NCORES
    nc = Builder(nseq=nseq).build()
    in_maps = []
    for c in range(NCORES):
        m = dict(shared)
        m["x"] = np.ascontiguousarray(x[c * nseq:(c + 1) * nseq])
        m["mem"] = np.ascontiguousarray(mem[c * nseq:(c + 1) * nseq])
        in_maps.append(m)
    res = run_bass_kernel_spmd(nc, in_maps, core_ids=list(range(NCORES)))
    return np.concatenate([np.asarray(r["out"], np.float32) for r in res.results], axis=0)
```

```python
import numpy as np
from contextlib import ExitStack
import concourse.bass as bass
import concourse.mybir as mybir
from concourse.bass_utils import run_bass_kernel_spmd

F32 = mybir.dt.float32
BF16 = mybir.dt.bfloat16
AF = mybir.ActivationFunctionType
ALU = mybir.AluOpType

D = 1024
SEQ = 2048
NT = SEQ // 128
NMEM = 256
DFF = 2816
A_IN = 2572
EPS = 1e-6
NCORES = 8
ENGS = ["pe", "act", "dve", "pool", "sp"]
CENGS = ["pe", "act", "dve", "pool"]
FCH = [4, 4, 4, 4, 4, 2]


class Sched:
    def __init__(self, esem, dsem):
        self.esem = esem
        self.dsem = dsem
        self.prog = {e: [] for e in ENGS}
        self.cnt = {e: 0 for e in CENGS}
        self.dcnt = [0] * len(dsem)
        self.dpool = {"sp": list(range(0, 16)), "pool": list(range(16, 24)), "act": list(range(16, 24))}
        self.dnext = {"sp": 0, "pool": 0, "act": 0}
        self.known = {e: {} for e in ENGS}
        self.state = {}

    def _sem(self, src):
        return self.esem[src] if isinstance(src, str) else self.dsem[src]

    def _need(self, eng, reads, writes):
        need = {}

        def add(w, kind):
            src, val = w
            if src == eng and (eng == "pe" or kind == "WAR"):
                return
            if need.get(src, 0) < val:
                need[src] = val

        for r in reads:
            st = self.state.get(r)
            if st is not None and st[0] is not None:
                add(st[0], "RAW")
        for w in writes:
            st = self.state.get(w)
            if st is not None:
                if st[0] is not None:
                    add(st[0], "WAW")
                for s_, v_ in st[1].items():
                    add((s_, v_), "WAR")
        out = []
        kn = self.known[eng]
        for src, val in need.items():
            if kn.get(src, 0) < val:
                kn[src] = val
                out.append((self._sem(src), val))
        return out

    def _update(self, me, reads, writes):
        src, val = me
        for r in reads:
            st = self.state.get(r)
            if st is None:
                st = [None, {}]
                self.state[r] = st
            st[1][src] = val
        for w in writes:
            self.state[w] = [me, {}]

    def op(self, eng, fn, reads=(), writes=()):
        wl = self._need(eng, reads, writes)
        self.cnt[eng] += 1
        sem = self.esem[eng]

        def emit(e, wl=wl, fn=fn, sem=sem):
            for sm, v in wl:
                e.wait_ge(sm, v)
            fn(e).then_inc(sem, 1)

        self.prog[eng].append(emit)
        self._update((eng, self.cnt[eng]), reads, writes)

    def dma(self, q, out, in_, reads=(), writes=()):
        pl = self.dpool[q]
        j = pl[self.dnext[q] % len(pl)]
        self.dnext[q] += 1
        wl = self._need(q, reads, writes)
        prev = 16 * self.dcnt[j]
        if prev > 0 and self.known[q].get(j, 0) < prev:
            self.known[q][j] = prev
            wl.append((self.dsem[j], prev))
        self.dcnt[j] += 1
        sem = self.dsem[j]

        def emit(e, wl=wl, out=out, in_=in_, sem=sem):
            for sm, v in wl:
                e.wait_ge(sm, v)
            e.dma_start(out=out, in_=in_).then_inc(sem, 16)

        self.prog[q].append(emit)
        self._update((j, 16 * self.dcnt[j]), reads, writes)

    def fence(self):
        srcs = [(e, self.cnt[e]) for e in CENGS] + [(j, 16 * c) for j, c in enumerate(self.dcnt)]
        for e in ENGS:
            wl = []
            for src, val in srcs:
                if src == e and e == "pe":
                    continue
                if val > self.known[e].get(src, 0):
                    self.known[e][src] = val
                    wl.append((self._sem(src), val))

            def emit(en, wl=wl):
                for sm, v in wl:
                    en.wait_ge(sm, v)

            self.prog[e].append(emit)
        self.state = {}


class Builder:
    def __init__(self, nseq=2, stop=None, dbg=(), phases=None, opt=None):
        self.phases = phases
        self.opt = opt or {}
        self.nseq = nseq
        self.stop = stop
        self.dbg = set(dbg)
        self.nc = bass.Bass("TRN2", target_bir_lowering=False)
        self.rr = {}
        self.norm_pending = []
        self.psa_wide = False
        self.ffn_preloaded = None
        self.wo_loaded = None
        self.x_prefetched = None

    def nxt(self, name, n):
        v = self.rr.get(name, 0)
        self.rr[name] = (v + 1) % n
        return v

    def pe(self, mms, reads, writes):
        def fn(e, mms=mms):
            ins = None
            for m in mms:
                if m[0] == "tr":
                    ins = e.transpose(out=m[1], in_=m[2], identity=self.IDENT)
                elif len(m) == 6:
                    out, lhsT, rhs, start, stop, _ = m
                    ins = e.matmul(out, lhsT, rhs, start=start, stop=stop, skip_group_check=True)
                else:
                    out, lhsT, rhs, start, stop = m
                    ins = e.matmul(out, lhsT, rhs, start=start, stop=stop)
            return ins

        self.S.op("pe", fn, reads, writes)

    def act(self, out, in_, func, reads, writes, **kw):
        self.S.op("act", lambda e: e.activation(out=out, in_=in_, func=func, **kw), reads, writes)

    def ts(self, eng, out, in0, s1, s2, op0, op1, reads, writes):
        if op1 is None:
            self.S.op(eng, lambda e: e.tensor_scalar(out=out, in0=in0, scalar1=s1, scalar2=s2, op0=op0), reads, writes)
        else:
            self.S.op(eng, lambda e: e.tensor_scalar(out=out, in0=in0, scalar1=s1, scalar2=s2, op0=op0, op1=op1), reads, writes)

    def tt(self, eng, out, in0, in1, op, reads, writes):
        self.S.op(eng, lambda e: e.tensor_tensor(out=out, in0=in0, in1=in1, op=op), reads, writes)

    def stt(self, out, in0, scalar, in1, op0, op1, reads, writes):
        self.S.op("dve", lambda e: e.scalar_tensor_tensor(out=out, in0=in0, scalar=scalar, in1=in1, op0=op0, op1=op1), reads, writes)

    def cp(self, eng, out, in_, reads, writes):
        if eng == "act":
            self.S.op("act", lambda e: e.activation(out=out, in_=in_, func=AF.Copy), reads, writes)
        else:
            self.S.op(eng, lambda e: e.tensor_copy(out=out, in_=in_), reads, writes)

    def memset(self, eng, ap, val, writes):
        self.S.op(eng, lambda e: e.memset(ap, val), (), writes)

    def dma(self, q, out, in_, reads, writes):
        self.S.dma(q, out, in_, reads, writes)

    def psA(self, z=False):
        if self.psa_wide:
            if z:
                k = self.nxt("psAz", 5)
                if k >= 3:
                    return self.LACC[k - 3], ("acc", 2 + k - 3)
                return self.PSA[k], ("psA", k)
            return self.PSTF, ("psT",)
        k = self.nxt("psA", 3)
        return self.PSA[k], ("psA", k)

    def norm_tile(self, xin, xreg, gb, dst, dstreg, evac_eng):
        b = self.nxt("hn", 2)
        ss = self.STAT[:, 2 * b:2 * b + 1]
        rs = self.STAT[:, 2 * b + 1:2 * b + 2]
        hn = self.HN[b]
        hreg = [("hn", b), ("PT", 3 + 2 * b), ("PT", 4 + 2 * b)]
        self.act(hn[:], xin, AF.Square, [xreg], hreg + [("ss", b)], scale=1.0 / 32.0, accum_out=ss)
        self.act(ss, ss, AF.Sqrt, [("ss", b)], [("ss", b)], bias=self.EPSC[:, 0:1])
        self.S.op("dve", lambda e: e.reciprocal(out=rs, in_=ss), [("ss", b)], [("rs", b)])
        if gb is None:
            self.ts("dve", hn[:], xin, rs, None, ALU.mult, None, [xreg, ("rs", b)], hreg)
        else:
            self.stt(hn[:], xin, rs, gb, ALU.mult, ALU.mult, [xreg, ("rs", b), ("gb",)], hreg)
        def part2():
            self.pe([("tr", self.PST[:, c, :], hn[:, c * 128:(c + 1) * 128]) for c in range(8)], hreg, [("psT",)])
            self.cp(evac_eng, dst, self.PST[:, :, :], [("psT",)], [dstreg])

        self.norm_flush()
        self.norm_pending.append(part2)

    def norm_flush(self):
        while self.norm_pending:
            self.norm_pending.pop(0)()

    def load_gb(self, idx):
        self.dma("sp", self.GB[:], self.d_gb[idx], [], [("gb",)])

    def wload(self, dst, src2d, reg, fold=None, part="all"):
        if part in ("all", "dma"):
            self.dma("pool", dst, src2d.rearrange("(c p) n -> p c n", p=128), [], [reg])
        if fold is not None and part in ("all", "fold"):
            for c in range(8):
                self.ts("pool", dst[:, c, :], dst[:, c, :], self.GC[:, fold, c:c + 1], 1.0, ALU.mult, ALU.mult, [reg], [reg])

    def bg_pop(self, bg, i, n):
        if not bg or not bg["list"]:
            return
        tot = len(bg["list"])
        den = max(1, int(0.8 * n))
        target = min(tot, -(-tot * (i + 1) // den))
        while bg["done"] < target:
            bg["list"][bg["done"]]()
            bg["done"] += 1

    def bg_flush(self, bg):
        if bg:
            while bg["done"] < len(bg["list"]):
                bg["list"][bg["done"]]()
                bg["done"] += 1

    def attn_softmax_pair(self, st, KZo, VP, nkt, causal, oc, kreg, vreg, bg=None):
        QZ = self.SETS[st]["QZ"]
        KZ = KZo if KZo is not None else self.SETS[st]["KZ"]
        items = []
        for qb in range(4):
            for hh in range(2):
                kts = list(range(4 * qb + 4)) if causal else list(range(nkt))
                for kt in kts:
                    items.append((qb, hh, kt, kt == kts[0], kt == kts[-1]))
        n = len(items)
        pend = {}

        def stage_a(it):
            qb, hh, kt, first, last = it
            j0 = kt - 4 * qb if causal else -1
            col0 = max(0, j0) * 128
            S_, sreg = self.psA()
            qc = slice(qb * 512 + col0, (qb + 1) * 512)
            self.pe([(S_[:, col0:512], KZ[hh][:, kt * 128:(kt + 1) * 128], QZ[hh][:, qc], True, True)],
                    [kreg(hh, kt), ("QZ", st, hh, qb)], [sreg])
            pb = self.nxt("PT", 7)
            PT = self.PT[pb]
            self.act(PT[:, col0:512], S_[:, col0:512], AF.Exp, [sreg], [("PT", pb)])
            if j0 >= 0:
                self.tt("dve", PT[:, col0:col0 + 128], PT[:, col0:col0 + 128], self.TRI, ALU.mult, [("PT", pb)], [("PT", pb)])
            pend[it] = (pb, col0)

        def stage_c(it):
            qb, hh, kt, first, last = it
            pb, col0 = pend.pop(it)
            PT = self.PT[pb]
            if first:
                self.accb = self.nxt("accb", 2)
            ab = self.accb
            self.pe([(self.OACC[ab][:, col0:512], VP(kt), PT[:, col0:512], first, last),
                     (self.LACC[ab][:, col0:512], self.ONES, PT[:, col0:512], first, last)],
                    [("PT", pb), vreg(kt)], [("acc", ab), ("acc", 2 + ab)])
            if last:
                rows = slice(64 * hh, 64 * hh + 64)
                rb = self.RB
                self.act(rb[rows, :], self.LACC[ab][rows, :], AF.Ln, [("acc", 2 + ab)], [("RB",)])
                self.act(rb[rows, :], rb[rows, :], AF.Exp, [("RB",)], [("RB",)], scale=-1.0)
                self.tt("dve", self.OT[rows, oc, qb * 512:(qb + 1) * 512], self.OACC[ab][rows, :], rb[rows, :], ALU.mult,
                        [("acc", ab), ("RB",)], [("OT", oc, 4 * qb + j) for j in range(4)])

        LG = self.opt.get("l0_lag", 3)
        for i in range(n + LG):
            if i < n:
                stage_a(items[i])
            if i >= LG:
                stage_c(items[i - LG])
            self.bg_pop(bg, i, n)
        self.bg_flush(bg)

    def attn_sb_pair(self, st, oc, bg=None):
        QZ = self.SETS[st]["QZ"]
        KP = self.SETS[st]["KZ"][0]
        VPt = self.SETS[st]["VP"]
        VP = lambda kt: VPt[:, kt, :]
        items = []
        for qb in range(4):
            for hh in range(2):
                kts = list(reversed(range(4 * qb + 4)))
                for kt in kts:
                    items.append((qb, hh, kt, kt == kts[0], kt == kts[-1]))
        n = len(items)
        pend = {}

        def stage_a(it):
            qb, hh, kt, first, last = it
            j0 = kt - 4 * qb
            col0 = max(0, j0) * 128
            Z, zreg = self.psA(z=True)
            qc = slice(qb * 512 + col0, (qb + 1) * 512)
            self.pe([(Z[:, col0:512], KP[:, kt * 128:(kt + 1) * 128], QZ[hh][:, qc], True, True)],
                    [("KZ", st, 0, kt // 4), ("QZ", st, hh, qb)], [zreg])
            eb = self.nxt("E", 2)
            E = self.EX[eb]
            self.act(E[:, col0:512], Z[:, col0:512], AF.Exp, [zreg], [("E", eb)])
            pend[it] = (Z, zreg, eb, col0, j0)

        def stage_a2(it):
            Z, zreg, eb, col0, j0 = pend[it]
            E = self.EX[eb]
            sb = self.nxt("SP", 4)
            SP = self.SPT[sb]
            self.act(SP[:, col0:512], E[:, col0:512], AF.Ln, [("E", eb)], [("SP", sb)], bias=1.0)
            if j0 >= 0:
                self.tt("dve", SP[:, col0:col0 + 128], SP[:, col0:col0 + 128], self.TRIS, ALU.mult, [("SP", sb)], [("SP", sb)])
            pend[it] = (Z, zreg, sb, col0, j0)

        def stage_c(it):
            qb, hh, kt, first, last = it
            Z, zreg, sb, col0, j0 = pend[it]
            SP = self.SPT[sb]
            if first:
                self.memset("dve", self.SL[:, :], 0.0, [("SL",)])
            mms = [(Z[:, col0:512], self.NTRI, SP[:, col0:512], False, first, "nochk")]
            rd = [("SP", sb)]
            if not first:
                mms.append((Z[:, col0:512], self.NONES, self.SL[:, col0:512], False, True, "nochk"))
                rd.append(("SL",))
            self.pe(mms, rd, [zreg])
            if not last:
                self.tt("dve", self.SL[:, col0:512], self.SL[:, col0:512], SP[:, col0:512], ALU.add, [("SL",), ("SP", sb)], [("SL",)])
            pb = self.nxt("PT", 7)
            PT = self.PT[pb]
            self.act(PT[:, col0:512], Z[:, col0:512], AF.Exp, [zreg], [("PT", pb)])
            if j0 >= 0:
                self.tt("dve", PT[:, col0:col0 + 128], PT[:, col0:col0 + 128], self.TRIS, ALU.mult, [("PT", pb)], [("PT", pb)])
            pend[it] = (pb, col0)

        def stage_e(it):
            qb, hh, kt, first, last = it
            pb, col0 = pend.pop(it)
            PT = self.PT[pb]
            mms = []
            if first:
                self.accb = self.nxt("accb", 2)
                mms.append((self.OACC[self.accb][:, :], self.ZEROS, QZ[hh][:, qb * 512:(qb + 1) * 512], True, False))
            ab = self.accb
            mms.append((self.OACC[ab][:, col0:512], VP(kt), PT[:, col0:512], False, last))
            self.pe(mms, [("PT", pb), ("V", st, kt), ("QZ", st, hh, qb)], [("acc", ab)])
            if last:
                rows = slice(64 * hh, 64 * hh + 64)
                self.cp("dve", self.OT[rows, oc, qb * 512:(qb + 1) * 512], self.OACC[ab][rows, :],
                        [("acc", ab)], [("OT", oc, 4 * qb + j) for j in range(4)])

        LC = self.opt.get("l1_lc", 2)
        LE = LC + 1
        self.psa_wide = True
        for i in range(n + LE):
            if i < n:
                stage_a(items[i])
            if LC <= i < n + LC:
                stage_c(items[i - LC])
            if i < n:
                stage_a2(items[i])
            if i >= LE:
                stage_e(items[i - LE])
            self.bg_pop(bg, i, n)
        self.bg_flush(bg)
        self.psa_wide = False

    def proj_fm(self, W, wsl, blk, wreg, dsts, scale, eng):
        ps, preg = self.psA()
        tsl = slice(blk * 512, (blk + 1) * 512)
        self.pe([(ps[:, :], W[:, c, wsl], self.HT[:, c, tsl], c == 0, c == 7) for c in range(8)],
                [wreg] + [("HT", 4 * blk + j) for j in range(4)], [preg])
        for dst, rows, dreg in dsts:
            if eng == "act":
                self.S.op("act", lambda e, dst=dst, rows=rows: e.activation(out=dst, in_=ps[rows, :], func=AF.Copy, scale=scale), [preg], [dreg])
            else:
                self.ts(eng, dst, ps[rows, :], scale, None, ALU.mult, None, [preg], [dreg])

    def v_proj(self, W, wsl, n, i, wreg, dst, src, sreg, vreg):
        ps, preg = self.psA()
        self.pe([(ps[:, 0:n], src[:, c, i * 128:(i + 1) * 128], W[:, c, wsl], c == 0, c == 7) for c in range(8)],
                [wreg, sreg], [preg])
        self.cp("dve", dst, ps[:, 0:n], [preg], [vreg])

    def pair_wload(self, layer, kind, p, widx, part="all"):
        W = self.WP[widx]
        wreg = ("WP", widx)
        idx = p if kind == "main" else 6 + p
        ncol = 384 if kind == "main" else 128
        src = (self.d_wa if layer == 0 else self.d_wb)[idx]
        if part in ("all", "dma"):
            self.dma("pool", W[:, :, 0:ncol], src[:, :, 0:ncol], [], [wreg])
        if layer == 1 and part in ("all", "fold"):
            for c in range(8):
                self.ts("pool", W[:, c, 0:128], W[:, c, 0:128], self.GC[:, 1, c:c + 1], 1.0, ALU.mult, ALU.mult, [wreg], [wreg])
                if kind == "main":
                    self.ts("pool", W[:, c, 128:384], W[:, c, 128:384], self.GC[:, 0, c:c + 1], 1.0, ALU.mult, ALU.mult, [wreg], [wreg])

    def pair_closures(self, layer, kind, p, st, widx, nxt_spec):
        S_ = self.SETS[st]
        W = self.WP[widx]
        wreg = ("WP", widx)
        h0 = slice(0, 64)
        h1 = slice(64, 128)
        brow = [64, 0]
        allq = lambda hh: [("QZ", st, hh, k) for k in range(4)]
        allk = lambda hh: [("KZ", st, hh, k) for k in range(4)]
        cl = []
        if layer == 0 and kind == "main":
            def bias():
                for hh in range(2):
                    h = 2 * p + hh
                    r = brow[hh]
                    self.dma("sp", S_["QZ"][hh][r:r + 1, :], self.CH[h:h + 1, 0, :], [("CH", 0)], allq(hh))
                    self.dma("sp", S_["QZ"][hh][r + 1:r + 2, :], self.CH[h:h + 1, 1, :], [("CH", 1)], allq(hh))
                    self.dma("sp", S_["KZ"][hh][r + 2:r + 3, :], self.CH[h:h + 1, 0, :], [("CH", 0)], allk(hh))
                    self.dma("sp", S_["KZ"][hh][r + 3:r + 4, :], self.CH[h:h + 1, 1, :], [("CH", 1)], allk(hh))
            cl.append(bias)
        if layer == 0 and kind == "mem":
            def clr():
                for hh in range(2):
                    r = brow[hh]
                    self.memset("pool", S_["QZ"][hh][r:r + 4, :], 0.0, allq(hh))
            cl.append(clr)
        qeng = "act" if layer == 0 else "dve"
        for blk in range(4):
            def qp(blk=blk):
                tsl = slice(blk * 512, (blk + 1) * 512)
                self.proj_fm(W, slice(0, 128), blk, wreg,
                             [(S_["QZ"][0][h0, tsl], h0, ("QZ", st, 0, blk)), (S_["QZ"][1][h1, tsl], h1, ("QZ", st, 1, blk))], 0.125, qeng)
            cl.append(qp)
        if kind == "main":
            for blk in range(4):
                def kp(blk=blk):
                    tsl = slice(blk * 512, (blk + 1) * 512)
                    if layer == 0:
                        dsts = [(S_["KZ"][0][h0, tsl], h0, ("KZ", st, 0, blk)), (S_["KZ"][1][h1, tsl], h1, ("KZ", st, 1, blk))]
                    else:
                        dsts = [(S_["KZ"][0][:, tsl], slice(0, 128), ("KZ", st, 0, blk))]
                    self.proj_fm(W, slice(128, 256), blk, wreg, dsts, 1.0, "dve")
                cl.append(kp)
            for i in range(NT):
                cl.append(lambda i=i: self.v_proj(W, slice(256, 384), 128, i, wreg, S_["VP"][:, i, :], self.HT, ("HT", i), ("V", st, i)))
        if nxt_spec is not None:
            cl.insert(0, lambda: self.pair_wload(layer, *nxt_spec, part="dma"))
            if layer == 1:
                cl.append(lambda: self.pair_wload(layer, *nxt_spec, part="fold"))
        return cl

    def run_pairs(self, layer, preloaded=False, after_first=None):
        nmain = 6 if layer == 0 else self.opt.get("l1_pairs", 6)
        plist = [("main", p) for p in range(nmain)]
        if layer == 0 or self.opt.get("l1_mem", True):
            plist += [("mem", p) for p in range(2)]
        spec = lambda idx: (plist[idx] + (idx % 2,)) if idx < len(plist) else None
        if not preloaded:
            self.pair_wload(layer, *spec(0))
        for c in self.pair_closures(layer, plist[0][0], plist[0][1], 0, 0, spec(1)):
            c()
        if after_first is not None:
            after_first()
        for idx, (kind, p) in enumerate(plist):
            st = idx % 2
            bgl = []
            if idx + 1 < len(plist):
                bgl = self.pair_closures(layer, plist[idx + 1][0], plist[idx + 1][1], 1 - st, (idx + 1) % 2, spec(idx + 2))
            if idx == len(plist) - 1 and st == 1 and self.opt.get("wo_prefetch", True):
                set0 = [("QZ", 0, hh, k) for hh in range(2) for k in range(4)] + [("KZ", 0, hh, k) for hh in range(2) for k in range(4)] \
                    + [("V", 0, i) for i in range(NT)]
                bgl = [lambda: self.dma("pool", self.WO, self.d_wout[layer].rearrange("(c p) n -> p c n", p=128), [], [("WO",)] + set0)]
                self.wo_loaded = layer
            bg = {"list": bgl, "done": 0}
            if kind == "main":
                if layer == 0:
                    VPt = self.SETS[st]["VP"]
                    self.attn_softmax_pair(st, None, lambda kt, VPt=VPt: VPt[:, kt, :], None, True, p,
                                           lambda hh, kt, st=st: ("KZ", st, hh, kt // 4), lambda kt, st=st: ("V", st, kt), bg)
                elif self.opt.get("l1_attn", True):
                    self.attn_sb_pair(st, p, bg)
                else:
                    self.bg_flush(bg)
            else:
                MK = self.MKT[:, p, :]
                self.attn_softmax_pair(st, [MK, MK], lambda kt, p=p: self.MVP[:, kt, 128 * p:128 * p + 128], 2, False, 6 + p,
                                       lambda hh, kt, p=p: ("MKT", p), lambda kt: ("MV", kt), bg)

    def mem_branch(self, s, l):
        self.load_gb(1 + l)
        self.wload(self.WM, self.d_wmemkv[l], ("WM",))
        for mt in range(2):
            self.dma("sp", self.MTMP[:], self.d_mem[s, mt * 128:(mt + 1) * 128, :], [], [("mtmp",)])
            self.norm_tile(self.MTMP[:], ("mtmp",), self.GB[:], self.HMT[:, :, mt * 128:(mt + 1) * 128], ("HMT", mt), "act")
        self.norm_flush()
        for p in range(2):
            ps, preg = self.psA()
            self.pe([(ps[:, 0:256], self.WM[:, c, 128 * p:128 * p + 128], self.HMT[:, c, :], c == 0, c == 7) for c in range(8)],
                    [("WM",), ("HMT", 0), ("HMT", 1)], [preg])
            self.cp("act", self.MKT[:, p, :], ps[:, 0:256], [preg], [("MKT", p)])
        for mt in range(2):
            self.v_proj(self.WM, slice(256, 512), 256, mt, ("WM",), self.MVP[:, mt, :], self.HMT, ("HMT", mt), ("MV", mt))

    def out_proj(self, l, post=None, prefetch=False):
        if self.wo_loaded != l:
            self.wload(self.WO, self.d_wout[l], ("WO",))
        self.wo_loaded = None
        if prefetch:
            self.ffn_prefetch(l)
        for i in range(NT):
            a = self.nxt("acc2", 2)
            ps = self.ACC2[a]
            regs = [("acc", 2 * a), ("acc", 2 * a + 1)]
            mms = []
            for h in range(2):
                for c in range(8):
                    mms.append((ps[:, 512 * h:512 * h + 512], self.OT[:, c, i * 128:(i + 1) * 128], self.WO[:, c, 512 * h:512 * h + 512], c == 0, c == 7))
            self.pe(mms, [("WO",)] + [("OT", c, i) for c in range(8)], regs)
            self.tt("dve", self.X[:, i, :], ps[:, :], self.X[:, i, :], ALU.add, regs + [("X", i)], [("X", i)])
            if post is not None:
                post(i)
        self.norm_flush()

    def norm_x_tile(self, i, gb):
        self.norm_tile(self.X[:, i, :], ("X", i), gb, self.HT[:, :, i * 128:(i + 1) * 128], ("HT", i), "act" if i % 2 else "dve")

    def ffn_prefetch(self, l):
        wup = self.d_wup[l]
        nf = 128 * FCH[0]
        self.wload(self.WUP[0][:, :, 0, 0:nf], wup[:, 0:nf], ("WUP", 0, 0))
        self.wload(self.WUP[0][:, :, 1, 0:nf], wup[:, DFF:DFF + nf], ("WUP", 0, 1))
        self.wload(self.WDN[0][:, 0:FCH[0], :], self.d_wdown[l][0:nf, :], ("WDN", 0))
        self.ffn_preloaded = l

    def ffn(self, l, post=None):
        wup = self.d_wup[l]
        wdn = self.d_wdown[l]
        nch = len(FCH)

        def load_up(fi):
            b = fi % 2
            f0 = 128 * sum(FCH[:fi])
            nf = 128 * FCH[fi]
            self.wload(self.WUP[b][:, :, 0, 0:nf], wup[:, f0:f0 + nf], ("WUP", b, 0))
            self.wload(self.WUP[b][:, :, 1, 0:nf], wup[:, DFF + f0:DFF + f0 + nf], ("WUP", b, 1))

        def load_dn(fi):
            b = fi % 2
            f0 = 128 * sum(FCH[:fi])
            nf = 128 * FCH[fi]
            self.wload(self.WDN[b][:, 0:FCH[fi], :], wdn[f0:f0 + nf, :], ("WDN", b))

        def down_tile(fi, i, final):
            b = fi % 2
            a = self.nxt("acc2", 2)
            ps = self.ACC2[a]
            regs = [("acc", 2 * a), ("acc", 2 * a + 1)]
            mms = []
            for h in range(2):
                for k in range(FCH[fi]):
                    mms.append((ps[:, 512 * h:512 * h + 512], self.AT[b][:, k, i * 128:(i + 1) * 128],
                                self.WDN[b][:, k, 512 * h:512 * h + 512], k == 0, k == FCH[fi] - 1))
            self.pe(mms, [("WDN", b)] + [("AT", b, k, i // 4) for k in range(FCH[fi])], regs)
            self.tt("dve", self.X[:, i, :], ps[:, :], self.X[:, i, :], ALU.add, regs + [("X", i)], [("X", i)])
            if final and post is not None:
                post(i)

        def step(fi, k, tb):
            b = fi % 2
            cbase = sum(FCH[:fi])
            tsl = slice(tb * 512, (tb + 1) * 512)
            gT = None
            for gv in range(2):
                ci = cbase + k + (DFF // 128) * gv
                ps, preg = self.psA()
                self.pe([(ps[:, :], self.WUP[b][:, c, gv, 128 * k:128 * k + 128], self.HT[:, c, tsl], c == 0, c == 7) for c in range(8)],
                        [("WUP", b, gv)] + [("HT", 4 * tb + j) for j in range(4)], [preg])
                UB = self.UBS[gv][tb % 2]
                ureg = ("UB", gv, tb % 2)
                if tb == 0:
                    self.memset("pool", UB[:, 0:2], 0.0, [ureg])
                else:
                    self.cp("pool", UB[:, 0:2], self.UBS[gv][(tb + 1) % 2][:, 512:514], [("UB", gv, (tb + 1) % 2)], [ureg])
                self.cp("act", UB[:, 2:514], ps[:, :], [preg], [ureg])
                tbuf = self.nxt("T%d" % gv, 2)
                T = (self.TG if gv == 0 else self.TV)[tbuf]
                treg = ("T", gv, tbuf)
                self.act(T[:, :], ps[:, :], AF.Identity, [preg, ("cw",), ("cb",)], [treg],
                         scale=self.CW[:, ci, 2:3], bias=self.CB[:, ci:ci + 1])
                self.stt(T[:, :], UB[:, 1:513], self.CW[:, ci, 1:2], T[:, :], ALU.mult, ALU.add, [ureg, treg, ("cw",)], [treg])
                self.stt(T[:, :], UB[:, 0:512], self.CW[:, ci, 0:1], T[:, :], ALU.mult, ALU.add, [ureg, treg, ("cw",)], [treg])
                if gv == 0:
                    gT = (T, treg)
                else:
                    self.act(gT[0][:, :], gT[0][:, :], AF.Silu, [gT[1]], [gT[1]])
                    self.tt("dve", self.AT[b][:, k, tsl], gT[0][:, :], T[:, :], ALU.mult, [gT[1], treg], [("AT", b, k, tb)])

        if self.ffn_preloaded != l:
            load_up(0)
            load_dn(0)
        self.ffn_preloaded = None
        pending = []
        for fi in range(nch):
            if fi + 1 < nch:
                load_up(fi + 1)
            steps = [(k, tb) for k in range(FCH[fi]) for tb in range(4)]
            for si, (k, tb) in enumerate(steps):
                step(fi, k, tb)
                left = len(steps) - si
                ne = (len(pending) + left - 1) // left
                for _ in range(ne):
                    pending.pop(0)()
            assert not pending
            if fi + 1 < nch:
                load_dn(fi + 1)
            pending = [(lambda fi=fi, i=i: down_tile(fi, i, fi == nch - 1)) for i in range(NT)]
        for cl in pending:
            cl()
        self.norm_flush()

    def layer0_mixer(self, s):
        self.dma("pool", self.WP[1][:, :, 0:12], self.d_wf, [], [("WP", 1)])
        self.pair_wload(0, "main", 0, 0)
        self.load_gb(0)
        brow = [64, 0]

        def init_set(st):
            S_ = self.SETS[st]
            for hh in range(2):
                r = brow[hh]
                self.memset("dve", S_["QZ"][hh][:, :], 0.0, [("QZ", st, hh, k) for k in range(4)])
                self.memset("dve", S_["KZ"][hh][:, :], 0.0, [("KZ", st, hh, k) for k in range(4)])
                self.memset("dve", S_["QZ"][hh][r:r + 4, :], -1.0, [("QZ", st, hh, k) for k in range(4)])
                self.memset("dve", S_["KZ"][hh][r:r + 4, :], 1.0, [("KZ", st, hh, k) for k in range(4)])

        init_set(0)
        cf_alias = [("QZ", 1, hh, k) for hh in range(2) for k in range(4)]
        for i in range(NT):
            self.norm_x_tile(i, self.GB[:])
        self.norm_flush()
        WF = self.WP[1][:, :, 0:12]
        cfo_alias = [("WP", 1), ("PT", 0), ("PT", 1)]
        for blk in range(4):
            ps, preg = self.psA()
            tsl = slice(blk * 512, (blk + 1) * 512)
            self.pe([(ps[0:12, :], WF[:, c, :], self.HT[:, c, tsl], c == 0, c == 7) for c in range(8)],
                    [("WP", 1)] + [("HT", 4 * blk + j) for j in range(4)], [preg])
            self.act(self.CF[0:12, tsl], ps[0:12, :], AF.Exp, [preg], [("CF", blk)] + cf_alias, scale=-1.0, bias=self.NB[:, 0:1])
            self.act(self.CF[0:12, tsl], self.CF[0:12, tsl], AF.Ln, [("CF", blk)], [("CF", blk)], bias=1.0)
            self.ts("dve", self.CF[0:12, tsl], self.CF[0:12, tsl], -0.5, None, ALU.mult, None, [("CF", blk)], [("CF", blk)])
        self.S.op("dve", lambda e: e.tensor_tensor_scan(out=self.CFO[0:12, :], data0=self.CF[0:12, :], data1=self.CF[0:12, :],
                                                         initial=0.0, op0=ALU.add, op1=ALU.add),
                  [("CF", k) for k in range(4)] + cf_alias, [("CFO",)] + cfo_alias)
        self.cp("dve", self.CH[0:12, 0, :], self.CFO[0:12, :], [("CFO",)] + cfo_alias, [("CH", 0)])
        self.tt("dve", self.CH[0:12, 1, :], self.CFO[0:12, :], self.CH[0:12, 0, :], ALU.subtract, [("CFO",), ("CH", 0)] + cfo_alias, [("CH", 1)])
        if "cf" in self.d_dbg:
            self.dma("sp", self.d_dbg["cf"], self.CFO[0:12, :], [("CFO",)] + cfo_alias, [])
        init_set(1)
        self.run_pairs(0, preloaded=True, after_first=lambda: self.mem_branch(s, 0))

    def layer1_mixer(self, s):
        for st in range(2):
            for hh in range(2):
                self.memset("dve", self.SETS[st]["QZ"][hh][:, :], 0.0, [("QZ", st, hh, k) for k in range(4)])
        if self.opt.get("l1_pairs", 6) > 0:
            self.pair_wload(1, "main", 0, 0)
        self.mem_branch(s, 1)
        if not self.prenormed:
            for i in range(NT):
                self.norm_x_tile(i, None)
            self.norm_flush()
        self.run_pairs(1, preloaded=self.opt.get("l1_pairs", 6) > 0)

    def dump_x(self, name):
        if name in self.d_dbg:
            for i in range(NT):
                self.dma("sp", self.d_dbg[name][i * 128:(i + 1) * 128, :], self.X[:, i, :], [("X", i)], [])

    def final_tile(self, s, i, normed):
        ob = self.nxt("outt", 2)
        O = self.OUTT[ob]
        if normed:
            b = self.nxt("hn", 2)
            ss = self.STAT[:, 2 * b:2 * b + 1]
            rs = self.STAT[:, 2 * b + 1:2 * b + 2]
            xin = self.X[:, i, :]
            self.act(self.HN[b][:], xin, AF.Square, [("X", i)], [("hn", b), ("PT", 3 + 2 * b), ("PT", 4 + 2 * b), ("ss", b)], scale=1.0 / 32.0, accum_out=ss)
            self.act(ss, ss, AF.Sqrt, [("ss", b)], [("ss", b)], bias=self.EPSC[:, 0:1])
            self.S.op("dve", lambda e, rs=rs, ss=ss: e.reciprocal(out=rs, in_=ss), [("ss", b)], [("rs", b)])
            self.stt(O[:], xin, rs, self.GB[:], ALU.mult, ALU.mult, [("X", i), ("rs", b), ("gb",)], [("outt", ob)])
            self.dma("sp", self.d_out[s, i * 128:(i + 1) * 128, :], O[:], [("outt", ob)], [])
            if s + 1 < self.nseq:
                self.dma("sp", self.X[:, i, :], self.d_x[s + 1, i * 128:(i + 1) * 128, :], [], [("X", i)])
                self.x_prefetched = s + 1
        else:
            self.dma("sp", self.d_out[s, i * 128:(i + 1) * 128, :], self.X[:, i, :], [("X", i)], [])

    def program(self):
        order = ["l0mix", "l0out", "l0ffn", "l1mix", "l1out", "l1ffn"]
        stop = self.stop
        nph = len(order) if stop is None else order.index(stop) + 1
        self.dma("pool", self.CONST[:], self.d_consts, [], [("const",)])
        self.dma("sp", self.GC[:], self.d_gc, [], [("gc",)])
        self.dma("sp", self.NB[:], self.d_bf, [], [("nb",)])
        self.memset("dve", self.EPSC[:], EPS, [("eps",)])
        self.ts("dve", self.NB[:], self.NB[:], -1.0, None, ALU.mult, None, [("nb",)], [("nb",)])
        self.S.fence()
        for s in range(self.nseq):
            if self.x_prefetched != s:
                for i in range(NT):
                    self.dma("sp", self.X[:, i, :], self.d_x[s, i * 128:(i + 1) * 128, :], [], [("X", i)])
            phs = order[:nph] if self.phases is None else self.phases
            full = self.phases is None and stop is None
            self.prenormed = False
            for ph in phs:
                if ph == "l0mix":
                    self.layer0_mixer(s)
                elif ph == "l1mix":
                    self.layer1_mixer(s)
                elif ph in ("l0out", "l1out"):
                    l = int(ph[1])
                    self.load_gb(3 + l)
                    self.dma("sp", self.CW[:], self.d_cw[l], [], [("cw",)])
                    self.dma("sp", self.CB[:], self.d_cb[l], [], [("cb",)])
                    self.out_proj(l, lambda i: self.norm_x_tile(i, self.GB[:]), prefetch=("l%dffn" % l) in phs)
                    if s == 0:
                        self.dump_x("x_l%dmix" % l)
                elif ph == "l0ffn":
                    self.ffn(0, (lambda i: self.norm_x_tile(i, None)) if full else None)
                    self.prenormed = full
                    if s == 0:
                        self.dump_x("x_l0")
                elif ph == "l1ffn":
                    if full:
                        self.load_gb(5)
                    self.ffn(1, (lambda i, s=s: self.final_tile(s, i, True)) if full else None)
                self.S.fence()
            if not full:
                for i in range(NT):
                    self.final_tile(s, i, False)
            self.S.fence()

    def build(self):
        nc = self.nc
        nseq = self.nseq
        dt_in = lambda name, shape: nc.dram_tensor(name, shape, F32, kind="ExternalInput").ap()
        self.d_x = dt_in("x", [nseq, SEQ, D])
        self.d_mem = dt_in("mem", [nseq, NMEM, D])
        self.d_wa = dt_in("wa", [8, 128, 8, 384])
        self.d_wb = dt_in("wb", [8, 128, 8, 384])
        self.d_wf = dt_in("wf", [128, 8, 12])
        self.d_wmemkv = dt_in("w_memkv", [2, D, 512])
        self.d_wout = dt_in("w_out", [2, D, D])
        self.d_wup = dt_in("w_up", [2, D, 2 * DFF])
        self.d_wdown = dt_in("w_down", [2, DFF, D])
        self.d_gb = dt_in("gb", [6, 128, D])
        self.d_gc = dt_in("gc", [128, 2, 8])
        self.d_bf = dt_in("bf", [12, 1])
        self.d_cw = dt_in("cw", [2, 128, 44, 3])
        self.d_cb = dt_in("cb", [2, 128, 44])
        self.d_consts = dt_in("consts", [128, 7, 128])
        self.d_out = nc.dram_tensor("out", [nseq, SEQ, D], F32, kind="ExternalOutput").ap()
        self.d_dbg = {}
        for name, shape in self.dbg_shapes().items():
            if name in self.dbg:
                self.d_dbg[name] = nc.dram_tensor("dbg_" + name, shape, F32, kind="ExternalOutput").ap()

        with ExitStack() as es:
            sb = lambda name, shape, dt: es.enter_context(nc.sbuf_tensor(name, shape, dt))
            self.X = sb("X", [128, NT, D], F32)
            self.HT = sb("HT", [128, 8, SEQ], BF16)
            self.CONST = sb("CONST", [128, 7, 128], BF16)
            self.IDENT = self.CONST[:, 0, :]
            self.TRI = self.CONST[:, 1, :]
            self.TRIS = self.CONST[:, 2, :]
            self.NTRI = self.CONST[:, 3, :]
            self.NONES = self.CONST[:, 4, :]
            self.ZEROS = self.CONST[:, 5, :]
            self.ONES = self.CONST[:, 6, :]
            self.GB = sb("GB", [128, D], F32)
            self.GC = sb("GC", [128, 2, 8], F32)
            self.NB = sb("NB", [12, 1], F32)
            self.CW = sb("CW", [128, 44, 3], F32)
            self.CB = sb("CB", [128, 44], F32)
            self.STAT = sb("STAT", [128, 16], F32)
            self.EPSC = sb("EPSC", [128, 1], F32)
            self.HN = [sb("HN%d" % i, [128, D], BF16) for i in range(2)]
            self.ARENA_B = 103000
            self.ARENA = sb("ARENA", [128, self.ARENA_B // 2], BF16)
            self.PSA = [es.enter_context(nc.psum_tensor("psA%d" % i, [128, 512], F32)) for i in range(3)]
            self.ACC2 = [es.enter_context(nc.psum_tensor("acc2_%d" % i, [128, 1024], F32)) for i in range(2)]
            self.OACC = [self.ACC2[0][:, 0:512], self.ACC2[0][:, 512:1024]]
            self.LACC = [self.ACC2[1][:, 0:512], self.ACC2[1][:, 512:1024]]
            self.PST = es.enter_context(nc.psum_tensor("psT", [128, 8, 128], BF16))
            self.PSTF = self.PST[:, :, :].rearrange("p a b -> p (a b)").bitcast(F32)
            esem = {e: es.enter_context(nc.semaphore("sem_" + e)) for e in CENGS}
            dsem = [es.enter_context(nc.semaphore("dsem%d" % i)) for i in range(24)]
            self.S = Sched(esem, dsem)
            self.carve_all()
            self.program()
            self.emit()
        return nc

    def dbg_shapes(self):
        return {"x_l0mix": [SEQ, D], "x_l0": [SEQ, D], "x_l1mix": [SEQ, D], "cf": [12, SEQ]}

    def carve(self, off, shape, dt):
        n = int(np.prod(shape[1:]))
        esz = 2 if dt == BF16 else 4
        assert off % 4 == 0, off
        a = self.ARENA[:, off // 2:off // 2 + n * esz // 2]
        if dt == F32:
            a = a.bitcast(F32)
        if len(shape) == 3:
            a = a.rearrange("p (a b) -> p a b", a=shape[1])
        elif len(shape) == 4:
            a = a.rearrange("p (a b c) -> p a b c", a=shape[1], b=shape[2])
        self._off = off + n * esz
        assert self._off <= self.ARENA_B, self._off
        return a

    def carve_all(self):
        self.OT = self.carve(0, [128, 8, SEQ], BF16)
        o = self._off
        self.MTMP = self.carve(0, [128, D], F32)
        self.WM = self.carve(self._off, [128, 8, 512], BF16)
        self.HMT = self.carve(self._off, [128, 8, NMEM], BF16)
        self.SETS = []
        for st in range(2):
            d = {"QZ": [], "KZ": []}
            for i in range(2):
                d["QZ"].append(self.carve(o, [128, SEQ], BF16)); o = self._off
            for i in range(2):
                d["KZ"].append(self.carve(o, [128, SEQ], BF16)); o = self._off
            d["VP"] = self.carve(o, [128, NT, 128], BF16); o = self._off
            self.SETS.append(d)
        self.WP = []
        for i in range(2):
            self.WP.append(self.carve(o, [128, 8, 384], BF16)); o = self._off
        self.PT = []
        for i in range(3):
            self.PT.append(self.carve(o, [128, 512], BF16)); o = self._off
        for b in range(2):
            self.PT.append(self.HN[b][:, 0:512])
            self.PT.append(self.HN[b][:, 512:1024])
        self.MKT = self.carve(o, [128, 2, NMEM], BF16); o = self._off
        self.MVP = self.carve(o, [128, 2, 256], BF16); o = self._off
        self.RB = self.carve(o, [128, 512], F32); o = self._off
        o0 = o
        self.CH = self.carve(o, [128, 2, SEQ], BF16); o = self._off
        o = o0
        self.EX = []
        for i in range(2):
            self.EX.append(self.carve(o, [128, 512], F32)); o = self._off
        self.SPT = []
        for i in range(4):
            self.SPT.append(self.carve(o, [128, 512], BF16)); o = self._off
        self.SL = self.carve(o, [128, 512], BF16); o = self._off
        self.CF = self.SETS[1]["QZ"][0].bitcast(F32)[:, 0:1024] if False else self.carve(53248, [128, SEQ], F32)
        self.CFO = self.carve(79872, [128, SEQ], F32)
        self.WO = self.carve(32768, [128, 8, D], BF16)
        top = self.ARENA_B - 24576
        self.WUP = [self.carve(top, [128, 8, 2, 512], BF16), self.carve(0, [128, 8, 2, 512], BF16)]
        self.WDN = [self.carve(top + 16384, [128, 4, D], BF16), self.carve(16384, [128, 4, D], BF16)]
        o = 24576
        self.AT = []
        for i in range(2):
            self.AT.append(self.carve(o, [128, 4, SEQ], BF16)); o = self._off
        self.UBS = [[None, None], [None, None]]
        for g in range(2):
            for i in range(2):
                self.UBS[g][i] = self.carve(o, [128, 516], F32); o = self._off
        self.TG = []
        self.TV = []
        for i in range(2):
            self.TG.append(self.carve(o, [128, 512], F32)); o = self._off
            self.TV.append(self.carve(o, [128, 512], F32)); o = self._off
        assert o <= top, (o, top)
        self.OUTT = [self.carve(24576, [128, D], F32), self.carve(24576 + 4096, [128, D], F32)]

    def emit(self):
        nc = self.nc
        S = self.S
        with nc.Block() as block:
            @block.tensor
            def _(e):
                for f in S.prog["pe"]:
                    f(e)

            @block.scalar
            def _(e):
                for f in S.prog["act"]:
                    f(e)

            @block.vector
            def _(e):
                for f in S.prog["dve"]:
                    f(e)

            @block.gpsimd
            def _(e):
                for f in S.prog["pool"]:
                    f(e)

            @block.sync
            def _(e):
                for f in S.prog["sp"]:
                    f(e)


def host_inputs(x, mem, ln_mix_g, w_in_a, b_f_a, w_in_b, ln_kv_g, w_kv, ln_mem_g, w_memkv, w_out, ln_ffn_g,
                w_up, conv_w, conv_b, w_down, final_g):
    f = lambda a: np.ascontiguousarray(np.asarray(a, dtype=np.float32))
    rep = lambda g: np.broadcast_to(np.asarray(g, np.float32)[None, :], (128, D))
    gb = f(np.stack([rep(ln_mix_g[0]), rep(ln_mem_g[0]), rep(ln_mem_g[1]), rep(ln_ffn_g[0]), rep(ln_ffn_g[1]), rep(final_g)]))
    col = lambda g: np.asarray(g, np.float32).reshape(8, 128).T
    gc = f(np.stack([col(ln_kv_g), col(ln_mix_g[1])], axis=1))
    cw = f(np.asarray(conv_w, np.float32).reshape(2, 3, 44, 128).transpose(0, 3, 2, 1))
    cb = f(np.asarray(conv_b, np.float32).reshape(2, 44, 128).transpose(0, 2, 1))
    i = np.arange(128)
    ident = (i[:, None] == i[None, :]).astype(np.float32)
    tri = (i[:, None] <= i[None, :]).astype(np.float32)
    tris = (i[:, None] < i[None, :]).astype(np.float32)
    ntri = -(i[:, None] >= i[None, :]).astype(np.float32)
    nones = -np.ones((128, 128), np.float32)
    consts = f(np.stack([ident, tri, tris, ntri, nones, 0.0 * nones, -nones], axis=1))
    pc = lambda w: np.asarray(w, np.float32).reshape(8, 128, -1).transpose(1, 0, 2)
    wia = np.asarray(w_in_a[0], np.float32)
    wib = np.asarray(w_in_b[0], np.float32)
    wkv = np.asarray(w_kv, np.float32)
    wa = np.zeros((8, 128, 8, 384), np.float32)
    wb = np.zeros((8, 128, 8, 384), np.float32)
    for p in range(6):
        for k in range(3):
            wa[p, :, :, 128 * k:128 * k + 128] = pc(wia[:, 768 * k + 128 * p:768 * k + 128 * p + 128])
        wb[p, :, :, 0:128] = pc(wib[:, 128 * p:128 * p + 128])
        wb[p, :, :, 128:256] = pc(wkv[:, 128 * p:128 * p + 128])
        wb[p, :, :, 256:384] = pc(wkv[:, 768 + 128 * p:768 + 128 * p + 128])
    for m_ in range(2):
        wa[6 + m_, :, :, 0:128] = pc(wia[:, 2316 + 128 * m_:2316 + 128 * m_ + 128])
        wb[6 + m_, :, :, 0:128] = pc(wib[:, 768 + 128 * m_:768 + 128 * m_ + 128])
    wf = f(pc(wia[:, 2304:2316]))
    return {
        "wa": wa, "wb": wb, "wf": wf, "w_memkv": f(w_memkv), "w_out": f(w_out),
        "w_up": f(w_up), "w_down": f(w_down), "gb": gb, "gc": gc, "bf": f(np.asarray(b_f_a, np.float32).reshape(12, 1)),
        "cw": cw, "cb": cb, "consts": consts,
    }


def kernel(**inputs):
    x = np.asarray(inputs["x"], np.float32)
    mem = np.asarray(inputs["mem"], np.float32)
    shared = host_inputs(**inputs)
    nseq = x.shape[0] // NCORES
    nc = Builder(nseq=nseq).build()
    in_maps = []
    for c in range(NCORES):
        m = dict(shared)
        m["x"] = np.ascontiguousarray(x[c * nseq:(c + 1) * nseq])
        m["mem"] = np.ascontiguousarray(mem[c * nseq:(c + 1) * nseq])
        in_maps.append(m)
    res = run_bass_kernel_spmd(nc, in_maps, core_ids=list(range(NCORES)))
    return np.concatenate([np.asarray(r["out"], np.float32) for r in res.results], axis=0)
```

```python
import numpy as np
from contextlib import ExitStack
import concourse.bass as bass
import concourse.mybir as mybir
from concourse.bass_utils import run_bass_kernel_spmd

F32 = mybir.dt.float32
BF16 = mybir.dt.bfloat16
AF = mybir.ActivationFunctionType
ALU = mybir.AluOpType

D = 1024
SEQ = 2048
NT = SEQ // 128
NMEM = 256
DFF = 2816
A_IN = 2572
EPS = 1e-6
NCORES = 8
ENGS = ["pe", "act", "dve", "pool", "sp"]
CENGS = ["pe", "act", "dve", "pool"]
FCH = [4, 4, 4, 4, 4, 2]


class Sched:
    def __init__(self, esem, dsem):
        self.esem = esem
        self.dsem = dsem
        self.prog = {e: [] for e in ENGS}
        self.cnt = {e: 0 for e in CENGS}
        self.dcnt = [0] * len(dsem)
        self.dpool = {"sp": list(range(0, 16)), "pool": list(range(16, 24)), "act": list(range(16, 24))}
        self.dnext = {"sp": 0, "pool": 0, "act": 0}
        self.known = {e: {} for e in ENGS}
        self.state = {}

    def _sem(self, src):
        return self.esem[src] if isinstance(src, str) else self.dsem[src]

    def _need(self, eng, reads, writes):
        need = {}

        def add(w, kind):
            src, val = w
            if src == eng and (eng == "pe" or kind == "WAR"):
                return
            if need.get(src, 0) < val:
                need[src] = val

        for r in reads:
            st = self.state.get(r)
            if st is not None and st[0] is not None:
                add(st[0], "RAW")
        for w in writes:
            st = self.state.get(w)
            if st is not None:
                if st[0] is not None:
                    add(st[0], "WAW")
                for s_, v_ in st[1].items():
                    add((s_, v_), "WAR")
        out = []
        kn = self.known[eng]
        for src, val in need.items():
            if kn.get(src, 0) < val:
                kn[src] = val
                out.append((self._sem(src), val))
        return out

    def _update(self, me, reads, writes):
        src, val = me
        for r in reads:
            st = self.state.get(r)
            if st is None:
                st = [None, {}]
                self.state[r] = st
            st[1][src] = val
        for w in writes:
            self.state[w] = [me, {}]

    def op(self, eng, fn, reads=(), writes=()):
        wl = self._need(eng, reads, writes)
        self.cnt[eng] += 1
        sem = self.esem[eng]

        def emit(e, wl=wl, fn=fn, sem=sem):
            for sm, v in wl:
                e.wait_ge(sm, v)
            fn(e).then_inc(sem, 1)

        self.prog[eng].append(emit)
        self._update((eng, self.cnt[eng]), reads, writes)

    def dma(self, q, out, in_, reads=(), writes=()):
        pl = self.dpool[q]
        j = pl[self.dnext[q] % len(pl)]
        self.dnext[q] += 1
        wl = self._need(q, reads, writes)
        prev = 16 * self.dcnt[j]
        if prev > 0 and self.known[q].get(j, 0) < prev:
            self.known[q][j] = prev
            wl.append((self.dsem[j], prev))
        self.dcnt[j] += 1
        sem = self.dsem[j]

        def emit(e, wl=wl, out=out, in_=in_, sem=sem):
            for sm, v in wl:
                e.wait_ge(sm, v)
            e.dma_start(out=out, in_=in_).then_inc(sem, 16)

        self.prog[q].append(emit)
        self._update((j, 16 * self.dcnt[j]), reads, writes)

    def fence(self):
        srcs = [(e, self.cnt[e]) for e in CENGS] + [(j, 16 * c) for j, c in enumerate(self.dcnt)]
        for e in ENGS:
            wl = []
            for src, val in srcs:
                if src == e and e == "pe":
                    continue
                if val > self.known[e].get(src, 0):
                    self.known[e][src] = val
                    wl.append((self._sem(src), val))

            def emit(en, wl=wl):
                for sm, v in wl:
                    en.wait_ge(sm, v)

            self.prog[e].append(emit)
        self.state = {}


class Builder:
    def __init__(self, nseq=2, stop=None, dbg=(), phases=None, opt=None):
        self.phases = phases
        self.opt = opt or {}
        self.nseq = nseq
        self.stop = stop
        self.dbg = set(dbg)
        self.nc = bass.Bass("TRN2", target_bir_lowering=False)
        self.rr = {}
        self.norm_pending = []
        self.psa_wide = False
        self.ffn_preloaded = None
        self.wo_loaded = None
        self.x_prefetched = None

    def nxt(self, name, n):
        v = self.rr.get(name, 0)
        self.rr[name] = (v + 1) % n
        return v

    def pe(self, mms, reads, writes):
        def fn(e, mms=mms):
            ins = None
            for m in mms:
                if m[0] == "tr":
                    ins = e.transpose(out=m[1], in_=m[2], identity=self.IDENT)
                elif len(m) == 6:
                    out, lhsT, rhs, start, stop, _ = m
                    ins = e.matmul(out, lhsT, rhs, start=start, stop=stop, skip_group_check=True)
                else:
                    out, lhsT, rhs, start, stop = m
                    ins = e.matmul(out, lhsT, rhs, start=start, stop=stop)
            return ins

        self.S.op("pe", fn, reads, writes)

    def act(self, out, in_, func, reads, writes, **kw):
        self.S.op("act", lambda e: e.activation(out=out, in_=in_, func=func, **kw), reads, writes)

    def ts(self, eng, out, in0, s1, s2, op0, op1, reads, writes):
        if op1 is None:
            self.S.op(eng, lambda e: e.tensor_scalar(out=out, in0=in0, scalar1=s1, scalar2=s2, op0=op0), reads, writes)
        else:
            self.S.op(eng, lambda e: e.tensor_scalar(out=out, in0=in0, scalar1=s1, scalar2=s2, op0=op0, op1=op1), reads, writes)

    def tt(self, eng, out, in0, in1, op, reads, writes):
        self.S.op(eng, lambda e: e.tensor_tensor(out=out, in0=in0, in1=in1, op=op), reads, writes)

    def stt(self, out, in0, scalar, in1, op0, op1, reads, writes):
        self.S.op("dve", lambda e: e.scalar_tensor_tensor(out=out, in0=in0, scalar=scalar, in1=in1, op0=op0, op1=op1), reads, writes)

    def cp(self, eng, out, in_, reads, writes):
        if eng == "act":
            self.S.op("act", lambda e: e.activation(out=out, in_=in_, func=AF.Copy), reads, writes)
        else:
            self.S.op(eng, lambda e: e.tensor_copy(out=out, in_=in_), reads, writes)

    def memset(self, eng, ap, val, writes):
        self.S.op(eng, lambda e: e.memset(ap, val), (), writes)

    def dma(self, q, out, in_, reads, writes):
        self.S.dma(q, out, in_, reads, writes)

    def psA(self, z=False):
        if self.psa_wide:
            if z:
                k = self.nxt("psAz", 4)
                if k >= 3:
                    return self.LACC[0], ("acc", 2)
                return self.PSA[k], ("psA", k)
            return self.PSTF, ("psT",)
        k = self.nxt("psA", 3)
        return self.PSA[k], ("psA", k)

    def norm_tile(self, xin, xreg, gb, dst, dstreg, evac_eng):
        b = self.nxt("hn", 2)
        ss = self.STAT[:, 2 * b:2 * b + 1]
        rs = self.STAT[:, 2 * b + 1:2 * b + 2]
        hn = self.HN[b]
        hreg = [("hn", b), ("PT", 3 + 2 * b), ("PT", 4 + 2 * b)]
        self.act(hn[:], xin, AF.Square, [xreg], hreg + [("ss", b)], scale=1.0 / 32.0, accum_out=ss)
        self.act(ss, ss, AF.Sqrt, [("ss", b)], [("ss", b)], bias=self.EPSC[:, 0:1])
        self.S.op("dve", lambda e: e.reciprocal(out=rs, in_=ss), [("ss", b)], [("rs", b)])
        if gb is None:
            self.ts("dve", hn[:], xin, rs, None, ALU.mult, None, [xreg, ("rs", b)], hreg)
        else:
            self.stt(hn[:], xin, rs, gb, ALU.mult, ALU.mult, [xreg, ("rs", b), ("gb",)], hreg)
        def part2():
            self.pe([("tr", self.PST[:, c, :], hn[:, c * 128:(c + 1) * 128]) for c in range(8)], hreg, [("psT",)])
            self.cp(evac_eng, dst, self.PST[:, :, :], [("psT",)], [dstreg])

        self.norm_flush()
        self.norm_pending.append(part2)

    def norm_flush(self):
        while self.norm_pending:
            self.norm_pending.pop(0)()

    def load_gb(self, idx):
        self.dma("sp", self.GB[:], self.d_gb[idx], [], [("gb",)])

    def wload(self, dst, src2d, reg, fold=None, part="all"):
        if part in ("all", "dma"):
            self.dma("pool", dst, src2d.rearrange("(c p) n -> p c n", p=128), [], [reg])
        if fold is not None and part in ("all", "fold"):
            for c in range(8):
                self.ts("pool", dst[:, c, :], dst[:, c, :], self.GC[:, fold, c:c + 1], 1.0, ALU.mult, ALU.mult, [reg], [reg])

    def bg_pop(self, bg, i, n):
        if not bg or not bg["list"]:
            return
        tot = len(bg["list"])
        den = max(1, int(0.8 * n))
        target = min(tot, -(-tot * (i + 1) // den))
        while bg["done"] < target:
            bg["list"][bg["done"]]()
            bg["done"] += 1

    def bg_flush(self, bg):
        if bg:
            while bg["done"] < len(bg["list"]):
                bg["list"][bg["done"]]()
                bg["done"] += 1

    def attn_softmax_pair(self, st, KZo, VP, nkt, causal, oc, kreg, vreg, bg=None):
        QZ = self.SETS[st]["QZ"]
        KZ = KZo if KZo is not None else self.SETS[st]["KZ"]
        items = []
        for qb in range(4):
            for hh in range(2):
                kts = list(range(4 * qb + 4)) if causal else list(range(nkt))
                for kt in kts:
                    items.append((qb, hh, kt, kt == kts[0], kt == kts[-1]))
        n = len(items)
        pend = {}

        def stage_a(it):
            qb, hh, kt, first, last = it
            j0 = kt - 4 * qb if causal else -1
            col0 = max(0, j0) * 128
            S_, sreg = self.psA()
            qc = slice(qb * 512 + col0, (qb + 1) * 512)
            self.pe([(S_[:, col0:512], KZ[hh][:, kt * 128:(kt + 1) * 128], QZ[hh][:, qc], True, True)],
                    [kreg(hh, kt), ("QZ", st, hh, qb)], [sreg])
            pb = self.nxt("PT", 7)
            PT = self.PT[pb]
            self.act(PT[:, col0:512], S_[:, col0:512], AF.Exp, [sreg], [("PT", pb)])
            if j0 >= 0:
                self.tt("dve", PT[:, col0:col0 + 128], PT[:, col0:col0 + 128], self.TRI, ALU.mult, [("PT", pb)], [("PT", pb)])
            pend[it] = (pb, col0)

        def stage_c(it):
            qb, hh, kt, first, last = it
            pb, col0 = pend.pop(it)
            PT = self.PT[pb]
            if first:
                self.accb = self.nxt("accb", 2)
            ab = self.accb
            self.pe([(self.OACC[ab][:, col0:512], VP(kt), PT[:, col0:512], first, last),
                     (self.LACC[ab][:, col0:512], self.ONES, PT[:, col0:512], first, last)],
                    [("PT", pb), vreg(kt)], [("acc", ab), ("acc", 2 + ab)])
            if last:
                rows = slice(64 * hh, 64 * hh + 64)
                rb = self.RB
                self.act(rb[rows, :], self.LACC[ab][rows, :], AF.Ln, [("acc", 2 + ab)], [("RB",)])
                self.act(rb[rows, :], rb[rows, :], AF.Exp, [("RB",)], [("RB",)], scale=-1.0)
                self.tt("dve", self.OT[rows, oc, qb * 512:(qb + 1) * 512], self.OACC[ab][rows, :], rb[rows, :], ALU.mult,
                        [("acc", ab), ("RB",)], [("OT", oc, 4 * qb + j) for j in range(4)])

        LG = self.opt.get("l0_lag", 3)
        for i in range(n + LG):
            if i < n:
                stage_a(items[i])
            if i >= LG:
                stage_c(items[i - LG])
            self.bg_pop(bg, i, n)
        self.bg_flush(bg)

    def attn_sb_pair(self, st, oc, bg=None):
        QZ = self.SETS[st]["QZ"]
        KP = self.SETS[st]["KZ"][0]
        VPt = self.SETS[st]["VP"]
        VP = lambda kt: VPt[:, kt, :]
        items = []
        for qb in range(4):
            for hh in range(2):
                kts = list(reversed(range(4 * qb + 4)))
                for kt in kts:
                    items.append((qb, hh, kt, kt == kts[0], kt == kts[-1]))
        n = len(items)
        pend = {}

        def stage_a(it):
            qb, hh, kt, first, last = it
            j0 = kt - 4 * qb
            col0 = max(0, j0) * 128
            Z, zreg = self.psA(z=True)
            qc = slice(qb * 512 + col0, (qb + 1) * 512)
            self.pe([(Z[:, col0:512], KP[:, kt * 128:(kt + 1) * 128], QZ[hh][:, qc], True, True)],
                    [("KZ", st, 0, kt // 4), ("QZ", st, hh, qb)], [zreg])
            E = self.LACC[1]
            self.act(E[:, col0:512], Z[:, col0:512], AF.Exp, [zreg], [("acc", 3)])
            pend[it] = (Z, zreg, E, col0, j0)

        def stage_a2(it):
            Z, zreg, E, col0, j0 = pend[it]
            sb = self.nxt("SP", 4)
            SP = self.SPT[sb]
            self.act(SP[:, col0:512], E[:, col0:512], AF.Ln, [("acc", 3)], [("SP", sb)], bias=1.0)
            if j0 >= 0:
                self.tt("dve", SP[:, col0:col0 + 128], SP[:, col0:col0 + 128], self.TRIS, ALU.mult, [("SP", sb)], [("SP", sb)])
            pend[it] = (Z, zreg, sb, col0, j0)

        def stage_c(it):
            qb, hh, kt, first, last = it
            Z, zreg, sb, col0, j0 = pend[it]
            SP = self.SPT[sb]
            if first:
                self.memset("dve", self.SL[:, :], 0.0, [("SL",)])
            mms = [(Z[:, col0:512], self.NTRI, SP[:, col0:512], False, first, "nochk")]
            rd = [("SP", sb)]
            if not first:
                mms.append((Z[:, col0:512], self.NONES, self.SL[:, col0:512], False, True, "nochk"))
                rd.append(("SL",))
            self.pe(mms, rd, [zreg])
            if not last:
                self.tt("dve", self.SL[:, col0:512], self.SL[:, col0:512], SP[:, col0:512], ALU.add, [("SL",), ("SP", sb)], [("SL",)])
            pb = self.nxt("PT", 7)
            PT = self.PT[pb]
            self.act(PT[:, col0:512], Z[:, col0:512], AF.Exp, [zreg], [("PT", pb)])
            if j0 >= 0:
                self.tt("dve", PT[:, col0:col0 + 128], PT[:, col0:col0 + 128], self.TRIS, ALU.mult, [("PT", pb)], [("PT", pb)])
            pend[it] = (pb, col0)

        def stage_e(it):
            qb, hh, kt, first, last = it
            pb, col0 = pend.pop(it)
            PT = self.PT[pb]
            mms = []
            if first:
                self.accb = self.nxt("accb", 2)
                mms.append((self.OACC[self.accb][:, :], self.ZEROS, QZ[hh][:, qb * 512:(qb + 1) * 512], True, False))
            ab = self.accb
            mms.append((self.OACC[ab][:, col0:512], VP(kt), PT[:, col0:512], False, last))
            self.pe(mms, [("PT", pb), ("V", st, kt), ("QZ", st, hh, qb)], [("acc", ab)])
            if last:
                rows = slice(64 * hh, 64 * hh + 64)
                self.cp("dve", self.OT[rows, oc, qb * 512:(qb + 1) * 512], self.OACC[ab][rows, :],
                        [("acc", ab)], [("OT", oc, 4 * qb + j) for j in range(4)])

        LC = self.opt.get("l1_lc", 2)
        LE = LC + 1
        self.psa_wide = True
        for i in range(n + LE):
            if i < n:
                stage_a(items[i])
            if LC <= i < n + LC:
                stage_c(items[i - LC])
            if i < n:
                stage_a2(items[i])
            if i >= LE:
                stage_e(items[i - LE])
            self.bg_pop(bg, i, n)
        self.bg_flush(bg)
        self.psa_wide = False

    def proj_fm(self, W, wsl, blk, wreg, dsts, scale, eng):
        ps, preg = self.psA()
        tsl = slice(blk * 512, (blk + 1) * 512)
        self.pe([(ps[:, :], W[:, c, wsl], self.HT[:, c, tsl], c == 0, c == 7) for c in range(8)],
                [wreg] + [("HT", 4 * blk + j) for j in range(4)], [preg])
        for dst, rows, dreg in dsts:
            if eng == "act":
                self.S.op("act", lambda e, dst=dst, rows=rows: e.activation(out=dst, in_=ps[rows, :], func=AF.Copy, scale=scale), [preg], [dreg])
            else:
                self.ts(eng, dst, ps[rows, :], scale, None, ALU.mult, None, [preg], [dreg])

    def v_proj(self, W, wsl, n, i, wreg, dst, src, sreg, vreg):
        ps, preg = self.psA()
        self.pe([(ps[:, 0:n], src[:, c, i * 128:(i + 1) * 128], W[:, c, wsl], c == 0, c == 7) for c in range(8)],
                [wreg, sreg], [preg])
        self.cp("dve", dst, ps[:, 0:n], [preg], [vreg])

    def pair_wload(self, layer, kind, p, widx, part="all"):
        W = self.WP[widx]
        wreg = ("WP", widx)
        idx = p if kind == "main" else 6 + p
        ncol = 384 if kind == "main" else 128
        src = (self.d_wa if layer == 0 else self.d_wb)[idx]
        if part in ("all", "dma"):
            self.dma("pool", W[:, :, 0:ncol], src[:, :, 0:ncol], [], [wreg])
        if layer == 1 and part in ("all", "fold"):
            for c in range(8):
                self.ts("pool", W[:, c, 0:128], W[:, c, 0:128], self.GC[:, 1, c:c + 1], 1.0, ALU.mult, ALU.mult, [wreg], [wreg])
                if kind == "main":
                    self.ts("pool", W[:, c, 128:384], W[:, c, 128:384], self.GC[:, 0, c:c + 1], 1.0, ALU.mult, ALU.mult, [wreg], [wreg])

    def pair_closures(self, layer, kind, p, st, widx, nxt_spec):
        S_ = self.SETS[st]
        W = self.WP[widx]
        wreg = ("WP", widx)
        h0 = slice(0, 64)
        h1 = slice(64, 128)
        brow = [64, 0]
        allq = lambda hh: [("QZ", st, hh, k) for k in range(4)]
        allk = lambda hh: [("KZ", st, hh, k) for k in range(4)]
        cl = []
        if layer == 0 and kind == "main":
            def bias():
                for hh in range(2):
                    h = 2 * p + hh
                    r = brow[hh]
                    self.dma("sp", S_["QZ"][hh][r:r + 1, :], self.CH[h:h + 1, 0, :], [("CH", 0)], allq(hh))
                    self.dma("sp", S_["QZ"][hh][r + 1:r + 2, :], self.CH[h:h + 1, 1, :], [("CH", 1)], allq(hh))
                    self.dma("sp", S_["KZ"][hh][r + 2:r + 3, :], self.CH[h:h + 1, 0, :], [("CH", 0)], allk(hh))
                    self.dma("sp", S_["KZ"][hh][r + 3:r + 4, :], self.CH[h:h + 1, 1, :], [("CH", 1)], allk(hh))
            cl.append(bias)
        if layer == 0 and kind == "mem":
            def clr():
                for hh in range(2):
                    r = brow[hh]
                    self.memset("pool", S_["QZ"][hh][r:r + 4, :], 0.0, allq(hh))
            cl.append(clr)
        qeng = "act" if layer == 0 else "dve"
        for blk in range(4):
            def qp(blk=blk):
                tsl = slice(blk * 512, (blk + 1) * 512)
                self.proj_fm(W, slice(0, 128), blk, wreg,
                             [(S_["QZ"][0][h0, tsl], h0, ("QZ", st, 0, blk)), (S_["QZ"][1][h1, tsl], h1, ("QZ", st, 1, blk))], 0.125, qeng)
            cl.append(qp)
        if kind == "main":
            for blk in range(4):
                def kp(blk=blk):
                    tsl = slice(blk * 512, (blk + 1) * 512)
                    if layer == 0:
                        dsts = [(S_["KZ"][0][h0, tsl], h0, ("KZ", st, 0, blk)), (S_["KZ"][1][h1, tsl], h1, ("KZ", st, 1, blk))]
                    else:
                        dsts = [(S_["KZ"][0][:, tsl], slice(0, 128), ("KZ", st, 0, blk))]
                    self.proj_fm(W, slice(128, 256), blk, wreg, dsts, 1.0, "dve")
                cl.append(kp)
            for i in range(NT):
                cl.append(lambda i=i: self.v_proj(W, slice(256, 384), 128, i, wreg, S_["VP"][:, i, :], self.HT, ("HT", i), ("V", st, i)))
        if nxt_spec is not None:
            cl.insert(0, lambda: self.pair_wload(layer, *nxt_spec, part="dma"))
            if layer == 1:
                cl.append(lambda: self.pair_wload(layer, *nxt_spec, part="fold"))
        return cl

    def run_pairs(self, layer, preloaded=False, after_first=None):
        nmain = 6 if layer == 0 else self.opt.get("l1_pairs", 6)
        plist = [("main", p) for p in range(nmain)]
        if layer == 0 or self.opt.get("l1_mem", True):
            plist += [("mem", p) for p in range(2)]
        spec = lambda idx: (plist[idx] + (idx % 2,)) if idx < len(plist) else None
        if not preloaded:
            self.pair_wload(layer, *spec(0))
        for c in self.pair_closures(layer, plist[0][0], plist[0][1], 0, 0, spec(1)):
            c()
        if after_first is not None:
            after_first()
        for idx, (kind, p) in enumerate(plist):
            st = idx % 2
            bgl = []
            if idx + 1 < len(plist):
                bgl = self.pair_closures(layer, plist[idx + 1][0], plist[idx + 1][1], 1 - st, (idx + 1) % 2, spec(idx + 2))
            if idx == len(plist) - 1 and st == 1 and self.opt.get("wo_prefetch", True):
                set0 = [("QZ", 0, hh, k) for hh in range(2) for k in range(4)] + [("KZ", 0, hh, k) for hh in range(2) for k in range(4)] \
                    + [("V", 0, i) for i in range(NT)]
                bgl = [lambda: self.dma("pool", self.WO, self.d_wout[layer].rearrange("(c p) n -> p c n", p=128), [], [("WO",)] + set0)]
                self.wo_loaded = layer
            bg = {"list": bgl, "done": 0}
            if kind == "main":
                if layer == 0:
                    VPt = self.SETS[st]["VP"]
                    self.attn_softmax_pair(st, None, lambda kt, VPt=VPt: VPt[:, kt, :], None, True, p,
                                           lambda hh, kt, st=st: ("KZ", st, hh, kt // 4), lambda kt, st=st: ("V", st, kt), bg)
                elif self.opt.get("l1_attn", True):
                    self.attn_sb_pair(st, p, bg)
                else:
                    self.bg_flush(bg)
            else:
                MK = self.MKT[:, p, :]
                self.attn_softmax_pair(st, [MK, MK], lambda kt, p=p: self.MVP[:, kt, 128 * p:128 * p + 128], 2, False, 6 + p,
                                       lambda hh, kt, p=p: ("MKT", p), lambda kt: ("MV", kt), bg)

    def mem_branch(self, s, l):
        self.load_gb(1 + l)
        self.wload(self.WM, self.d_wmemkv[l], ("WM",))
        for mt in range(2):
            self.dma("sp", self.MTMP[:], self.d_mem[s, mt * 128:(mt + 1) * 128, :], [], [("mtmp",)])
            self.norm_tile(self.MTMP[:], ("mtmp",), self.GB[:], self.HMT[:, :, mt * 128:(mt + 1) * 128], ("HMT", mt), "act")
        self.norm_flush()
        for p in range(2):
            ps, preg = self.psA()
            self.pe([(ps[:, 0:256], self.WM[:, c, 128 * p:128 * p + 128], self.HMT[:, c, :], c == 0, c == 7) for c in range(8)],
                    [("WM",), ("HMT", 0), ("HMT", 1)], [preg])
            self.cp("act", self.MKT[:, p, :], ps[:, 0:256], [preg], [("MKT", p)])
        for mt in range(2):
            self.v_proj(self.WM, slice(256, 512), 256, mt, ("WM",), self.MVP[:, mt, :], self.HMT, ("HMT", mt), ("MV", mt))

    def out_proj(self, l, post=None, prefetch=False):
        if self.wo_loaded != l:
            self.wload(self.WO, self.d_wout[l], ("WO",))
        self.wo_loaded = None
        if prefetch:
            self.ffn_prefetch(l)
        for i in range(NT):
            a = self.nxt("acc2", 2)
            ps = self.ACC2[a]
            regs = [("acc", 2 * a), ("acc", 2 * a + 1)]
            mms = []
            for h in range(2):
                for c in range(8):
                    mms.append((ps[:, 512 * h:512 * h + 512], self.OT[:, c, i * 128:(i + 1) * 128], self.WO[:, c, 512 * h:512 * h + 512], c == 0, c == 7))
            self.pe(mms, [("WO",)] + [("OT", c, i) for c in range(8)], regs)
            self.tt("dve", self.X[:, i, :], ps[:, :], self.X[:, i, :], ALU.add, regs + [("X", i)], [("X", i)])
            if post is not None:
                post(i)
        self.norm_flush()

    def norm_x_tile(self, i, gb):
        self.norm_tile(self.X[:, i, :], ("X", i), gb, self.HT[:, :, i * 128:(i + 1) * 128], ("HT", i), "act" if i % 2 else "dve")

    def ffn_prefetch(self, l):
        wup = self.d_wup[l]
        nf = 128 * FCH[0]
        self.wload(self.WUP[0][:, :, 0, 0:nf], wup[:, 0:nf], ("WUP", 0, 0))
        self.wload(self.WUP[0][:, :, 1, 0:nf], wup[:, DFF:DFF + nf], ("WUP", 0, 1))
        self.wload(self.WDN[0][:, 0:FCH[0], :], self.d_wdown[l][0:nf, :], ("WDN", 0))
        self.ffn_preloaded = l

    def ffn(self, l, post=None):
        wup = self.d_wup[l]
        wdn = self.d_wdown[l]
        nch = len(FCH)

        def load_up(fi):
            b = fi % 2
            f0 = 128 * sum(FCH[:fi])
            nf = 128 * FCH[fi]
            self.wload(self.WUP[b][:, :, 0, 0:nf], wup[:, f0:f0 + nf], ("WUP", b, 0))
            self.wload(self.WUP[b][:, :, 1, 0:nf], wup[:, DFF + f0:DFF + f0 + nf], ("WUP", b, 1))

        def load_dn(fi):
            b = fi % 2
            f0 = 128 * sum(FCH[:fi])
            nf = 128 * FCH[fi]
            self.wload(self.WDN[b][:, 0:FCH[fi], :], wdn[f0:f0 + nf, :], ("WDN", b))

        def down_tile(fi, i, final):
            b = fi % 2
            a = self.nxt("acc2", 2)
            ps = self.ACC2[a]
            regs = [("acc", 2 * a), ("acc", 2 * a + 1)]
            mms = []
            for h in range(2):
                for k in range(FCH[fi]):
                    mms.append((ps[:, 512 * h:512 * h + 512], self.AT[b][:, k, i * 128:(i + 1) * 128],
                                self.WDN[b][:, k, 512 * h:512 * h + 512], k == 0, k == FCH[fi] - 1))
            self.pe(mms, [("WDN", b)] + [("AT", b, k, i // 4) for k in range(FCH[fi])], regs)
            self.tt("dve", self.X[:, i, :], ps[:, :], self.X[:, i, :], ALU.add, regs + [("X", i)], [("X", i)])
            if final and post is not None:
                post(i)

        def step(fi, k, tb):
            b = fi % 2
            cbase = sum(FCH[:fi])
            tsl = slice(tb * 512, (tb + 1) * 512)
            gT = None
            for gv in range(2):
                ci = cbase + k + (DFF // 128) * gv
                ps, preg = self.psA()
                self.pe([(ps[:, :], self.WUP[b][:, c, gv, 128 * k:128 * k + 128], self.HT[:, c, tsl], c == 0, c == 7) for c in range(8)],
                        [("WUP", b, gv)] + [("HT", 4 * tb + j) for j in range(4)], [preg])
                UB = self.UBS[gv][tb % 2]
                ureg = ("UB", gv, tb % 2)
                if tb == 0:
                    self.memset("pool", UB[:, 0:2], 0.0, [ureg])
                else:
                    self.cp("pool", UB[:, 0:2], self.UBS[gv][(tb + 1) % 2][:, 512:514], [("UB", gv, (tb + 1) % 2)], [ureg])
                self.cp("act", UB[:, 2:514], ps[:, :], [preg], [ureg])
                tbuf = self.nxt("T%d" % gv, 2)
                T = (self.TG if gv == 0 else self.TV)[tbuf]
                treg = ("T", gv, tbuf)
                self.act(T[:, :], ps[:, :], AF.Identity, [preg, ("cw",), ("cb",)], [treg],
                         scale=self.CW[:, ci, 2:3], bias=self.CB[:, ci:ci + 1])
                self.stt(T[:, :], UB[:, 1:513], self.CW[:, ci, 1:2], T[:, :], ALU.mult, ALU.add, [ureg, treg, ("cw",)], [treg])
                self.stt(T[:, :], UB[:, 0:512], self.CW[:, ci, 0:1], T[:, :], ALU.mult, ALU.add, [ureg, treg, ("cw",)], [treg])
                if gv == 0:
                    gT = (T, treg)
                else:
                    self.act(gT[0][:, :], gT[0][:, :], AF.Silu, [gT[1]], [gT[1]])
                    self.tt("dve", self.AT[b][:, k, tsl], gT[0][:, :], T[:, :], ALU.mult, [gT[1], treg], [("AT", b, k, tb)])

        if self.ffn_preloaded != l:
            load_up(0)
            load_dn(0)
        self.ffn_preloaded = None
        pending = []
        for fi in range(nch):
            if fi + 1 < nch:
                load_up(fi + 1)
            steps = [(k, tb) for k in range(FCH[fi]) for tb in range(4)]
            for si, (k, tb) in enumerate(steps):
                step(fi, k, tb)
                left = len(steps) - si
                ne = (len(pending) + left - 1) // left
                for _ in range(ne):
                    pending.pop(0)()
            assert not pending
            if fi + 1 < nch:
                load_dn(fi + 1)
            pending = [(lambda fi=fi, i=i: down_tile(fi, i, fi == nch - 1)) for i in range(NT)]
        for cl in pending:
            cl()
        self.norm_flush()

    def layer0_mixer(self, s):
        self.dma("pool", self.WP[1][:, :, 0:12], self.d_wf, [], [("WP", 1)])
        self.pair_wload(0, "main", 0, 0)
        self.load_gb(0)
        brow = [64, 0]

        def init_set(st):
            S_ = self.SETS[st]
            for hh in range(2):
                r = brow[hh]
                self.memset("dve", S_["QZ"][hh][:, :], 0.0, [("QZ", st, hh, k) for k in range(4)])
                self.memset("dve", S_["KZ"][hh][:, :], 0.0, [("KZ", st, hh, k) for k in range(4)])
                self.memset("dve", S_["QZ"][hh][r:r + 4, :], -1.0, [("QZ", st, hh, k) for k in range(4)])
                self.memset("dve", S_["KZ"][hh][r:r + 4, :], 1.0, [("KZ", st, hh, k) for k in range(4)])

        init_set(0)
        cf_alias = [("QZ", 1, hh, k) for hh in range(2) for k in range(4)]
        for i in range(NT):
            self.norm_x_tile(i, self.GB[:])
        self.norm_flush()
        WF = self.WP[1][:, :, 0:12]
        cfo_alias = [("WP", 1), ("PT", 0), ("PT", 1)]
        for blk in range(4):
            ps, preg = self.psA()
            tsl = slice(blk * 512, (blk + 1) * 512)
            self.pe([(ps[0:12, :], WF[:, c, :], self.HT[:, c, tsl], c == 0, c == 7) for c in range(8)],
                    [("WP", 1)] + [("HT", 4 * blk + j) for j in range(4)], [preg])
            self.act(self.CF[0:12, tsl], ps[0:12, :], AF.Exp, [preg], [("CF", blk)] + cf_alias, scale=-1.0, bias=self.NB[:, 0:1])
            self.act(self.CF[0:12, tsl], self.CF[0:12, tsl], AF.Ln, [("CF", blk)], [("CF", blk)], bias=1.0)
            self.ts("dve", self.CF[0:12, tsl], self.CF[0:12, tsl], -0.5, None, ALU.mult, None, [("CF", blk)], [("CF", blk)])
        self.S.op("dve", lambda e: e.tensor_tensor_scan(out=self.CFO[0:12, :], data0=self.CF[0:12, :], data1=self.CF[0:12, :],
                                                         initial=0.0, op0=ALU.add, op1=ALU.add),
                  [("CF", k) for k in range(4)] + cf_alias, [("CFO",)] + cfo_alias)
        self.cp("dve", self.CH[0:12, 0, :], self.CFO[0:12, :], [("CFO",)] + cfo_alias, [("CH", 0)])
        self.tt("dve", self.CH[0:12, 1, :], self.CFO[0:12, :], self.CH[0:12, 0, :], ALU.subtract, [("CFO",), ("CH", 0)] + cfo_alias, [("CH", 1)])
        if "cf" in self.d_dbg:
            self.dma("sp", self.d_dbg["cf"], self.CFO[0:12, :], [("CFO",)] + cfo_alias, [])
        init_set(1)
        self.run_pairs(0, preloaded=True, after_first=lambda: self.mem_branch(s, 0))

    def layer1_mixer(self, s):
        for st in range(2):
            for hh in range(2):
                self.memset("dve", self.SETS[st]["QZ"][hh][:, :], 0.0, [("QZ", st, hh, k) for k in range(4)])
        if self.opt.get("l1_pairs", 6) > 0:
            self.pair_wload(1, "main", 0, 0)
        self.mem_branch(s, 1)
        if not self.prenormed:
            for i in range(NT):
                self.norm_x_tile(i, None)
            self.norm_flush()
        self.run_pairs(1, preloaded=self.opt.get("l1_pairs", 6) > 0)

    def dump_x(self, name):
        if name in self.d_dbg:
            for i in range(NT):
                self.dma("sp", self.d_dbg[name][i * 128:(i + 1) * 128, :], self.X[:, i, :], [("X", i)], [])

    def final_tile(self, s, i, normed):
        ob = self.nxt("outt", 2)
        O = self.OUTT[ob]
        if normed:
            b = self.nxt("hn", 2)
            ss = self.STAT[:, 2 * b:2 * b + 1]
            rs = self.STAT[:, 2 * b + 1:2 * b + 2]
            xin = self.X[:, i, :]
            self.act(self.HN[b][:], xin, AF.Square, [("X", i)], [("hn", b), ("PT", 3 + 2 * b), ("PT", 4 + 2 * b), ("ss", b)], scale=1.0 / 32.0, accum_out=ss)
            self.act(ss, ss, AF.Sqrt, [("ss", b)], [("ss", b)], bias=self.EPSC[:, 0:1])
            self.S.op("dve", lambda e, rs=rs, ss=ss: e.reciprocal(out=rs, in_=ss), [("ss", b)], [("rs", b)])
            self.stt(O[:], xin, rs, self.GB[:], ALU.mult, ALU.mult, [("X", i), ("rs", b), ("gb",)], [("outt", ob)])
            self.dma("sp", self.d_out[s, i * 128:(i + 1) * 128, :], O[:], [("outt", ob)], [])
            if s + 1 < self.nseq:
                self.dma("sp", self.X[:, i, :], self.d_x[s + 1, i * 128:(i + 1) * 128, :], [], [("X", i)])
                self.x_prefetched = s + 1
        else:
            self.dma("sp", self.d_out[s, i * 128:(i + 1) * 128, :], self.X[:, i, :], [("X", i)], [])

    def program(self):
        order = ["l0mix", "l0out", "l0ffn", "l1mix", "l1out", "l1ffn"]
        stop = self.stop
        nph = len(order) if stop is None else order.index(stop) + 1
        self.dma("pool", self.CONST[:], self.d_consts, [], [("const",)])
        self.dma("sp", self.GC[:], self.d_gc, [], [("gc",)])
        self.dma("sp", self.NB[:], self.d_bf, [], [("nb",)])
        self.memset("dve", self.EPSC[:], EPS, [("eps",)])
        self.ts("dve", self.NB[:], self.NB[:], -1.0, None, ALU.mult, None, [("nb",)], [("nb",)])
        self.S.fence()
        for s in range(self.nseq):
            if self.x_prefetched != s:
                for i in range(NT):
                    self.dma("sp", self.X[:, i, :], self.d_x[s, i * 128:(i + 1) * 128, :], [], [("X", i)])
            phs = order[:nph] if self.phases is None else self.phases
            full = self.phases is None and stop is None
            self.prenormed = False
            for ph in phs:
                if ph == "l0mix":
                    self.layer0_mixer(s)
                elif ph == "l1mix":
                    self.layer1_mixer(s)
                elif ph in ("l0out", "l1out"):
                    l = int(ph[1])
                    self.load_gb(3 + l)
                    self.dma("sp", self.CW[:], self.d_cw[l], [], [("cw",)])
                    self.dma("sp", self.CB[:], self.d_cb[l], [], [("cb",)])
                    self.out_proj(l, lambda i: self.norm_x_tile(i, self.GB[:]), prefetch=("l%dffn" % l) in phs)
                    if s == 0:
                        self.dump_x("x_l%dmix" % l)
                elif ph == "l0ffn":
                    self.ffn(0, (lambda i: self.norm_x_tile(i, None)) if full else None)
                    self.prenormed = full
                    if s == 0:
                        self.dump_x("x_l0")
                elif ph == "l1ffn":
                    if full:
                        self.load_gb(5)
                    self.ffn(1, (lambda i, s=s: self.final_tile(s, i, True)) if full else None)
                self.S.fence()
            if not full:
                for i in range(NT):
                    self.final_tile(s, i, False)
            self.S.fence()

    def build(self):
        nc = self.nc
        nseq = self.nseq
        dt_in = lambda name, shape: nc.dram_tensor(name, shape, F32, kind="ExternalInput").ap()
        self.d_x = dt_in("x", [nseq, SEQ, D])
        self.d_mem = dt_in("mem", [nseq, NMEM, D])
        self.d_wa = dt_in("wa", [8, 128, 8, 384])
        self.d_wb = dt_in("wb", [8, 128, 8, 384])
        self.d_wf = dt_in("wf", [128, 8, 12])
        self.d_wmemkv = dt_in("w_memkv", [2, D, 512])
        self.d_wout = dt_in("w_out", [2, D, D])
        self.d_wup = dt_in("w_up", [2, D, 2 * DFF])
        self.d_wdown = dt_in("w_down", [2, DFF, D])
        self.d_gb = dt_in("gb", [6, 128, D])
        self.d_gc = dt_in("gc", [128, 2, 8])
        self.d_bf = dt_in("bf", [12, 1])
        self.d_cw = dt_in("cw", [2, 128, 44, 3])
        self.d_cb = dt_in("cb", [2, 128, 44])
        self.d_consts = dt_in("consts", [128, 7, 128])
        self.d_out = nc.dram_tensor("out", [nseq, SEQ, D], F32, kind="ExternalOutput").ap()
        self.d_dbg = {}
        for name, shape in self.dbg_shapes().items():
            if name in self.dbg:
                self.d_dbg[name] = nc.dram_tensor("dbg_" + name, shape, F32, kind="ExternalOutput").ap()

        with ExitStack() as es:
            sb = lambda name, shape, dt: es.enter_context(nc.sbuf_tensor(name, shape, dt))
            self.X = sb("X", [128, NT, D], F32)
            self.HT = sb("HT", [128, 8, SEQ], BF16)
            self.CONST = sb("CONST", [128, 7, 128], BF16)
            self.IDENT = self.CONST[:, 0, :]
            self.TRI = self.CONST[:, 1, :]
            self.TRIS = self.CONST[:, 2, :]
            self.NTRI = self.CONST[:, 3, :]
            self.NONES = self.CONST[:, 4, :]
            self.ZEROS = self.CONST[:, 5, :]
            self.ONES = self.CONST[:, 6, :]
            self.GB = sb("GB", [128, D], F32)
            self.GC = sb("GC", [128, 2, 8], F32)
            self.NB = sb("NB", [12, 1], F32)
            self.CW = sb("CW", [128, 44, 3], F32)
            self.CB = sb("CB", [128, 44], F32)
            self.STAT = sb("STAT", [128, 16], F32)
            self.EPSC = sb("EPSC", [128, 1], F32)
            self.HN = [sb("HN%d" % i, [128, D], BF16) for i in range(2)]
            self.ARENA_B = 103000
            self.ARENA = sb("ARENA", [128, self.ARENA_B // 2], BF16)
            self.PSA = [es.enter_context(nc.psum_tensor("psA%d" % i, [128, 512], F32)) for i in range(3)]
            self.ACC2 = [es.enter_context(nc.psum_tensor("acc2_%d" % i, [128, 1024], F32)) for i in range(2)]
            self.OACC = [self.ACC2[0][:, 0:512], self.ACC2[0][:, 512:1024]]
            self.LACC = [self.ACC2[1][:, 0:512], self.ACC2[1][:, 512:1024]]
            self.PST = es.enter_context(nc.psum_tensor("psT", [128, 8, 128], BF16))
            self.PSTF = self.PST[:, :, :].rearrange("p a b -> p (a b)").bitcast(F32)
            esem = {e: es.enter_context(nc.semaphore("sem_" + e)) for e in CENGS}
            dsem = [es.enter_context(nc.semaphore("dsem%d" % i)) for i in range(24)]
            self.S = Sched(esem, dsem)
            self.carve_all()
            self.program()
            self.emit()
        return nc

    def dbg_shapes(self):
        return {"x_l0mix": [SEQ, D], "x_l0": [SEQ, D], "x_l1mix": [SEQ, D], "cf": [12, SEQ]}

    def carve(self, off, shape, dt):
        n = int(np.prod(shape[1:]))
        esz = 2 if dt == BF16 else 4
        assert off % 4 == 0, off
        a = self.ARENA[:, off // 2:off // 2 + n * esz // 2]
        if dt == F32:
            a = a.bitcast(F32)
        if len(shape) == 3:
            a = a.rearrange("p (a b) -> p a b", a=shape[1])
        elif len(shape) == 4:
            a = a.rearrange("p (a b c) -> p a b c", a=shape[1], b=shape[2])
        self._off = off + n * esz
        assert self._off <= self.ARENA_B, self._off
        return a

    def carve_all(self):
        self.OT = self.carve(0, [128, 8, SEQ], BF16)
        o = self._off
        self.MTMP = self.carve(0, [128, D], F32)
        self.WM = self.carve(self._off, [128, 8, 512], BF16)
        self.HMT = self.carve(self._off, [128, 8, NMEM], BF16)
        self.SETS = []
        for st in range(2):
            d = {"QZ": [], "KZ": []}
            for i in range(2):
                d["QZ"].append(self.carve(o, [128, SEQ], BF16)); o = self._off
            for i in range(2):
                d["KZ"].append(self.carve(o, [128, SEQ], BF16)); o = self._off
            d["VP"] = self.carve(o, [128, NT, 128], BF16); o = self._off
            self.SETS.append(d)
        self.WP = []
        for i in range(2):
            self.WP.append(self.carve(o, [128, 8, 384], BF16)); o = self._off
        self.PT = []
        for i in range(3):
            self.PT.append(self.carve(o, [128, 512], BF16)); o = self._off
        for b in range(2):
            self.PT.append(self.HN[b][:, 0:512])
            self.PT.append(self.HN[b][:, 512:1024])
        self.MKT = self.carve(o, [128, 2, NMEM], BF16); o = self._off
        self.MVP = self.carve(o, [128, 2, 256], BF16); o = self._off
        self.RB = self.carve(o, [128, 512], F32); o = self._off
        o0 = o
        self.CH = self.carve(o, [128, 2, SEQ], BF16); o = self._off
        o = o0
        self.EX = []
        for i in range(2):
            self.EX.append(self.carve(o, [128, 512], F32)); o = self._off
        self.SPT = []
        for i in range(4):
            self.SPT.append(self.carve(o, [128, 512], BF16)); o = self._off
        self.SL = self.carve(o, [128, 512], BF16); o = self._off
        self.CF = self.SETS[1]["QZ"][0].bitcast(F32)[:, 0:1024] if False else self.carve(53248, [128, SEQ], F32)
        self.CFO = self.carve(79872, [128, SEQ], F32)
        self.WO = self.carve(32768, [128, 8, D], BF16)
        top = self.ARENA_B - 24576
        self.WUP = [self.carve(top, [128, 8, 2, 512], BF16), self.carve(0, [128, 8, 2, 512], BF16)]
        self.WDN = [self.carve(top + 16384, [128, 4, D], BF16), self.carve(16384, [128, 4, D], BF16)]
        o = 24576
        self.AT = []
        for i in range(2):
            self.AT.append(self.carve(o, [128, 4, SEQ], BF16)); o = self._off
        self.UBS = [[None, None], [None, None]]
        for g in range(2):
            for i in range(2):
                self.UBS[g][i] = self.carve(o, [128, 516], F32); o = self._off
        self.TG = []
        self.TV = []
        for i in range(2):
            self.TG.append(self.carve(o, [128, 512], F32)); o = self._off
            self.TV.append(self.carve(o, [128, 512], F32)); o = self._off
        assert o <= top, (o, top)
        self.OUTT = [self.carve(24576, [128, D], F32), self.carve(24576 + 4096, [128, D], F32)]

    def emit(self):
        nc = self.nc
        S = self.S
        with nc.Block() as block:
            @block.tensor
            def _(e):
                for f in S.prog["pe"]:
                    f(e)

            @block.scalar
            def _(e):
                for f in S.prog["act"]:
                    f(e)

            @block.vector
            def _(e):
                for f in S.prog["dve"]:
                    f(e)

            @block.gpsimd
            def _(e):
                for f in S.prog["pool"]:
                    f(e)

            @block.sync
            def _(e):
                for f in S.prog["sp"]:
                    f(e)


def host_inputs(x, mem, ln_mix_g, w_in_a, b_f_a, w_in_b, ln_kv_g, w_kv, ln_mem_g, w_memkv, w_out, ln_ffn_g,
                w_up, conv_w, conv_b, w_down, final_g):
    f = lambda a: np.ascontiguousarray(np.asarray(a, dtype=np.float32))
    rep = lambda g: np.broadcast_to(np.asarray(g, np.float32)[None, :], (128, D))
    gb = f(np.stack([rep(ln_mix_g[0]), rep(ln_mem_g[0]), rep(ln_mem_g[1]), rep(ln_ffn_g[0]), rep(ln_ffn_g[1]), rep(final_g)]))
    col = lambda g: np.asarray(g, np.float32).reshape(8, 128).T
    gc = f(np.stack([col(ln_kv_g), col(ln_mix_g[1])], axis=1))
    cw = f(np.asarray(conv_w, np.float32).reshape(2, 3, 44, 128).transpose(0, 3, 2, 1))
    cb = f(np.asarray(conv_b, np.float32).reshape(2, 44, 128).transpose(0, 2, 1))
    i = np.arange(128)
    ident = (i[:, None] == i[None, :]).astype(np.float32)
    tri = (i[:, None] <= i[None, :]).astype(np.float32)
    tris = (i[:, None] < i[None, :]).astype(np.float32)
    ntri = -(i[:, None] >= i[None, :]).astype(np.float32)
    nones = -np.ones((128, 128), np.float32)
    consts = f(np.stack([ident, tri, tris, ntri, nones, 0.0 * nones, -nones], axis=1))
    pc = lambda w: np.asarray(w, np.float32).reshape(8, 128, -1).transpose(1, 0, 2)
    wia = np.asarray(w_in_a[0], np.float32)
    wib = np.asarray(w_in_b[0], np.float32)
    wkv = np.asarray(w_kv, np.float32)
    wa = np.zeros((8, 128, 8, 384), np.float32)
    wb = np.zeros((8, 128, 8, 384), np.float32)
    for p in range(6):
        for k in range(3):
            wa[p, :, :, 128 * k:128 * k + 128] = pc(wia[:, 768 * k + 128 * p:768 * k + 128 * p + 128])
        wb[p, :, :, 0:128] = pc(wib[:, 128 * p:128 * p + 128])
        wb[p, :, :, 128:256] = pc(wkv[:, 128 * p:128 * p + 128])
        wb[p, :, :, 256:384] = pc(wkv[:, 768 + 128 * p:768 + 128 * p + 128])
    for m_ in range(2):
        wa[6 + m_, :, :, 0:128] = pc(wia[:, 2316 + 128 * m_:2316 + 128 * m_ + 128])
        wb[6 + m_, :, :, 0:128] = pc(wib[:, 768 + 128 * m_:768 + 128 * m_ + 128])
    wf = f(pc(wia[:, 2304:2316]))
    return {
        "wa": wa, "wb": wb, "wf": wf, "w_memkv": f(w_memkv), "w_out": f(w_out),
        "w_up": f(w_up), "w_down": f(w_down), "gb": gb, "gc": gc, "bf": f(np.asarray(b_f_a, np.float32).reshape(12, 1)),
        "cw": cw, "cb": cb, "consts": consts,
    }


def kernel(**inputs):
    x = np.asarray(inputs["x"], np.float32)
    mem = np.asarray(inputs["mem"], np.float32)
    shared = host_inputs(**inputs)
    nseq = x.shape[0] // NCORES
    nc = Builder(nseq=nseq).build()
    in_maps = []
    for c in range(NCORES):
        m = dict(shared)
        m["x"] = np.ascontiguousarray(x[c * nseq:(c + 1) * nseq])
        m["mem"] = np.ascontiguousarray(mem[c * nseq:(c + 1) * nseq])
        in_maps.append(m)
    res = run_bass_kernel_spmd(nc, in_maps, core_ids=list(range(NCORES)))
    return np.concatenate([np.asarray(r["out"], np.float32) for r in res.results], axis=0)
```
